# Optimizing a Trainium2 kernel written in Bass

```python
import math
import jax, jax.numpy as jnp
from jax import lax
import numpy as np

D_MODEL = 1024
BATCH = 32
SEQ = 256
DEPTH = 2
DEC_BATCH = 2
DEC_SEQ = 1024
PAST_LEN = 512

GRID_W = 64
N_AB = (DEPTH + 1) // 2
N_CD = DEPTH // 2
EPS = 1e-6
NEG_INF = -1e30

S5_WIDTH = D_MODEL // 2
S5_GROUP = 16
S5_GROUPS = S5_WIDTH // S5_GROUP
S5_STATE = 64
S5_DIR_PARAMS = ('s5_lam_re', 's5_lam_im', 's5_log_dt', 's5_b_re', 's5_b_im', 's5_c_re', 's5_c_im')
GDN_DK = 128
GDN_DV = 128
GDN_HEADS = D_MODEL // 256
GDN_WIDTH = GDN_HEADS * GDN_DV
GDN_CHUNK = 64
SHORT_CONV = 3
HEAD_DIM = 64
C_HEADS = 8
C_KV_HEADS = 2
C_GROUP = C_HEADS // C_KV_HEADS
WINDOW = 128
Q_BLOCK = 128
D_HEADS = 4
D_VDIM = 2 * HEAD_DIM
ATTN_SCALE = HEAD_DIM ** -0.5
ROPE_THETA = 10000.0
D_FF = 2816
FFN_CONV = 3

MIX_WIDTH = S5_WIDTH + GDN_WIDTH
AB_SPLITS = [S5_WIDTH, S5_WIDTH + 3 * GDN_WIDTH, S5_WIDTH + 4 * GDN_WIDTH, S5_WIDTH + 4 * GDN_WIDTH + 2 * GDN_HEADS]
AB_IN = S5_WIDTH + 4 * GDN_WIDTH + 4 * GDN_HEADS
C_Q = C_HEADS * HEAD_DIM
C_KV = C_KV_HEADS * HEAD_DIM
D_QK = D_HEADS * 2 * HEAD_DIM
CD_SPLITS = [C_Q, C_Q + C_KV, C_Q + 2 * C_KV, C_Q + 2 * C_KV + D_QK, C_Q + 2 * C_KV + 2 * D_QK]
CD_IN = C_Q + 2 * C_KV + 2 * D_QK + D_HEADS * D_VDIM

kernel_name = 'hybrid_diffusion_prefix_trunk_step'


def _rms(x, g):
    xf = x.astype(jnp.float32)
    y = xf * lax.rsqrt(jnp.mean(xf * xf, axis=-1, keepdims=True) + EPS)
    return (y * g.astype(jnp.float32)).astype(x.dtype)


def _l2norm(x):
    return x * lax.rsqrt(jnp.sum(x * x, axis=-1, keepdims=True) + EPS)


def _adaln(cvec, w, b):
    m = (jax.nn.silu(cvec) @ w + b)[:, None, :]
    return jnp.split(m, 6, axis=-1)


def _modulate(x, g, shift, scale):
    return _rms(x, g) * (1 + scale) + shift


def _dwconv(x, w, b=None):
    k = w.shape[0]
    pad = k // 2
    n = x.shape[1]
    xp = jnp.pad(x, ((0, 0), (pad, pad), (0, 0)))
    y = sum(xp[:, i:i + n] * w[i] for i in range(k))
    return y if b is None else y + b


def _rope2d(x):
    n = x.shape[1]
    rows = n // GRID_W
    row = jnp.repeat(jnp.arange(rows), GRID_W).astype(jnp.float32)
    col = jnp.tile(jnp.arange(GRID_W), rows).astype(jnp.float32)
    half = HEAD_DIM // 2
    quarter = half // 2
    inv = ROPE_THETA ** (-jnp.arange(quarter, dtype=jnp.float32) / quarter)
    bshape = (1, n) + (1,) * (x.ndim - 3) + (quarter,)
    xf = x.astype(jnp.float32)

    def rot(xa, pos):
        ang = (pos[:, None] * inv[None, :]).reshape(bshape)
        cos, sin = jnp.cos(ang), jnp.sin(ang)
        x1, x2 = xa[..., :quarter], xa[..., quarter:]
        return jnp.concatenate([x1 * cos - x2 * sin, x2 * cos + x1 * sin], axis=-1)

    out = jnp.concatenate([rot(xf[..., :half], row), rot(xf[..., half:], col)], axis=-1)
    return out.astype(x.dtype)


def _cplx_combine(e1, e2):
    a1r, a1i, b1r, b1i = e1
    a2r, a2i, b2r, b2i = e2
    return (a2r * a1r - a2i * a1i, a2r * a1i + a2i * a1r,
            a2r * b1r - a2i * b1i + b2r, a2r * b1i + a2i * b1r + b2i)


def _s5_scan(u, lam_re, lam_im, log_dt, b_re, b_im, c_re, c_im, h0_re, h0_im):
    dt = jnp.exp(log_dt)[:, None]
    mag = jnp.exp(lam_re * dt)
    ar, ai = mag * jnp.cos(lam_im * dt), mag * jnp.sin(lam_im * dt)
    den = lam_re * lam_re + lam_im * lam_im
    fr = ((ar - 1.0) * lam_re + ai * lam_im) / den
    fi = (ai * lam_re - (ar - 1.0) * lam_im) / den
    bbr = fr[..., None] * b_re - fi[..., None] * b_im
    bbi = fr[..., None] * b_im + fi[..., None] * b_re
    xr = jnp.einsum('blgc,gpc->blgp', u, bbr)
    xi = jnp.einsum('blgc,gpc->blgp', u, bbi)
    xr = xr.at[:, 0].add(ar * h0_re - ai * h0_im)
    xi = xi.at[:, 0].add(ar * h0_im + ai * h0_re)
    elems = (jnp.broadcast_to(ar, xr.shape), jnp.broadcast_to(ai, xr.shape), xr, xi)
    _, _, hr, hi = lax.associative_scan(_cplx_combine, elems, axis=1)
    y = jnp.einsum('blgp,gcp->blgc', hr, c_re) - jnp.einsum('blgp,gcp->blgc', hi, c_im)
    return y, hr[:, -1], hi[:, -1]


def _s5_mixer(u, p, j, h0_re, h0_im):
    bsz, n, _ = u.shape
    uf = u.astype(jnp.float32).reshape(bsz, n, S5_GROUPS, S5_GROUP)

    def direction(ud, dr):
        prm = [p[name][j, dr].astype(jnp.float32) for name in S5_DIR_PARAMS]
        return _s5_scan(ud, *prm, h0_re[:, dr].astype(jnp.float32), h0_im[:, dr].astype(jnp.float32))

    y_f, hr_f, hi_f = direction(uf, 0)
    y_b, hr_b, hi_b = direction(uf[:, ::-1], 1)
    y = (y_f + y_b[:, ::-1]).reshape(bsz, n, S5_WIDTH) + p['s5_d'][j].astype(jnp.float32) * u.astype(jnp.float32)
    g = jax.nn.gelu(y)
    out = g * jax.nn.sigmoid(g @ p['s5_w_glu'][j].astype(jnp.float32) + p['s5_b_glu'][j].astype(jnp.float32))
    return out.astype(u.dtype), jnp.stack([hr_f, hr_b], axis=1), jnp.stack([hi_f, hi_b], axis=1)


def _chunk_gated_delta(q, k, v, g, beta, s0):
    bsz, n, nh, _ = q.shape
    dvv = v.shape[-1]
    cs = GDN_CHUNK
    nc = n // cs

    def to_chunks(t):
        t = t.reshape((bsz, nc, cs, nh) + t.shape[3:])
        return jnp.moveaxis(jnp.moveaxis(t, 1, 0), 2, 3)

    q, k, v, g, beta = (to_chunks(t) for t in (q, k, v, g, beta))
    gc = jnp.cumsum(g, axis=-1)
    idx = jnp.arange(cs)
    incl = idx[:, None] >= idx[None, :]
    strict = idx[:, None] > idx[None, :]
    decay = jnp.exp(jnp.where(incl, gc[..., :, None] - gc[..., None, :], NEG_INF))
    kb = k * beta[..., None]
    nmat = jnp.where(strict, jnp.einsum('nbhid,nbhjd->nbhij', kb, k) * decay, 0.0)
    eye = jnp.eye(cs, dtype=jnp.float32)
    tmat = lax.linalg.triangular_solve(nmat + eye, jnp.broadcast_to(eye, nmat.shape),
                                       left_side=True, lower=True, unit_diagonal=True)
    u = tmat @ (v * beta[..., None])
    w = tmat @ (kb * jnp.exp(gc)[..., None])
    amat = jnp.where(incl, jnp.einsum('nbhid,nbhjd->nbhij', q, k) * decay, 0.0)

    def step(s, xs):
        qi, ki, ui, wi, gi, ai = xs
        vn = ui - wi @ s
        o = (qi * jnp.exp(gi)[..., None]) @ s + ai @ vn
        gl = gi[..., -1:]
        s = s * jnp.exp(gl)[..., None] + jnp.einsum('bhcd,bhce->bhde', ki * jnp.exp(gl - gi)[..., None], vn)
        return s, o

    s_fin, o = lax.scan(step, s0, (q, k, u, w, gc, amat))
    o = jnp.moveaxis(jnp.moveaxis(o, 3, 2), 0, 1).reshape(bsz, n, nh, dvv)
    return o, s_fin


def _gdn_mixer(qkv, z, a, b, p, j, s0):
    bsz, n, _ = qkv.shape
    qkv = jax.nn.silu(_dwconv(qkv, p['gdn_conv_w'][j])).astype(jnp.float32)
    q, k, v = jnp.split(qkv, 3, axis=-1)
    q = _l2norm(q.reshape(bsz, n, GDN_HEADS, GDN_DK)) * (GDN_DK ** -0.5)
    k = _l2norm(k.reshape(bsz, n, GDN_HEADS, GDN_DK))
    v = v.reshape(bsz, n, GDN_HEADS, GDN_DV)
    beta = jax.nn.sigmoid(b.astype(jnp.float32))
    g = -jnp.exp(p['gdn_a_log'][j].astype(jnp.float32)) * jax.nn.softplus(
        a.astype(jnp.float32) + p['gdn_dt_bias'][j].astype(jnp.float32))
    s0 = s0.astype(jnp.float32)
    o_f, s_f = _chunk_gated_delta(q, k, v, g[:, :, 0], beta[:, :, 0], s0[:, 0])
    r = lambda t: t[:, ::-1]
    o_b, s_b = _chunk_gated_delta(r(q), r(k), r(v), r(g[:, :, 1]), r(beta[:, :, 1]), s0[:, 1])
    o = _rms(o_f + r(o_b), p['gdn_norm_g'][j]) * jax.nn.silu(
        z.astype(jnp.float32).reshape(bsz, n, GDN_HEADS, GDN_DV))
    return o.reshape(bsz, n, GDN_WIDTH).astype(z.dtype), jnp.stack([s_f, s_b], axis=1)


def _ab_mix(h, s5_re0, s5_im0, gdn0, p, j):
    bsz, n, _ = h.shape
    proj = h @ p['w_in_ab'][j]
    u, qkv, z, a, b = jnp.split(proj, AB_SPLITS, axis=-1)
    a = a.reshape(bsz, n, 2, GDN_HEADS)
    b = b.reshape(bsz, n, 2, GDN_HEADS)
    ya, s5r, s5i = _s5_mixer(u, p, j, s5_re0, s5_im0)
    yb, sg = _gdn_mixer(qkv, z, a, b, p, j, gdn0)
    out = jnp.concatenate([ya, yb], axis=-1).astype(h.dtype) @ p['w_out_ab'][j]
    return out, s5r, s5i, sg


def _to_blocks(t):
    bsz, n = t.shape[:2]
    return jnp.moveaxis(t.reshape((bsz, n // Q_BLOCK, Q_BLOCK) + t.shape[2:]), 1, 0)


def _from_blocks(t):
    t = jnp.moveaxis(t, 0, 1)
    return t.reshape((t.shape[0], t.shape[1] * t.shape[2]) + t.shape[3:])


def _cd_project(h, p, j):
    bsz, n, _ = h.shape
    proj = h @ p['w_in_cd'][j]
    qc, kc, vc, qd, kd, vd = jnp.split(proj, CD_SPLITS, axis=-1)
    qc = _rms(qc.reshape(bsz, n, C_HEADS, HEAD_DIM), p['c_qn'][j])
    kc = _rms(kc.reshape(bsz, n, C_KV_HEADS, HEAD_DIM), p['c_kn'][j])
    vc = vc.reshape(bsz, n, C_KV_HEADS, HEAD_DIM)
    qd = _rms(qd.reshape(bsz, n, D_HEADS, 2, HEAD_DIM), p['d_qn'][j])
    kd = _rms(kd.reshape(bsz, n, D_HEADS, 2, HEAD_DIM), p['d_kn'][j])
    vd = vd.reshape(bsz, n, D_HEADS, D_VDIM)
    return qc, kc, vc, qd, kd, vd


def _sink_logits(sink, like):
    sk = sink.astype(jnp.float32).reshape(1, C_KV_HEADS, C_GROUP, 1, 1)
    return jnp.broadcast_to(sk, like.shape[:-1] + (1,))


def _gqa_sink_dense(q, k, v, sink):
    bsz, n = q.shape[:2]
    qg = q.reshape(bsz, n, C_KV_HEADS, C_GROUP, HEAD_DIM)
    vf = v.astype(jnp.float32)

    def block(qi):
        s = jnp.einsum('bqgrd,bkgd->bgrqk', qi, k, preferred_element_type=jnp.float32) * ATTN_SCALE
        pr = jax.nn.softmax(jnp.concatenate([s, _sink_logits(sink, s)], axis=-1), axis=-1)[..., :-1]
        return jnp.einsum('bgrqk,bkgd->bqgrd', pr, vf)

    o = _from_blocks(lax.map(block, _to_blocks(qg)))
    return o.reshape(bsz, n, C_HEADS * HEAD_DIM)


def _gqa_sink_window(q, k, v, kc, vc, sink):
    bsz, n = q.shape[:2]
    span = Q_BLOCK + 2 * WINDOW
    n_ctx = kc.shape[1]
    qg = q.reshape(bsz, n, C_KV_HEADS, C_GROUP, HEAD_DIM)
    padw = ((0, 0), (WINDOW, WINDOW), (0, 0), (0, 0))
    kp, vp = jnp.pad(k, padw), jnp.pad(v, padw)
    vcf = vc.astype(jnp.float32)

    def block(bi):
        start = bi * Q_BLOCK
        qi = lax.dynamic_slice_in_dim(qg, start, Q_BLOCK, axis=1)
        ki = lax.dynamic_slice_in_dim(kp, start, span, axis=1)
        vi = lax.dynamic_slice_in_dim(vp, start, span, axis=1).astype(jnp.float32)
        qpos = start + jnp.arange(Q_BLOCK)
        kpos = start - WINDOW + jnp.arange(span)
        ok = (jnp.abs(qpos[:, None] - kpos[None, :]) <= WINDOW) & (kpos >= 0) & (kpos < n)
        s_loc = jnp.where(ok, jnp.einsum('bqgrd,bkgd->bgrqk', qi, ki, preferred_element_type=jnp.float32) * ATTN_SCALE, NEG_INF)
        s_ctx = jnp.einsum('bqgrd,bkgd->bgrqk', qi, kc, preferred_element_type=jnp.float32) * ATTN_SCALE
        pr = jax.nn.softmax(jnp.concatenate([s_ctx, s_loc, _sink_logits(sink, s_loc)], axis=-1), axis=-1)
        return (jnp.einsum('bgrqk,bkgd->bqgrd', pr[..., :n_ctx], vcf)
                + jnp.einsum('bgrqk,bkgd->bqgrd', pr[..., n_ctx:n_ctx + span], vi))

    o = _from_blocks(lax.map(block, jnp.arange(n // Q_BLOCK)))
    return o.reshape(bsz, n, C_HEADS * HEAD_DIM)


def _diff_attn_dense(q, k, v, lam):
    vf = v.astype(jnp.float32)

    def block(qi):
        s = jnp.einsum('bqhcd,bkhcd->bhcqk', qi, k, preferred_element_type=jnp.float32) * ATTN_SCALE
        pr = jax.nn.softmax(s, axis=-1)
        att = pr[:, :, 0] - lam * pr[:, :, 1]
        return jnp.einsum('bhqk,bkhe->bqhe', att, vf)

    return _from_blocks(lax.map(block, _to_blocks(q)))


def _lambda_init(layer):
    return 0.8 - 0.6 * math.exp(-0.3 * layer)


def _diff_lambda(p, j, lam_init):
    f = lambda name: p[name][j].astype(jnp.float32)
    return jnp.exp(jnp.sum(f('d_lq1') * f('d_lk1'))) - jnp.exp(jnp.sum(f('d_lq2') * f('d_lk2'))) + lam_init


def _cd_merge(oc, od, h, p, j, lam_init):
    bsz, n, _ = h.shape
    od = _rms(od, p['d_subln'][j]) * (1.0 - lam_init)
    mix = jnp.concatenate([oc, od.reshape(bsz, n, D_HEADS * D_VDIM)], axis=-1).astype(h.dtype)
    return mix @ p['w_out_cd'][j]


def _ffn(h, p, l):
    up = h @ p['ffn_up'][l]
    a, b = jnp.split(up, 2, axis=-1)
    a = _dwconv(a, p['ffn_conv_w'][l], p['ffn_conv_b'][l])
    return (jax.nn.silu(a) * b) @ p['ffn_down'][l]


def _context_pass(x, c_ctx, p):
    bsz = x.shape[0]
    s5r, s5i, gdn, ck, cv, dk, dv = [], [], [], [], [], [], []
    for l in range(DEPTH):
        j = l // 2
        sh1, sc1, g1, sh2, sc2, g2 = _adaln(c_ctx[None, :], p['w_mod'][l], p['b_mod'][l])
        h = _modulate(x, p['norm1_g'][l], sh1, sc1)
        if l % 2 == 0:
            zs = jnp.zeros((bsz, 2, S5_GROUPS, S5_STATE), jnp.float32)
            zg = jnp.zeros((bsz, 2, GDN_HEADS, GDN_DK, GDN_DV), jnp.float32)
            out, hr, hi, sg = _ab_mix(h, zs, zs, zg, p, j)
            s5r.append(hr)
            s5i.append(hi)
            gdn.append(sg)
        else:
            qc, kc, vc, qd, kd, vd = _cd_project(h, p, j)
            lam_init = _lambda_init(l)
            oc = _gqa_sink_dense(qc, kc, vc, p['c_sink'][j])
            od = _diff_attn_dense(qd, kd, vd, _diff_lambda(p, j, lam_init))
            out = _cd_merge(oc, od, h, p, j, lam_init)
            ck.append(kc)
            cv.append(vc)
            dk.append(kd)
            dv.append(vd)
        x = (x + g1 * out).astype(h.dtype)
        x = (x + g2 * _ffn(_modulate(x, p['norm2_g'][l], sh2, sc2), p, l)).astype(h.dtype)
    st = lambda lst: jnp.stack(lst, axis=1)
    return x, st(s5r), st(s5i), st(gdn), st(ck), st(cv), st(dk), st(dv)


def _latent_pass(x, c, s5_re, s5_im, gdn, ck, cv, dk, dv, p):
    for l in range(DEPTH):
        j = l // 2
        sh1, sc1, g1, sh2, sc2, g2 = _adaln(c, p['w_mod'][l], p['b_mod'][l])
        h = _modulate(x, p['norm1_g'][l], sh1, sc1)
        if l % 2 == 0:
            out = _ab_mix(h, s5_re[:, j], s5_im[:, j], gdn[:, j], p, j)[0]
        else:
            qc, kc, vc, qd, kd, vd = _cd_project(h, p, j)
            qc, kc, qd, kd = _rope2d(qc), _rope2d(kc), _rope2d(qd), _rope2d(kd)
            lam_init = _lambda_init(l)
            oc = _gqa_sink_window(qc, kc, vc, ck[:, j], cv[:, j], p['c_sink'][j])
            od = _diff_attn_dense(qd, jnp.concatenate([dk[:, j], kd], axis=1),
                                  jnp.concatenate([dv[:, j], vd], axis=1), _diff_lambda(p, j, lam_init))
            out = _cd_merge(oc, od, h, p, j, lam_init)
        x = (x + g1 * out).astype(h.dtype)
        x = (x + g2 * _ffn(_modulate(x, p['norm2_g'][l], sh2, sc2), p, l)).astype(h.dtype)
    return x


def setup_inputs(seed: int = 0) -> dict:
    key = jax.random.key(seed)
    ks = iter(jax.random.split(key, 64))
    nrm = lambda shape, scale=1.0: scale * jax.random.normal(next(ks), shape, jnp.float32)
    gain = lambda shape: 1.0 + 0.02 * nrm(shape)
    unif = lambda shape, lo, hi: jax.random.uniform(next(ks), shape, jnp.float32, lo, hi)
    D = D_MODEL
    inp = {}
    inp['x_prompt'] = nrm((BATCH, SEQ, D))
    inp['x_sample'] = nrm((DEC_BATCH, DEC_SEQ, D))
    inp['c'] = nrm((DEC_BATCH, D))
    inp['state_s5_re'] = nrm((DEC_BATCH, N_AB, 2, S5_GROUPS, S5_STATE), 0.1)
    inp['state_s5_im'] = nrm((DEC_BATCH, N_AB, 2, S5_GROUPS, S5_STATE), 0.1)
    inp['state_gdn'] = nrm((DEC_BATCH, N_AB, 2, GDN_HEADS, GDN_DK, GDN_DV), 0.1)
    inp['cache_c_k'] = nrm((DEC_BATCH, N_CD, PAST_LEN, C_KV_HEADS, HEAD_DIM))
    inp['cache_c_v'] = nrm((DEC_BATCH, N_CD, PAST_LEN, C_KV_HEADS, HEAD_DIM))
    inp['cache_d_k'] = nrm((DEC_BATCH, N_CD, PAST_LEN, D_HEADS, 2, HEAD_DIM))
    inp['cache_d_v'] = nrm((DEC_BATCH, N_CD, PAST_LEN, D_HEADS, D_VDIM))
    inp['c_ctx'] = nrm((D,))
    inp['w_mod'] = nrm((DEPTH, D, 6 * D), 0.5 * D ** -0.5)
    inp['b_mod'] = nrm((DEPTH, 6 * D), 0.02)
    inp['norm1_g'] = gain((DEPTH, D))
    inp['norm2_g'] = gain((DEPTH, D))
    inp['w_in_ab'] = nrm((N_AB, D, AB_IN), D ** -0.5)
    inp['w_out_ab'] = nrm((N_AB, MIX_WIDTH, D), MIX_WIDTH ** -0.5)
    inp['s5_lam_re'] = -0.5 + 0.01 * nrm((N_AB, 2, S5_GROUPS, S5_STATE))
    inp['s5_lam_im'] = math.pi * jnp.arange(S5_STATE, dtype=jnp.float32) + 0.01 * nrm((N_AB, 2, S5_GROUPS, S5_STATE))
    inp['s5_log_dt'] = unif((N_AB, 2, S5_GROUPS), math.log(1e-3), math.log(1e-1))
    inp['s5_b_re'] = nrm((N_AB, 2, S5_GROUPS, S5_STATE, S5_GROUP), (2 * S5_GROUP) ** -0.5)
    inp['s5_b_im'] = nrm((N_AB, 2, S5_GROUPS, S5_STATE, S5_GROUP), (2 * S5_GROUP) ** -0.5)
    inp['s5_c_re'] = nrm((N_AB, 2, S5_GROUPS, S5_GROUP, S5_STATE), S5_STATE ** -0.5)
    inp['s5_c_im'] = nrm((N_AB, 2, S5_GROUPS, S5_GROUP, S5_STATE), S5_STATE ** -0.5)
    inp['s5_d'] = nrm((N_AB, S5_WIDTH))
    inp['s5_w_glu'] = nrm((N_AB, S5_WIDTH, S5_WIDTH), S5_WIDTH ** -0.5)
    inp['s5_b_glu'] = nrm((N_AB, S5_WIDTH), 0.02)
    inp['gdn_conv_w'] = nrm((N_AB, SHORT_CONV, 3 * GDN_WIDTH), SHORT_CONV ** -0.5)
    inp['gdn_a_log'] = jnp.log(unif((N_AB, 2, GDN_HEADS), 1.0, 16.0))
    dt = jnp.exp(unif((N_AB, 2, GDN_HEADS), math.log(1e-3), math.log(1e-1)))
    inp['gdn_dt_bias'] = dt + jnp.log(-jnp.expm1(-dt))
    inp['gdn_norm_g'] = gain((N_AB, GDN_DV))
    inp['w_in_cd'] = nrm((N_CD, D, CD_IN), D ** -0.5)
    inp['w_out_cd'] = nrm((N_CD, MIX_WIDTH, D), MIX_WIDTH ** -0.5)
    inp['c_qn'] = gain((N_CD, HEAD_DIM))
    inp['c_kn'] = gain((N_CD, HEAD_DIM))
    inp['c_sink'] = nrm((N_CD, C_HEADS))
    inp['d_qn'] = gain((N_CD, HEAD_DIM))
    inp['d_kn'] = gain((N_CD, HEAD_DIM))
    inp['d_lq1'] = nrm((N_CD, HEAD_DIM), 0.1)
    inp['d_lk1'] = nrm((N_CD, HEAD_DIM), 0.1)
    inp['d_lq2'] = nrm((N_CD, HEAD_DIM), 0.1)
    inp['d_lk2'] = nrm((N_CD, HEAD_DIM), 0.1)
    inp['d_subln'] = gain((N_CD, D_VDIM))
    inp['ffn_up'] = nrm((DEPTH, D, 2 * D_FF), D ** -0.5)
    inp['ffn_conv_w'] = nrm((DEPTH, FFN_CONV, D_FF), FFN_CONV ** -0.5)
    inp['ffn_conv_b'] = nrm((DEPTH, D_FF), 0.02)
    inp['ffn_down'] = nrm((DEPTH, D_FF, D), D_FF ** -0.5)
    return inp


def reference(x_prompt, x_sample, c, state_s5_re, state_s5_im, state_gdn, cache_c_k, cache_c_v, cache_d_k, cache_d_v,
              c_ctx, w_mod, b_mod, norm1_g, norm2_g, w_in_ab, w_out_ab, s5_lam_re, s5_lam_im, s5_log_dt,
              s5_b_re, s5_b_im, s5_c_re, s5_c_im, s5_d, s5_w_glu, s5_b_glu, gdn_conv_w, gdn_a_log, gdn_dt_bias,
              gdn_norm_g, w_in_cd, w_out_cd, c_qn, c_kn, c_sink, d_qn, d_kn, d_lq1, d_lk1, d_lq2, d_lk2, d_subln,
              ffn_up, ffn_conv_w, ffn_conv_b, ffn_down):
    p = dict(w_mod=w_mod, b_mod=b_mod, norm1_g=norm1_g, norm2_g=norm2_g, w_in_ab=w_in_ab, w_out_ab=w_out_ab,
             s5_lam_re=s5_lam_re, s5_lam_im=s5_lam_im, s5_log_dt=s5_log_dt, s5_b_re=s5_b_re, s5_b_im=s5_b_im,
             s5_c_re=s5_c_re, s5_c_im=s5_c_im, s5_d=s5_d, s5_w_glu=s5_w_glu, s5_b_glu=s5_b_glu,
             gdn_conv_w=gdn_conv_w, gdn_a_log=gdn_a_log, gdn_dt_bias=gdn_dt_bias, gdn_norm_g=gdn_norm_g,
             w_in_cd=w_in_cd, w_out_cd=w_out_cd, c_qn=c_qn, c_kn=c_kn, c_sink=c_sink, d_qn=d_qn, d_kn=d_kn,
             d_lq1=d_lq1, d_lk1=d_lk1, d_lq2=d_lq2, d_lk2=d_lk2, d_subln=d_subln,
             ffn_up=ffn_up, ffn_conv_w=ffn_conv_w, ffn_conv_b=ffn_conv_b, ffn_down=ffn_down)
    y_prompt, new_s5_re, new_s5_im, new_gdn, new_c_k, new_c_v, new_d_k, new_d_v = _context_pass(x_prompt, c_ctx, p)
    y_sample = _latent_pass(x_sample, c, state_s5_re, state_s5_im, state_gdn,
                            cache_c_k, cache_c_v, cache_d_k, cache_d_v, p)
    return (y_prompt, y_sample, new_s5_re, new_s5_im, new_gdn, new_c_k, new_c_v, new_d_k, new_d_v)
```

```python
import contextlib
import math
import numpy as np
import concourse.bass as bass
import concourse.mybir as mybir
from concourse.bass_utils import run_bass_kernel_spmd

F32 = mybir.dt.float32
BF16 = mybir.dt.bfloat16
ALU = mybir.AluOpType
AF = mybir.ActivationFunctionType
AX = mybir.AxisListType

COMPUTE = ("pe", "act", "dve", "pool")
NDMASEM = 64

D = 1024
T = 1024
DFF = 2816
NJ = 22
EPS = 1e-6


class MK:
    def __init__(self, nc):
        self.nc = nc
        self.ops = []

    def op(self, eng, fn, reads=(), writes=(), dma=False):
        self.ops.append((eng, fn, tuple(reads), tuple(writes), dma, False))

    def dma(self, out, in_, reads=(), writes=(), q="sp", **kw):
        self.ops.append((q, lambda e: e.dma_start(out=out, in_=in_, **kw), tuple(reads), tuple(writes), True, False))

    def fence(self):
        self.ops.append(("pool", None, (), (), False, True))

    @staticmethod
    def _norm(k):
        if isinstance(k, tuple):
            return (k[0], k[1:] if len(k) > 1 else None)
        return (k, None)

    def emit(self, final_keys=()):
        ops = list(self.ops)
        ops.append(("sp", None, tuple(final_keys), (), False, False))
        n = len(ops)
        st = {}
        seqcnt = 0
        deps_of = [None] * n
        dma_sem_of = [None] * n
        dma_val_of = [0] * n
        dma_sem_last = [None] * NDMASEM
        dma_sem_cnt = [0] * NDMASEM
        ndma = 0
        last_fence = None
        all_since_fence = []

        def entries(name, sub):
            d = st.setdefault(name, {})
            if sub is None:
                return list(d.values())
            out = []
            if sub in d:
                out.append(d[sub])
            if None in d:
                out.append(d[None])
            return out

        for i, (eng, fn, reads, writes, isdma, isfence) in enumerate(ops):
            deps = set()
            if isfence:
                deps.update(all_since_fence)
                if last_fence is not None:
                    deps.add(last_fence)
                all_since_fence = []
                last_fence = i
                deps_of[i] = deps
                continue
            if last_fence is not None:
                deps.add(last_fence)
            for k in reads:
                name, sub = self._norm(k)
                for e in entries(name, sub):
                    if e[0] is not None:
                        deps.add(e[0])
            for k in writes:
                name, sub = self._norm(k)
                for e in entries(name, sub):
                    if e[0] is not None:
                        deps.add(e[0])
                    deps.update(e[1])
            if isdma:
                s = ndma % NDMASEM
                ndma += 1
                if dma_sem_last[s] is not None:
                    deps.add(dma_sem_last[s])
                dma_sem_last[s] = i
                dma_sem_cnt[s] += 16
                dma_sem_of[i] = s
                dma_val_of[i] = dma_sem_cnt[s]
            deps.discard(i)
            deps_of[i] = deps
            all_since_fence.append(i)
            for k in reads:
                name, sub = self._norm(k)
                d = st.setdefault(name, {})
                e = d.setdefault(sub, [None, []])
                if not isdma:
                    e[1] = [r for r in e[1] if ops[r][4] or ops[r][0] != eng]
                e[1].append(i)
            for k in writes:
                name, sub = self._norm(k)
                d = st.setdefault(name, {})
                if sub is None:
                    d.clear()
                d[sub] = [i, []]
        need_inc = [False] * n
        for i in range(n):
            for dd in deps_of[i]:
                if not ops[dd][4]:
                    need_inc[dd] = True
        sig_cnt = {}
        sig_val = [0] * n
        for i in range(n):
            eng = ops[i][0]
            if not ops[i][4] and need_inc[i]:
                sig_cnt[eng] = sig_cnt.get(eng, 0) + 1
                sig_val[i] = sig_cnt[eng]
        known = {}
        known_dma = {}
        streams = {}
        for i in range(n):
            eng = ops[i][0]
            kn = known.setdefault(eng, {})
            kd = known_dma.setdefault(eng, set())
            waits = []
            emax = {}
            dmax = {}
            for dd in deps_of[i]:
                deng = ops[dd][0]
                if ops[dd][4]:
                    if dd in kd:
                        continue
                    kd.add(dd)
                    sl = dma_sem_of[dd]
                    if dma_val_of[dd] > dmax.get(sl, 0):
                        dmax[sl] = dma_val_of[dd]
                else:
                    if deng == eng and eng == "pe":
                        continue
                    v = sig_val[dd]
                    if v > emax.get(deng, 0):
                        emax[deng] = v
            for deng, v in emax.items():
                if kn.get(deng, 0) >= v:
                    continue
                kn[deng] = v
                waits.append(("eng", deng, v))
            for sl, v in dmax.items():
                waits.append(("dma", sl, v))
            streams.setdefault(eng, []).append((i, waits))
        ptr = {e: 0 for e in streams}
        sval = {}
        progress = True
        while progress:
            progress = False
            for e, lst in streams.items():
                while ptr[e] < len(lst):
                    i, waits = lst[ptr[e]]
                    ok = True
                    for w in waits:
                        if sval.get((w[0], w[1]), 0) < w[2]:
                            ok = False
                            break
                    if not ok:
                        break
                    if ops[i][4]:
                        sval[("dma", dma_sem_of[i])] = sval.get(("dma", dma_sem_of[i]), 0) + 16
                    elif need_inc[i]:
                        sval[("eng", ops[i][0])] = sval.get(("eng", ops[i][0]), 0) + 1
                    ptr[e] += 1
                    progress = True
        for e, lst in streams.items():
            if ptr[e] < len(lst):
                raise RuntimeError("deadlock in sync plan: engine %s stuck at op %d waits %s" % (e, lst[ptr[e]][0], lst[ptr[e]][1]))
        self.stats = {e: len(l) for e, l in streams.items()}
        nc = self.nc
        with contextlib.ExitStack() as es:
            esem = {e: es.enter_context(nc.semaphore("s_" + e)) for e in COMPUTE}
            dsem = [es.enter_context(nc.semaphore("d%d" % k)) for k in range(NDMASEM)]
            block = es.enter_context(nc.Block())

            def run(engname):
                def body(e):
                    for (i, waits) in streams.get(engname, []):
                        for w in waits:
                            if w[0] == "dma":
                                e.wait_ge(dsem[w[1]], w[2])
                            else:
                                e.wait_ge(esem[w[1]], w[2])
                        eng, fn, reads, writes, isdma, isfence = ops[i]
                        if fn is None:
                            if need_inc[i]:
                                e.nop().then_inc(esem[eng], 1)
                            continue
                        ins = fn(e)
                        if isdma:
                            ins.then_inc(dsem[dma_sem_of[i]], 16)
                        elif need_inc[i]:
                            ins.then_inc(esem[eng], 1)
                return body

            block.tensor(run("pe"))
            block.scalar(run("act"))
            block.vector(run("dve"))
            block.gpsimd(run("pool"))
            block.sync(run("sp"))
        return n


def rap(t, offset, pattern):
    th = t.tensor if hasattr(t, "tensor") else t
    return bass.AP(th, offset, [list(p) for p in pattern])


class Prog:
    def __init__(self, nc):
        self.nc = nc
        self.mk = MK(nc)
        self.es = contextlib.ExitStack()
        self.dram = {}
        self.psn = 0
        self.evn = 0
        self.outs = []
        self.stn = 0
        self.atop = 0
        self.psmod = 8

    def din(self, name, shape):
        a = self.nc.dram_tensor(name, list(shape), F32, kind="ExternalInput")
        self.dram[name] = a
        return a

    def dout(self, name, shape):
        a = self.nc.dram_tensor(name, list(shape), F32, kind="ExternalOutput")
        self.dram[name] = a
        return a

    def sb(self, name, shape, dt=F32):
        return self.es.enter_context(self.nc.sbuf_tensor(name, list(shape), dt))

    def psum(self):
        k = self.psn % self.psmod
        self.psn += 1
        return self.ps[k], "ps%d" % k

    def evac_eng(self):
        self.evn += 1
        return "dve" if self.evn % 2 else "act"

    def copy(self, eng, out, in_, reads, writes):
        if eng == "act":
            self.mk.op("act", lambda e: e.activation(out=out, in_=in_, func=AF.Copy), reads, writes)
        else:
            self.mk.op(eng, lambda e: e.tensor_copy(out=out, in_=in_), reads, writes)

    def mm(self, out, lhsT, rhs, start, stop, reads, writes):
        self.mk.op("pe", lambda e: e.matmul(out, lhsT=lhsT, rhs=rhs, start=start, stop=stop), reads, writes)

    def load_T(self, dst, dram_t, offset, n, wkey):
        i = self.stn % 2
        self.stn += 1
        stg, sk = self.stg[i], "stg%d" % i
        self.mk.dma(stg[0:n, :], rap(dram_t, offset, [[128, n], [1, 128]]), writes=[sk])
        pst, psk = self.psum()
        self.transpose(pst[:, 0:n], stg[0:n, :], self.ident[0:n, 0:n], reads=[sk, "ident"], writes=[psk])
        self.copy("dve", dst, pst[:, 0:n], reads=[psk], writes=[wkey])

    def aalloc(self, shape, dt=F32):
        n = 1
        for x in shape[1:]:
            n *= x
        words = n if dt == F32 else (n + 1) // 2
        words += words % 2
        off = self.atop
        self.atop += words
        assert self.atop <= self.asize, ("arena overflow", self.atop, self.asize)
        v = self.arena[:, off:off + words]
        if dt != F32:
            v = v.bitcast(dt)[:, 0:n]
        else:
            v = v[:, 0:n]
        if len(shape) > 2:
            names = ["d%d" % i for i in range(len(shape) - 1)]
            kw = {names[i]: shape[i + 1] for i in range(1, len(names))}
            v = v.rearrange("p (" + " ".join(names) + ") -> p " + " ".join(names), **kw)
        return v

    def transpose(self, out, in_, ident, reads, writes):
        self.mk.op("pe", lambda e: e.transpose(out, in_, ident), reads, writes)


ATTN_SCALE = 64 ** -0.5
LAM_INIT1 = 0.8 - 0.6 * math.exp(-0.3 * 1)
CD_IN = 2304
AB_IN = 2576
ASIZE = 28160


def build(nc):
    P = Prog(nc)
    mk = P.mk
    xin = P.din("xin", [2, T, D])
    cvec = P.din("cvec", [2, D])
    w_mod = P.din("w_mod", [2, D, 6 * D])
    b_mod = P.din("b_mod", [2, 6 * D])
    norm1_g = P.din("norm1_g", [2, D])
    norm2_g = P.din("norm2_g", [2, D])
    ffn_up = P.din("ffn_up", [2, D, 2 * DFF])
    ffn_conv_w = P.din("ffn_conv_w", [2, 3, DFF])
    ffn_conv_b = P.din("ffn_conv_b", [2, DFF])
    ffn_down = P.din("ffn_down", [2, DFF, D])
    w_in_cd = P.din("w_in_cd", [D, CD_IN])
    w_out_cd = P.din("w_out_cd", [D, D])
    w_out_ab = P.din("w_out_ab", [D, D])
    gcat = P.din("gcat", [4, 64])
    c_sink = P.din("c_sink", [8])
    d_subln = P.din("d_subln", [128])
    lqk = P.din("lqk", [4, 64])
    ccat = P.din("ccat", [512, 1280])
    c_ident = P.din("c_ident", [128, 128])
    w_in_ab = P.din("w_in_ab", [D, AB_IN])
    gdn_conv_w = P.din("gdn_conv_w", [3, 1536])
    gdn_a_log = P.din("gdn_a_log", [8])
    gdn_dt_bias = P.din("gdn_dt_bias", [8])
    gdn_norm_g = P.din("gdn_norm_g", [128])
    gstate = P.din("gstate", [2, 4, 128, 128])
    c_tri = P.din("c_tri", [6, 128, 128])
    s5_lam_re = P.din("s5_lam_re", [64, 64])
    s5_lam_im = P.din("s5_lam_im", [64, 64])
    s5_log_dt = P.din("s5_log_dt", [64])
    s5_b_re = P.din("s5_b_re", [64, 64, 16])
    s5_b_im = P.din("s5_b_im", [64, 64, 16])
    s5_c_re = P.din("s5_c_re", [64, 16, 64])
    s5_c_im = P.din("s5_c_im", [64, 16, 64])
    s5_d = P.din("s5_d", [512])
    s5_w_glu = P.din("s5_w_glu", [512, 512])
    s5_b_glu = P.din("s5_b_glu", [512])
    s5h_re = P.din("s5h_re", [64, 64])
    s5h_im = P.din("s5h_im", [64, 64])
    c_kk = P.din("c_kk", [128, 8])
    c_arow = P.din("c_arow", [128, 33])
    c_mfb = P.din("c_mfb", [2, 128, 128])
    c_sel = P.din("c_sel", [4, 128, 128])
    c_rope = P.din("c_rope", [4, T, 16])
    c_wmask = P.din("c_wmask", [128, 384])
    yout = P.dout("yout", [2, T, D])
    s5scr = nc.dram_tensor("s5scr", [2, 128, 7744], F32, kind="Internal")
    o_ck = P.dout("o_ck", [T, 128])
    o_cv = P.dout("o_cv", [T, 128])
    o_dk = P.dout("o_dk", [T, 512])
    o_dv = P.dout("o_dv", [T, 512])
    o_gdn = P.dout("o_gdn", [4, 2, 4, 128, 128])
    o_s5r = P.dout("o_s5r", [4, 2, 32, 64])
    o_s5i = P.dout("o_s5i", [4, 2, 32, 64])

    P.ps = [P.es.enter_context(nc.psum_tensor("ps%d" % k, [128, 512], F32)) for k in range(8)]
    ident = P.sb("ident", [128, 128])
    identb = P.sb("identb", [128, 128], BF16)
    P.ident = ident
    P.stg = [P.sb("stg%d" % i, [128, 128]) for i in range(2)]
    onesD = P.sb("onesD", [128, 128])
    epsc = P.sb("epsc", [128, 1])
    xT = P.sb("xT", [128, 8, T])
    hT = P.sb("hT", [128, 8, T], BF16)
    mixT = P.sb("mixT", [128, 8, T], BF16)
    wbuf = [P.sb("wbuf%d" % i, [128, 8, 512], BF16) for i in range(2)]
    scv = P.sb("scv", [128, 8, 2])
    modv = P.sb("modv", [128, 2, 48, 2])
    bmod = P.sb("bmod", [128, 2, 48])
    ng = P.sb("ng", [128, 2, 2, 8])
    gs = P.sb("gs", [128, 2, 2, 2, 8])
    cw = P.sb("cw", [128, 2, 3, NJ])
    cb = P.sb("cb", [128, 2, NJ])
    sq = [P.sb("sq%d" % i, [128, 512]) for i in range(2)]
    rstd = P.sb("rstd", [128, 512])
    tmpn = [P.sb("tmpn%d" % i, [128, 512]) for i in range(2)]
    P.arena = P.sb("arena", [128, ASIZE])
    P.asize = ASIZE

    mk.dma(ident[:], c_ident.ap(), writes=["ident"])
    mk.op("dve", lambda e: e.tensor_copy(out=identb[:], in_=ident[:]), reads=["ident"], writes=["identb"])
    mk.op("pool", lambda e: e.memset(onesD[:], 1.0 / D), writes=["onesD"])
    mk.op("pool", lambda e: e.memset(epsc[:], EPS), writes=["epsc"])
    for v in range(2):
        P.load_T(scv[:, :, v], cvec, v * D, 8, ("scv", v))
    for l in range(2):
        P.load_T(bmod[:, l, :], b_mod, l * 6 * D, 48, ("bmod", l))
        P.load_T(ng[:, 0, l, :], norm1_g, l * D, 8, ("ng", 0, l))
        P.load_T(ng[:, 1, l, :], norm2_g, l * D, 8, ("ng", 1, l))
        for tap in range(3):
            P.load_T(cw[:, l, tap, :], ffn_conv_w, (l * 3 + tap) * DFF, NJ, ("cw", l, tap))
        P.load_T(cb[:, l, :], ffn_conv_b, l * DFF, NJ, ("cb", l))
    mk.op("act", lambda e: e.activation(out=scv[:], in_=scv[:], func=AF.Silu), reads=["scv"], writes=["scv"])

    wmb = [P.aalloc([128, 8, 512], BF16) for _ in range(2)]
    scvb = P.aalloc([128, 8, 2], BF16)
    adaln_top = P.atop
    mk.op("dve", lambda e: e.tensor_copy(out=scvb, in_=scv[:]), reads=["scv"], writes=["scvb"])

    def adaln_gen():
        wn = 0
        for l in range(2):
            pst, psk = P.ps[7], "ps7"
            for jg in range(12):
                wb = wmb[wn % 2]
                wk = "wmb%d" % (wn % 2)
                wn += 1
                mk.dma(wb, rap(w_mod, l * D * 6 * D + jg * 512, [[6 * D, 128], [128 * 6 * D, 8], [1, 512]]), writes=[wk], q="pool")
                for jj in range(4):
                    j = jg * 4 + jj
                    for kc in range(8):
                        P.mm(pst[:, 2 * j:2 * j + 2], wb[:, kc, jj * 128:(jj + 1) * 128], scvb[:, kc, :],
                             kc == 0, kc == 7, reads=[wk, "scvb"], writes=[psk])
                yield
            mk.op("dve", lambda e, l=l, pst=pst: e.tensor_tensor(
                out=modv[:, l, :, :], in0=pst[:, 0:96].rearrange("p (j v) -> p j v", v=2),
                in1=rap(bmod, l * 48, [[96, 128], [1, 48], [0, 2]]), op=ALU.add),
                reads=[psk, ("bmod", l)], writes=[("modv", l)])
            yield
    def adaln_finish():
        for l in range(2):
            for which in range(2):
                for v in range(2):
                    sc_ap = rap(modv, (l * 48 + (1 + 3 * which) * 8) * 2 + v, [[192, 128], [2, 8]])
                    mk.op("dve", lambda e, l=l, which=which, v=v, sc_ap=sc_ap: e.scalar_tensor_tensor(
                        out=gs[:, l, which, v, :], in0=sc_ap, scalar=1.0, in1=ng[:, which, l, :],
                        op0=ALU.add, op1=ALU.mult),
                        reads=[("modv", l), ("ng", which, l)], writes=[("gs", l, which, v)])
        mk.fence()
        P.atop = 0

    def mod_col(l, split, v, c):
        return rap(modv, (l * 48 + split * 8 + c) * 2 + v, [[192, 128], [1, 1]])

    state = {"w": 0, "wd": 0, "sq": 0, "tn": 0}

    def load_w(dram_t, base_off, rowlen, ncols):
        i = state["w"] % 2
        state["w"] += 1
        mk.dma(wbuf[i][:, :, 0:ncols], rap(dram_t, base_off, [[rowlen, 128], [128 * rowlen, 8], [1, ncols]]),
               writes=["wbuf%d" % i], q="pool")
        return wbuf[i], "wbuf%d" % i

    def norm_mod(l, which, v):
        for th in range(2):
            ts = slice(th * 512, (th + 1) * 512)
            pst, psk = P.psum()
            for c in range(8):
                s = state["sq"] % 2
                state["sq"] += 1
                if c % 2 == 0:
                    mk.op("pool", lambda e, s=s, c=c, ts=ts: e.tensor_tensor(out=sq[s][:], in0=xT[:, c, ts], in1=xT[:, c, ts], op=ALU.mult),
                          reads=[("xT", th)], writes=["sq%d" % s])
                else:
                    mk.op("act", lambda e, s=s, c=c, ts=ts: e.activation(out=sq[s][:], in_=xT[:, c, ts], func=AF.Square),
                          reads=[("xT", th)], writes=["sq%d" % s])
                P.mm(pst[:], onesD[:], sq[s][:], c == 0, c == 7, reads=["onesD", "sq%d" % s], writes=[psk])
            mk.op("act", lambda e, pst=pst: e.activation(out=rstd[:], in_=pst[:], func=AF.Ln, bias=epsc[:, 0:1]),
                  reads=[psk, "epsc"], writes=["rstd"])
            mk.op("act", lambda e: e.activation(out=rstd[:], in_=rstd[:], func=AF.Exp, scale=-0.5), reads=["rstd"], writes=["rstd"])
            for c in range(8):
                s = state["tn"] % 2
                state["tn"] += 1
                mk.op("dve", lambda e, s=s, c=c, ts=ts: e.tensor_tensor(out=tmpn[s][:], in0=xT[:, c, ts], in1=rstd[:], op=ALU.mult),
                      reads=[("xT", th), "rstd"], writes=["tmpn%d" % s])
                g_ap = rap(gs, (((l * 2 + which) * 2 + v) * 8 + c), [[64, 128], [1, 1]])
                sh_ap = mod_col(l, 3 * which, v, c)
                mk.op("act", lambda e, s=s, c=c, ts=ts, g_ap=g_ap, sh_ap=sh_ap: e.activation(
                    out=hT[:, c, ts], in_=tmpn[s][:], func=AF.Identity, scale=g_ap, bias=sh_ap),
                    reads=["tmpn%d" % s, ("gs", l, which, v), ("modv", l)], writes=[("hT", th)])

    def out_proj(w_dram, l, v):
        for mg in range(2):
            w, wk = load_w(w_dram, mg * 512, D, 512)
            for mm_ in range(4):
                m = mg * 4 + mm_
                for th in range(2):
                    ts = slice(th * 512, (th + 1) * 512)
                    po, pok = P.psum()
                    for kc in range(8):
                        P.mm(po[:], w[:, kc, mm_ * 128:(mm_ + 1) * 128], mixT[:, kc, ts], kc == 0, kc == 7,
                             reads=[wk, "mixT"], writes=[pok])
                    gate = mod_col(l, 2, v, m)
                    mk.op("dve", lambda e, po=po, gate=gate, m=m, ts=ts: e.scalar_tensor_tensor(
                        out=xT[:, m, ts], in0=po[:], scalar=gate, in1=xT[:, m, ts], op0=ALU.mult, op1=ALU.add),
                        reads=[pok, ("modv", l), ("xT", th)], writes=[("xT", th)])

    def ffn(l, v, seqlen):
        P.atop = 0
        hid = P.aalloc([128, NJ, T], BF16)
        wdn = [P.aalloc([128, NJ, 256], BF16) for _ in range(2)]
        abuf = [P.aalloc([128, T]) for _ in range(2)]
        cbuf = [P.aalloc([128, T]) for _ in range(2)]
        wx = [P.aalloc([128, 8, 512], BF16) for _ in range(2)]
        ring = [(wbuf[0], "wbuf0"), (wbuf[1], "wbuf1"), (wx[0], "ffn_wx0"), (wx[1], "ffn_wx1")]
        rstate = {"n": 0}

        def load_up(jg):
            nj = 4 if jg < 5 else 2
            res = []
            for part in range(2):
                w_, wk_ = ring[rstate["n"] % 4]
                rstate["n"] += 1
                mk.dma(w_[:, :, 0:nj * 128], rap(ffn_up, l * D * 2 * DFF + part * DFF + jg * 512, [[2 * DFF, 128], [128 * 2 * DFF, 8], [1, nj * 128]]),
                       writes=[wk_], q="pool")
                res.append((w_, wk_))
            return res

        def load_dn(mg):
            i = state["wd"] % 2
            state["wd"] += 1
            wd, wdk = wdn[i], "wdn%d" % i
            for jh in range(2):
                mk.dma(wd[:, jh * 11:(jh + 1) * 11, :],
                       rap(ffn_down, l * DFF * D + jh * 11 * 128 * D + mg * 256, [[D, 128], [128 * D, 11], [1, 256]]),
                       writes=[(wdk, jh)], q="pool")
            return wd, wdk

        pend = [load_up(0), load_up(1)]
        dn_pend = [load_dn(0)]
        for jg in range(6):
            nj = 4 if jg < 5 else 2
            (wa, wak), (wb_, wbk) = pend.pop(0)
            if jg == 1:
                dn_pend.append(load_dn(1))
            for jj in range(nj):
                j = jg * 4 + jj
                s = j % 2
                ab, abk = abuf[s], "abuf%d" % s
                cbf, cbk = cbuf[s], "cbuf%d" % s
                pbs = []
                for th in range(2):
                    ts = slice(th * 512, (th + 1) * 512)
                    pa, pak = P.psum()
                    pb, pbk = P.psum()
                    pbs.append((pb, pbk))
                    for kc in range(8):
                        P.mm(pa[:], wa[:, kc, jj * 128:(jj + 1) * 128], hT[:, kc, ts], kc == 0, kc == 7,
                             reads=[wak, ("hT", th)], writes=[pak])
                    for kc in range(8):
                        P.mm(pb[:], wb_[:, kc, jj * 128:(jj + 1) * 128], hT[:, kc, ts], kc == 0, kc == 7,
                             reads=[wbk, ("hT", th)], writes=[pbk])
                    mk.op("act", lambda e, ab=ab, pa=pa, th=th: e.activation(
                        out=ab[:, th * 512:(th + 1) * 512], in_=pa[:], func=AF.Copy),
                        reads=[pak], writes=[(abk, th)])
                w0 = rap(cw, (l * 3 + 0) * NJ + j, [[2 * 3 * NJ, 128], [1, 1]])
                w1 = rap(cw, (l * 3 + 1) * NJ + j, [[2 * 3 * NJ, 128], [1, 1]])
                w2 = rap(cw, (l * 3 + 2) * NJ + j, [[2 * 3 * NJ, 128], [1, 1]])
                bb = rap(cb, l * NJ + j, [[2 * NJ, 128], [1, 1]])
                mk.op("dve", lambda e, cbf=cbf, ab=ab, w1=w1, bb=bb: e.tensor_scalar(
                    out=cbf, in0=ab, scalar1=w1, scalar2=bb, op0=ALU.mult, op1=ALU.add),
                    reads=[abk, ("cw", l, 1), ("cb", l)], writes=[cbk])
                c3 = cbf.rearrange("p (s t) -> p s t", t=seqlen)
                a3 = ab.rearrange("p (s t) -> p s t", t=seqlen)
                mk.op("dve", lambda e, c3=c3, a3=a3, w0=w0: e.scalar_tensor_tensor(
                    out=c3[:, :, 1:seqlen], in0=a3[:, :, 0:seqlen - 1], scalar=w0, in1=c3[:, :, 1:seqlen],
                    op0=ALU.mult, op1=ALU.add), reads=[abk, cbk, ("cw", l, 0)], writes=[cbk])
                mk.op("dve", lambda e, c3=c3, a3=a3, w2=w2: e.scalar_tensor_tensor(
                    out=c3[:, :, 0:seqlen - 1], in0=a3[:, :, 1:seqlen], scalar=w2, in1=c3[:, :, 0:seqlen - 1],
                    op0=ALU.mult, op1=ALU.add), reads=[abk, cbk, ("cw", l, 2)], writes=[cbk])
                mk.op("act", lambda e, cbf=cbf: e.activation(out=cbf, in_=cbf, func=AF.Silu),
                      reads=[cbk], writes=[cbk])
                for th in range(2):
                    ts = slice(th * 512, (th + 1) * 512)
                    pb, pbk = pbs[th]
                    mk.op("dve", lambda e, cbf=cbf, pb=pb, j=j, ts=ts: e.tensor_tensor(
                        out=hid[:, j, ts], in0=cbf[:, ts], in1=pb[:], op=ALU.mult),
                        reads=[cbk, pbk], writes=[("hid", j, th)])
            if jg + 2 < 6:
                pend.append(load_up(jg + 2))
        for mg in range(4):
            wd, wdk = dn_pend.pop(0)
            for mm_ in range(2):
                m = mg * 2 + mm_
                for th in range(2):
                    ts = slice(th * 512, (th + 1) * 512)
                    po, pok = P.psum()
                    for j in range(NJ):
                        P.mm(po[:], wd[:, j, mm_ * 128:(mm_ + 1) * 128], hid[:, j, ts], j == 0, j == NJ - 1,
                             reads=[(wdk, j // 11), ("hid", j, th)], writes=[pok])
                    gate = mod_col(l, 5, v, m)
                    mk.op("dve", lambda e, po=po, gate=gate, m=m, ts=ts: e.scalar_tensor_tensor(
                        out=xT[:, m, ts], in0=po[:], scalar=gate, in1=xT[:, m, ts], op0=ALU.mult, op1=ALU.add),
                        reads=[pok, ("modv", l), ("xT", th)], writes=[("xT", th)])
            if mg + 2 < 4:
                dn_pend.append(load_dn(mg + 2))
        mk.fence()
        P.atop = 0

    def attn_layer(v):
        l = 1
        P.atop = 0
        koff = 0 if v == 0 else 4
        NKT = 8 + koff
        NK = NKT * 128
        qcT = P.aalloc([128, 4, T], BF16)
        kcT2 = P.aalloc([128, 2, NK], BF16)
        qdT = P.aalloc([128, 4, T], BF16)
        kdT = P.aalloc([128, 4, NK], BF16)
        vc_b = P.aalloc([128, NKT, 2, 65], BF16)
        vd_b = P.aalloc([128, NKT, 4, 129], BF16)
        gain_t = P.aalloc([128, 4, 64])
        sink_t = P.aalloc([128, 8])
        dsubg = P.aalloc([128, 128])
        lqk_t = P.aalloc([128, 4, 64])
        lstat = P.aalloc([128, 8])
        rope_t = P.aalloc([128, 4, 8, 16])
        wmask = P.aalloc([128, 384])
        qn = P.aalloc([128, 8, 16])
        nbq = P.aalloc([128, 8, 16])
        kmx = P.aalloc([128, 128])
        ksq = P.aalloc([128, 128])
        kcol = P.aalloc([128, 2])
        mark = P.atop
        wcd = P.aalloc([128, 8, 1280], BF16)
        pjb = [P.aalloc([128, CD_IN]) for _ in range(2)]
        sqt = P.aalloc([128, 1024])
        ms26 = P.aalloc([128, 32])
        rtmp = [P.aalloc([128, 16, 16]) for _ in range(4)]
        A = "at%d_" % v

        mk.dma(gain_t, rap(gcat, 0, [[0, 128], [64, 4], [1, 64]]), writes=[A + "gain"])
        mk.dma(sink_t, rap(c_sink, 0, [[0, 128], [1, 8]]), writes=[A + "sink"])
        mk.dma(dsubg, rap(d_subln, 0, [[0, 128], [1, 128]]), writes=[A + "dsubg"])
        mk.dma(lqk_t, rap(lqk, 0, [[0, 128], [64, 4], [1, 64]]), writes=[A + "lqk"])
        mk.dma(wmask, c_wmask.ap(), writes=[A + "wmask"])
        if v == 1:
            for i in range(4):
                mk.dma(rope_t[:, i, :, :], rap(c_rope, i * T * 16, [[16, 128], [128 * 16, 8], [1, 16]]),
                       writes=[(A + "rope", i)])
        mk.op("pool", lambda e: e.tensor_scalar(out=dsubg, in0=dsubg, scalar1=1.0 - LAM_INIT1, scalar2=None, op0=ALU.mult),
              reads=[A + "dsubg"], writes=[A + "dsubg"])
        for i in range(2):
            mk.op("dve", lambda e, i=i: e.tensor_tensor(out=lqk_t[:, 2 * i, :], in0=lqk_t[:, 2 * i, :], in1=lqk_t[:, 2 * i + 1, :], op=ALU.mult),
                  reads=[A + "lqk"], writes=[A + "lqk"])
            mk.op("dve", lambda e, i=i: e.tensor_reduce(out=lstat[:, i:i + 1], in_=lqk_t[:, 2 * i, :], axis=AX.X, op=ALU.add),
                  reads=[A + "lqk"], writes=[A + "lstat"])
        mk.op("act", lambda e: e.activation(out=lstat[:, 0:2], in_=lstat[:, 0:2], func=AF.Exp), reads=[A + "lstat"], writes=[A + "lstat"])
        mk.op("dve", lambda e: e.tensor_tensor(out=lstat[:, 2:3], in0=lstat[:, 1:2], in1=lstat[:, 0:1], op=ALU.subtract),
              reads=[A + "lstat"], writes=[A + "lstat"])
        mk.op("dve", lambda e: e.tensor_scalar(out=lstat[:, 4:5], in0=lstat[:, 2:3], scalar1=-LAM_INIT1, scalar2=None, op0=ALU.add),
              reads=[A + "lstat"], writes=[A + "lstat"])
        neg_lam = lstat[:, 4:5]

        wA, wAk = load_w(w_in_cd, 0, CD_IN, 512)
        wB, wBk = load_w(w_in_cd, 512, CD_IN, 512)
        mk.dma(wcd, rap(w_in_cd, 1024, [[CD_IN, 128], [128 * CD_IN, 8], [1, 1280]]), writes=[A + "wcd"], q="pool")
        groups = [(wA, wAk, 0, 0, 512), (wB, wBk, 0, 512, 512), (wcd, A + "wcd", 0, 1024, 512),
                  (wcd, A + "wcd", 512, 1536, 512), (wcd, A + "wcd", 1024, 2048, 256)]

        kfirst = [True]
        mk.op("dve", lambda e: e.memset(kmx, 0.0), writes=[A + "kmx"])

        def build_operands(pj, pjk, kt, tt, with_q):
            mk.op("act", lambda e: e.activation(out=sqt[:, 0:128], in_=pj[:, 512:640], func=AF.Square), reads=[pjk], writes=[A + "sqt"])
            mk.op("act", lambda e: e.activation(out=sqt[:, 128:640], in_=pj[:, 1280:1792], func=AF.Square), reads=[pjk], writes=[A + "sqt"])
            mk.op("dve", lambda e: e.tensor_reduce(out=ksq[:, 0:10], in_=sqt[:, 0:640].rearrange("p (h d) -> p h d", d=64), axis=AX.X, op=ALU.add),
                  reads=[A + "sqt"], writes=[A + "ksq"])
            mk.op("dve", lambda e: e.tensor_tensor(out=kmx[:, 0:10], in0=kmx[:, 0:10], in1=ksq[:, 0:10], op=ALU.max), reads=[A + "kmx", A + "ksq"], writes=[A + "kmx"])
            if with_q:
                mk.op("act", lambda e: e.activation(out=sqt[:, 0:512], in_=pj[:, 0:512], func=AF.Square), reads=[pjk], writes=[A + "sqt"])
                mk.op("act", lambda e: e.activation(out=sqt[:, 512:1024], in_=pj[:, 768:1280], func=AF.Square), reads=[pjk], writes=[A + "sqt"])
                mk.op("dve", lambda e: e.tensor_reduce(out=qn[:, tt, :], in_=sqt[:, 0:1024].rearrange("p (h d) -> p h d", d=64), axis=AX.X, op=ALU.add),
                      reads=[A + "sqt"], writes=[A + "qn"])
            jobs = []
            if with_q:
                jobs.append((qcT, [pj[:, j * 128:(j + 1) * 128] for j in range(4)], tt, A + "qcT"))
                jobs.append((qdT, [pj[:, 768 + j * 128:768 + (j + 1) * 128] for j in range(4)], tt, A + "qdT"))
            pst, psk = P.psum()
            for g in range(2):
                for dd in range(2):
                    P.mm(pst[dd * 64:(dd + 1) * 64, g * 128:(g + 1) * 128], pj[:, 512 + g * 64:512 + (g + 1) * 64],
                         ident[:], True, True, reads=[pjk, "ident"], writes=[psk])
            P.copy(P.evac_eng(), kcT2[:, 0:2, kt * 128:(kt + 1) * 128],
                   pst[:, 0:256].rearrange("p (c t) -> p c t", t=128), reads=[psk], writes=[(A + "kcT2", kt)])
            jobs.append((kdT, [pj[:, 1280 + j * 128:1280 + (j + 1) * 128] for j in range(4)], kt, A + "kdT"))
            for dst, srcs, col, dk in jobs:
                pst, psk = P.psum()
                n = len(srcs)
                for j, sap in enumerate(srcs):
                    P.transpose(pst[:, j * 128:(j + 1) * 128], sap, ident[:], reads=[pjk, "ident"], writes=[psk])
                P.copy(P.evac_eng(), dst[:, 0:n, col * 128:(col + 1) * 128],
                       pst[:, 0:n * 128].rearrange("p (c t) -> p c t", t=128), reads=[psk], writes=[(dk, col)])
            mk.op("pool", lambda e: e.memset(vc_b[:, kt, :, 64:65], 1.0), writes=[(A + "vc_b", kt)])
            mk.op("pool", lambda e: e.memset(vd_b[:, kt, :, 128:129], 1.0), writes=[(A + "vd_b", kt)])
            mk.op("pool", lambda e: e.tensor_copy(out=vc_b[:, kt, :, 0:64], in_=pj[:, 640:768].rearrange("p (g d) -> p g d", d=64)), reads=[pjk], writes=[(A + "vc_b", kt)])
            mk.op("pool", lambda e: e.tensor_copy(out=vd_b[:, kt, :, 0:128], in_=pj[:, 1792:2304].rearrange("p (g d) -> p g d", d=128)), reads=[pjk], writes=[(A + "vd_b", kt)])

        pjn = 0
        if v == 1:
            for ct in range(4):
                pj, pjk = pjb[pjn % 2], A + "pj%d" % (pjn % 2)
                pjn += 1
                r0 = ct * 128
                mk.dma(pj[:, 512:768], rap(ccat, r0 * 1280, [[1280, 128], [1, 256]]), writes=[pjk])
                mk.dma(pj[:, 1280:2304], rap(ccat, r0 * 1280 + 256, [[1280, 128], [1, 1024]]), writes=[pjk])
                build_operands(pj, pjk, ct, None, False)
        for tt in range(8):
            pj, pjk = pjb[pjn % 2], A + "pj%d" % (pjn % 2)
            pjn += 1
            for (w, wk, wc0, c0, n) in groups:
                pst, psk = P.psum()
                for kc in range(8):
                    P.mm(pst[:, 0:n], hT[:, kc, tt * 128:(tt + 1) * 128], w[:, kc, wc0:wc0 + n], kc == 0, kc == 7,
                         reads=[wk, ("hT", tt // 4)], writes=[psk])
                P.copy(P.evac_eng(), pj[:, c0:c0 + n], pst[:, 0:n], reads=[psk], writes=[pjk])
            for (c0, n, h0, g0) in ((0, 640, 0, 0), (768, 1024, 10, 640)):
                H = n // 64
                mk.op("act", lambda e, c0=c0, n=n, pj=pj: e.activation(out=sqt[:, 0:n], in_=pj[:, c0:c0 + n], func=AF.Square),
                      reads=[pjk], writes=[A + "sqt"])
                mk.op("dve", lambda e, n=n, h0=h0, H=H: e.tensor_reduce(
                    out=ms26[:, h0:h0 + H], in_=sqt[:, 0:n].rearrange("p (h d) -> p h d", d=64), axis=AX.X, op=ALU.add),
                    reads=[A + "sqt"], writes=[A + "ms26"])
            mk.op("act", lambda e: e.activation(out=ms26[:, 0:26], in_=ms26[:, 0:26], func=AF.Sqrt, scale=1.0 / 64, bias=EPS),
                  reads=[A + "ms26"], writes=[A + "ms26"])
            mk.op("dve", lambda e: e.reciprocal(out=ms26[:, 0:26], in_=ms26[:, 0:26]), reads=[A + "ms26"], writes=[A + "ms26"])
            for (c0, n, h0, g0) in ((0, 640, 0, 0), (768, 1024, 10, 640)):
                H = n // 64
                bc = rap(P.arena, ms26.offset + h0, [[ASIZE, 128], [1, H], [0, 64]])
                mk.op("dve", lambda e, c0=c0, n=n, pj=pj, bc=bc: e.tensor_tensor(
                    out=pj[:, c0:c0 + n].rearrange("p (h d) -> p h d", d=64),
                    in0=pj[:, c0:c0 + n].rearrange("p (h d) -> p h d", d=64), in1=bc, op=ALU.mult),
                    reads=[pjk, A + "ms26"], writes=[pjk])
            for (c0, H, gi) in ((0, 8, 0), (512, 2, 1), (768, 8, 2), (1280, 8, 3)):
                gb = rap(P.arena, gain_t.offset + gi * 64, [[ASIZE, 128], [0, H], [1, 64]])
                mk.op("pool", lambda e, c0=c0, H=H, gb=gb, pj=pj: e.tensor_tensor(
                    out=pj[:, c0:c0 + H * 64].rearrange("p (h d) -> p h d", d=64),
                    in0=pj[:, c0:c0 + H * 64].rearrange("p (h d) -> p h d", d=64), in1=gb, op=ALU.mult),
                    reads=[pjk, A + "gain"], writes=[pjk])
            if v == 0:
                r = slice(tt * 128, (tt + 1) * 128)
                for (dst, c0, n, nm) in ((o_ck, 512, 128, "o_ck"), (o_cv, 640, 128, "o_cv"),
                                         (o_dk, 1280, 512, "o_dk"), (o_dv, 1792, 512, "o_dv")):
                    ok = (nm, tt)
                    mk.dma(dst.ap()[r, :], pj[:, c0:c0 + n], reads=[pjk], writes=[ok])
                    P.outs.append(ok)
            else:
                for (c0, H) in ((0, 10), (768, 16)):
                    for half in range(2):
                        cs = rap(P.arena, rope_t.offset + ((2 * half) * 8 + tt) * 16, [[ASIZE, 128], [0, H], [1, 16]])
                        sn = rap(P.arena, rope_t.offset + ((2 * half + 1) * 8 + tt) * 16, [[ASIZE, 128], [0, H], [1, 16]])
                        x1 = rap(P.arena, pj.offset + c0 + half * 32, [[ASIZE, 128], [64, H], [1, 16]])
                        x2 = rap(P.arena, pj.offset + c0 + half * 32 + 16, [[ASIZE, 128], [64, H], [1, 16]])
                        t = [rt[:, 0:H, :] for rt in rtmp]
                        rk = [A + "rtmp%d" % i for i in range(4)]
                        rp = [(A + "rope", 2 * half), (A + "rope", 2 * half + 1)]
                        mk.op("dve", lambda e, t=t, x1=x1, cs=cs: e.tensor_tensor(out=t[0], in0=x1, in1=cs, op=ALU.mult), reads=[pjk, rp[0]], writes=[rk[0]])
                        mk.op("pool", lambda e, t=t, x2=x2, sn=sn: e.tensor_tensor(out=t[1], in0=x2, in1=sn, op=ALU.mult), reads=[pjk, rp[1]], writes=[rk[1]])
                        mk.op("dve", lambda e, t=t, x2=x2, cs=cs: e.tensor_tensor(out=t[2], in0=x2, in1=cs, op=ALU.mult), reads=[pjk, rp[0]], writes=[rk[2]])
                        mk.op("pool", lambda e, t=t, x1=x1, sn=sn: e.tensor_tensor(out=t[3], in0=x1, in1=sn, op=ALU.mult), reads=[pjk, rp[1]], writes=[rk[3]])
                        mk.op("dve", lambda e, t=t, x1=x1: e.tensor_tensor(out=x1, in0=t[0], in1=t[1], op=ALU.subtract), reads=[rk[0], rk[1]], writes=[pjk])
                        mk.op("dve", lambda e, t=t, x2=x2: e.tensor_tensor(out=x2, in0=t[2], in1=t[3], op=ALU.add), reads=[rk[2], rk[3]], writes=[pjk])
            build_operands(pj, pjk, koff + tt, tt, True)
        pst, psk = P.psum()
        P.transpose(pst[:, 0:128], kmx, ident[:], reads=[A + "kmx", "ident"], writes=[psk])
        mk.op("dve", lambda e, pst=pst: e.tensor_reduce(out=kcol[:, 0:1], in_=pst[:, 0:128], axis=AX.X, op=ALU.max), reads=[psk], writes=[A + "kcol"])
        mk.op("dve", lambda e: e.tensor_scalar(out=ksq, in0=ident[:], scalar1=kcol[:, 0:1], scalar2=None, op0=ALU.mult),
              reads=[A + "kcol", "ident"], writes=[A + "ksq"])
        pst2, ps2k = P.psum()
        P.mm(pst2[:, 0:128], onesD[:], ksq, True, True, reads=["onesD", A + "ksq"], writes=[ps2k])
        mk.op("act", lambda e, pst2=pst2: e.activation(out=kmx, in_=pst2[:, 0:128], func=AF.Sqrt, scale=float(D)), reads=[ps2k], writes=[A + "kmx"])
        mk.op("act", lambda e: e.activation(out=qn, in_=qn, func=AF.Sqrt), reads=[A + "qn"], writes=[A + "qn"])
        mk.op("dve", lambda e: e.memset(ksq, 0.0), reads=[A + "ksq"], writes=[A + "ksq"])
        mk.op("dve", lambda e: e.tensor_reduce(out=ksq[:, 0:16], in_=qn.rearrange("p t h -> p h t"), axis=AX.X, op=ALU.max), reads=[A + "qn", A + "ksq"], writes=[A + "ksq"])
        pst3, ps3k = P.psum()
        P.transpose(pst3[:, 0:128], ksq, ident[:], reads=[A + "ksq", "ident"], writes=[ps3k])
        mk.op("dve", lambda e, pst3=pst3: e.tensor_reduce(out=kcol[:, 1:2], in_=pst3[:, 0:128], axis=AX.X, op=ALU.max), reads=[ps3k], writes=[A + "kcol"])
        mk.op("dve", lambda e: e.tensor_scalar(out=ksq, in0=ident[:], scalar1=kcol[:, 1:2], scalar2=None, op0=ALU.mult),
              reads=[A + "kcol", "ident"], writes=[A + "ksq"])
        pst4, ps4k = P.psum()
        P.mm(pst4[:, 0:128], onesD[:], ksq, True, True, reads=["onesD", A + "ksq"], writes=[ps4k])
        nbh = nbq[:, 0, :]
        for g in range(2):
            mk.op("dve", lambda e, g=g, pst4=pst4: e.tensor_scalar(out=nbh[:, g * 4:(g + 1) * 4], in0=pst4[:, g * 4:(g + 1) * 4], scalar1=kmx[:, g:g + 1],
                                                                 scalar2=-ATTN_SCALE * float(D), op0=ALU.mult, op1=ALU.mult), reads=[ps4k, A + "kmx"], writes=[A + "nbq"])
        for j in range(8):
            mk.op("dve", lambda e, j=j, pst4=pst4: e.tensor_scalar(out=nbh[:, 8 + j:9 + j], in0=pst4[:, 8 + j:9 + j], scalar1=kmx[:, 2 + j:3 + j],
                                                                 scalar2=-ATTN_SCALE * float(D), op0=ALU.mult, op1=ALU.mult), reads=[ps4k, A + "kmx"], writes=[A + "nbq"])
        mk.fence()
        P.atop = mark
        NPT = 7
        PT = [P.aalloc([128, 12, 128], BF16) for _ in range(NPT)]
        msb = [P.aalloc([128, 384]) for _ in range(2)]
        Otok = [P.aalloc([128, 512]) for _ in range(2)]
        Onb = [P.aalloc([128, 512]) for _ in range(2)]
        sqs = P.aalloc([128, 512])
        tmpD = [P.aalloc([128, 128]) for _ in range(2)]
        stat = P.aalloc([128, 64, 16])
        cnt = {"pn": 0, "pt": 0, "st": 0, "ms": 0, "td": 0}
        qsel_i = [0]
        freeb = [0, 1, 2, 3, 4]

        def runs_of(ktiles):
            runs = []
            for kt in ktiles:
                if runs and runs[-1][-1] + 1 == kt and len(runs[-1]) < 4:
                    runs[-1].append(kt)
                else:
                    runs.append([kt])
            return runs

        def row_gen(q_ap, kbase, bias_ap, rk, kn, ktiles, moff, pv_fn, done_ctr=None):
            runs = runs_of(ktiles)
            nkt = len(ktiles)
            sidx = cnt["st"] % 64
            cnt["st"] += 1
            stk = (A + "stat", sidx)
            ti = cnt["pt"] % NPT
            cnt["pt"] += 1
            pt_, ptk = PT[ti], A + "PT%d" % ti
            blk = 0
            for ri, run in enumerate(runs):
                while not freeb:
                    yield
                bk = freeb.pop(0)
                pst, psk = P.ps[bk], "ps%d" % bk
                n = len(run) * 128
                for j, kt in enumerate(run):
                    P.mm(pst[:, j * 128:(j + 1) * 128], kbase[:, kt * 128:(kt + 1) * 128], q_ap, True, True,
                         reads=rk + [(kn, kt)], writes=[psk])
                src, srck = pst[:, 0:n], psk
                if moff is not None and ri == 1:
                    mi = cnt["ms"] % 2
                    cnt["ms"] += 1
                    mb, mbk = msb[mi], A + "msb%d" % mi
                    mk.op("dve", lambda e, mb=mb, n=n, pst=pst: e.tensor_tensor(out=mb[:, 0:n], in0=pst[:, 0:n], in1=wmask[:, moff:moff + n], op=ALU.add),
                          reads=[psk, A + "wmask"], writes=[mbk])
                    src, srck = mb[:, 0:n], mbk
                dstv = pt_[:, blk:blk + len(run), :].rearrange("p c t -> p (c t)")
                mk.op("act", lambda e, src=src, dstv=dstv: e.activation(out=dstv, in_=src, func=AF.Exp, scale=ATTN_SCALE, bias=bias_ap),
                      reads=[srck, A + "nbq"], writes=[ptk])
                freeb.append(bk)
                blk += len(run)
                yield
            pv_fn(pt_, ptk, sidx, stk)
            if done_ctr is not None:
                done_ctr[0] += 1
            yield

        def pipeline(gens, depth):
            active = []
            gens = list(gens)
            while gens or active:
                if gens and len(active) < depth:
                    active.append(gens.pop(0))
                for g_ in list(active):
                    try:
                        next(g_)
                    except StopIteration:
                        active.remove(g_)

        def qtile_rows(qt, ktilesC, moff, ktilesD):
            qs = slice(qt * 128, (qt + 1) * 128)
            oi = qsel_i[0] % 2
            qsel_i[0] += 1
            ot, otk = Otok[oi], A + "Otok%d" % oi
            On, onkey = Onb[oi], A + "On%d" % oi
            done = [0]
            gens = []
            cpo = {}
            nktC = len(ktilesC)
            for hq in range(8):
                pb = (hq % 2) * 64
                g = hq // 4

                def pvC(pt_, ptk, sidx, stk, hq=hq, g=g):
                    bank = hq // 4
                    po, pok = P.ps[5 + bank], "ps%d" % (5 + bank)
                    cs = slice((hq % 4) * 65, (hq % 4) * 65 + 65)
                    for j, kt in enumerate(ktilesC):
                        P.mm(po[:, cs], pt_[:, j, :], vc_b[:, kt, g, :], j == 0, j == nktC - 1, reads=[ptk, (A + "vc_b", kt)], writes=[pok])
                    st = stat[:, sidx, :]
                    mk.op("act", lambda e: e.activation(out=st[:, 6:7], in_=nbq[:, 0, hq:hq + 1], func=AF.Exp, bias=sink_t[:, hq:hq + 1]), reads=[A + "nbq", A + "sink"], writes=[stk])
                    mk.op("dve", lambda e: e.tensor_tensor(out=st[:, 7:8], in0=po[:, cs][:, 64:65], in1=st[:, 6:7], op=ALU.add), reads=[pok, stk], writes=[stk])
                    mk.op("dve", lambda e: e.reciprocal(out=st[:, 7:8], in_=st[:, 7:8]), reads=[stk], writes=[stk])
                    mk.op("dve", lambda e: e.tensor_scalar(out=ot[:, hq * 64:(hq + 1) * 64], in0=po[:, cs][:, 0:64], scalar1=st[:, 7:8], scalar2=None, op0=ALU.mult),
                          reads=[pok, stk], writes=[(otk, hq)])
                gens.append(row_gen(qcT[pb:pb + 64, hq // 2, qs], kcT2[pb:pb + 64, g, :], nbq[:, 0, hq:hq + 1], [(A + "qcT", qt)], A + "kcT2", ktilesC, moff, pvC, done))
            nktD = len(ktilesD)
            dpo = {}
            for h in range(4):
                for c in range(2):
                    pb = c * 64

                    def pvD(pt_, ptk, sidx, stk, h=h, c=c):
                        if c == 0:
                            dpo[h] = ((P.ps[7], "ps7"), sidx, stk)
                        (po, pok), sidx0, stk0 = dpo[h]
                        cs = slice(c * 129, c * 129 + 129)
                        for j, kt in enumerate(ktilesD):
                            P.mm(po[:, cs], pt_[:, j, :], vd_b[:, kt, h, :], j == 0, j == nktD - 1, reads=[ptk, (A + "vd_b", kt)], writes=[pok])
                        if c == 0:
                            return
                        st0 = stat[:, sidx0, :]
                        st1 = stat[:, sidx, :]
                        ti = cnt["td"] % 2
                        cnt["td"] += 1
                        td, tdk = tmpD[ti], A + "tmpD%d" % ti
                        mk.op("dve", lambda e: e.reciprocal(out=st0[:, 8:9], in_=po[:, 128:129]), reads=[pok], writes=[stk0])
                        mk.op("dve", lambda e: e.reciprocal(out=st1[:, 8:9], in_=po[:, 257:258]), reads=[pok], writes=[stk])
                        mk.op("dve", lambda e: e.tensor_tensor(out=st1[:, 9:10], in0=st1[:, 8:9], in1=neg_lam, op=ALU.mult), reads=[stk, A + "lstat"], writes=[stk])
                        mk.op("dve", lambda e: e.tensor_scalar(out=td, in0=po[:, 129:257], scalar1=st1[:, 9:10], scalar2=None, op0=ALU.mult), reads=[pok, stk], writes=[tdk])
                        mk.op("dve", lambda e: e.scalar_tensor_tensor(out=On[:, h * 128:(h + 1) * 128], in0=po[:, 0:128], scalar=st0[:, 8:9], in1=td, op0=ALU.mult, op1=ALU.add),
                              reads=[pok, stk0, tdk], writes=[(onkey, h)])
                    gens.append(row_gen(qdT[pb:pb + 64, h, qs], kdT[pb:pb + 64, h, :], nbq[:, 0, 8 + 2 * h + c:9 + 2 * h + c], [(A + "qdT", qt)], A + "kdT", ktilesD, None, pvD, done))

            def fin_gen():
                while done[0] < 16:
                    yield
                while not freeb:
                    yield
                bk = freeb.pop(0)
                pst, psk = P.ps[bk], "ps%d" % bk
                for cc in range(4):
                    P.transpose(pst[:, cc * 128:(cc + 1) * 128], ot[:, cc * 128:(cc + 1) * 128], ident[:], reads=[otk, "ident"], writes=[psk])
                P.copy(P.evac_eng(), mixT[:, 0:4, qs], pst[:].rearrange("p (c t) -> p c t", t=128), reads=[psk], writes=["mixT"])
                freeb.append(bk)
                yield
                sidx = cnt["st"] % 64
                cnt["st"] += 1
                stk = (A + "stat", sidx)
                st = stat[:, sidx, :]
                onk = onkey
                O3 = On.rearrange("p (h x) -> p h x", x=128)
                mk.op("act", lambda e: e.activation(out=sqs, in_=On, func=AF.Square), reads=[onk], writes=[A + "sqs"])
                mk.op("dve", lambda e: e.tensor_reduce(out=st[:, 0:4], in_=sqs.rearrange("p (h x) -> p h x", x=128), axis=AX.X, op=ALU.add),
                      reads=[A + "sqs"], writes=[stk])
                mk.op("act", lambda e: e.activation(out=st[:, 0:4], in_=st[:, 0:4], func=AF.Sqrt, scale=1.0 / 128, bias=EPS), reads=[stk], writes=[stk])
                mk.op("dve", lambda e: e.reciprocal(out=st[:, 0:4], in_=st[:, 0:4]), reads=[stk], writes=[stk])
                yield
                mk.op("dve", lambda e: e.tensor_tensor(out=O3, in0=O3, in1=rap(P.arena, stat.offset + sidx * 16, [[ASIZE, 128], [1, 4], [0, 128]]), op=ALU.mult),
                      reads=[onk, stk], writes=[onk])
                mk.op("pool", lambda e: e.tensor_tensor(out=O3, in0=O3, in1=rap(P.arena, dsubg.offset, [[ASIZE, 128], [0, 4], [1, 128]]), op=ALU.mult),
                      reads=[onk, A + "dsubg"], writes=[onk])
                yield
                while not freeb:
                    yield
                bk = freeb.pop(0)
                pt2, pt2k = P.ps[bk], "ps%d" % bk
                for r in range(4):
                    P.transpose(pt2[:, r * 128:(r + 1) * 128], On[:, r * 128:(r + 1) * 128], ident[:], reads=[onk, "ident"], writes=[pt2k])
                P.copy(P.evac_eng(), mixT[:, 4:8, qs], pt2[:].rearrange("p (c t) -> p c t", t=128), reads=[pt2k], writes=["mixT"])
                freeb.append(bk)
                yield
            return gens, fin_gen

        allgens = []
        def chain_q(gens, fin):
            return gens, fin
        if v == 0:
            plan = []
            for s in range(4):
                kts = [2 * s, 2 * s + 1]
                for qt in kts:
                    plan.append((qt, kts, None, kts))
        else:
            plan = []
            for qt in range(8):
                lo, hi = max(qt - 1, 0), min(qt + 2, 8)
                plan.append((qt, [0, 1, 2, 3] + [4 + t_ for t_ in range(lo, hi)], 128 if qt == 0 else 0, list(range(12))))
        P.psmod = 5
        allg = []
        for (qt, ktc, moff, ktd) in plan:
            gens, fin = qtile_rows(qt, ktc, moff, ktd)
            allg.extend(gens)
            allg.append(fin())
        pipeline(allg, 6)
        P.psmod = 8
        mk.fence()
        P.atop = 0

    def gdn_part(v, chains):
        G = "gd%d_" % v
        P.atop = 0
        qT = P.aalloc([128, 4, T], BF16)
        kT = P.aalloc([128, 4, T], BF16)
        k_tok = P.aalloc([128, 8, 512], BF16)
        v_tok = P.aalloc([128, 8, 512], BF16)
        sz = P.aalloc([128, 8, 512], BF16)
        o_acc = P.aalloc([128, 8, 512])
        gbuf = P.aalloc([128, 2, 32])
        gc = P.aalloc([128, 2, 32])
        egc = P.aalloc([128, 2, 32])
        beta = P.aalloc([128, 2, 32])
        nbeta = P.aalloc([128, 2, 32])
        bgc = P.aalloc([128, 2, 32])
        glb = P.aalloc([128, 2, 2, 32])
        egl = P.aalloc([128, 2, 2, 32])
        ekd = P.aalloc([128, 2, 32])
        abt = P.aalloc([128, 8, 16])
        prm = P.aalloc([128, 3, 8])
        gng = P.aalloc([128, 128])
        tri = P.aalloc([128, 6, 128])
        sel = P.aalloc([128, 4, 128])
        cwq = P.aalloc([128, 3, 12])
        onesk = P.aalloc([128, 128])
        Sst = [P.aalloc([128, 512]) for _ in range(2)]
        Sbf = [P.aalloc([128, 512], BF16) for _ in range(2)]
        mark = P.atop
        vT = P.aalloc([128, 4, T], BF16)
        cst = P.aalloc([128, T])
        cso = P.aalloc([128, T])
        sqb = P.aalloc([128, T])
        seqlen = 256 if v == 0 else 1024
        mk.dma(tri, rap(c_tri, 0, [[128, 128], [128 * 128, 6], [1, 128]]), writes=[G + "tri"])
        mk.dma(sel, rap(c_sel, 0, [[128, 128], [128 * 128, 4], [1, 128]]), writes=[G + "sel"])
        mk.dma(prm[:, 0, :], rap(gdn_a_log, 0, [[0, 128], [1, 8]]), writes=[G + "prm"])
        mk.dma(prm[:, 1, :], rap(gdn_dt_bias, 0, [[0, 128], [1, 8]]), writes=[G + "prm"])
        mk.dma(gng, rap(gdn_norm_g, 0, [[0, 128], [1, 128]]), writes=[G + "gng"])
        mk.op("pool", lambda e: e.memset(onesk, 1.0), writes=[G + "onesk"])
        for tap in range(3):
            P.load_T(cwq[:, tap, :], gdn_conv_w, tap * 1536, 12, (G + "cwq", tap))
        mk.op("act", lambda e: e.activation(out=prm[:, 2, :], in_=prm[:, 0, :], func=AF.Exp), reads=[G + "prm"], writes=[G + "prm"])
        mk.op("dve", lambda e: e.tensor_scalar(out=prm[:, 2, :], in0=prm[:, 2, :], scalar1=-1.0, scalar2=None, op0=ALU.mult),
              reads=[G + "prm"], writes=[G + "prm"])
        wg = None
        for ch in range(12):
            c0 = 512 + ch * 128
            if ch % 4 == 0:
                wg, wgk = load_w(w_in_ab, 512 + ch * 128, AB_IN, 512)
            for th in range(2):
                ts = slice(th * 512, (th + 1) * 512)
                pst, psk = P.psum()
                for kc in range(8):
                    P.mm(pst[:], wg[:, kc, (ch % 4) * 128:(ch % 4 + 1) * 128], hT[:, kc, ts], kc == 0, kc == 7,
                         reads=[wgk, ("hT", th)], writes=[psk])
                P.copy(P.evac_eng(), cst[:, ts], pst[:], reads=[psk], writes=[(G + "cst", th)])
            w0 = cwq[:, 0, ch:ch + 1]
            w1 = cwq[:, 1, ch:ch + 1]
            w2 = cwq[:, 2, ch:ch + 1]
            mk.op("dve", lambda e, w1=w1: e.tensor_scalar(out=cso, in0=cst, scalar1=w1, scalar2=None, op0=ALU.mult),
                  reads=[G + "cst", (G + "cwq", 1)], writes=[G + "cso"])
            c3 = cso.rearrange("p (s t) -> p s t", t=seqlen)
            a3 = cst.rearrange("p (s t) -> p s t", t=seqlen)
            mk.op("dve", lambda e, c3=c3, a3=a3, w0=w0: e.scalar_tensor_tensor(
                out=c3[:, :, 1:seqlen], in0=a3[:, :, 0:seqlen - 1], scalar=w0, in1=c3[:, :, 1:seqlen],
                op0=ALU.mult, op1=ALU.add), reads=[G + "cst", G + "cso", (G + "cwq", 0)], writes=[G + "cso"])
            mk.op("dve", lambda e, c3=c3, a3=a3, w2=w2: e.scalar_tensor_tensor(
                out=c3[:, :, 0:seqlen - 1], in0=a3[:, :, 1:seqlen], scalar=w2, in1=c3[:, :, 0:seqlen - 1],
                op0=ALU.mult, op1=ALU.add), reads=[G + "cst", G + "cso", (G + "cwq", 2)], writes=[G + "cso"])
            if ch >= 8:
                mk.op("act", lambda e, ch=ch: e.activation(out=vT[:, ch - 8, :], in_=cso, func=AF.Silu),
                      reads=[G + "cso"], writes=[(G + "vT", ch - 8)])
                continue
            mk.op("act", lambda e: e.activation(out=cso, in_=cso, func=AF.Silu), reads=[G + "cso"], writes=[G + "cso"])
            mk.op("act", lambda e: e.activation(out=sqb, in_=cso, func=AF.Square), reads=[G + "cso"], writes=[G + "sqb"])
            for th in range(2):
                ts = slice(th * 512, (th + 1) * 512)
                pst, psk = P.psum()
                P.mm(pst[:], onesk, sqb[:, ts], True, True, reads=[G + "onesk", G + "sqb"], writes=[psk])
                mk.op("act", lambda e, pst=pst, ts=ts: e.activation(out=cst[:, ts], in_=pst[:], func=AF.Ln, bias=epsc[:, 0:1]),
                      reads=[psk, "epsc"], writes=[(G + "cst", th)])
            mk.op("act", lambda e: e.activation(out=cst, in_=cst, func=AF.Exp, scale=-0.5), reads=[G + "cst"], writes=[G + "cst"])
            dst = qT[:, ch, :] if ch < 4 else kT[:, ch - 4, :]
            dk_ = (G + "qT", ch) if ch < 4 else (G + "kT", ch - 4)
            if ch < 4:
                mk.op("dve", lambda e, dst=dst: e.scalar_tensor_tensor(out=dst, in0=cso, scalar=128 ** -0.5, in1=cst, op0=ALU.mult, op1=ALU.mult),
                      reads=[G + "cso", G + "cst"], writes=[dk_])
            else:
                mk.op("dve", lambda e, dst=dst: e.tensor_tensor(out=dst, in0=cso, in1=cst, op=ALU.mult),
                      reads=[G + "cso", G + "cst"], writes=[dk_])
        for tt in range(8):
            for (src, dst, nm) in ((kT, k_tok, "k_tok"), (vT, v_tok, "v_tok")):
                pst, psk = P.psum()
                psb = pst[:].bitcast(BF16)
                for h in range(4):
                    P.transpose(psb[:, h * 128:(h + 1) * 128], src[:, h, tt * 128:(tt + 1) * 128], identb[:],
                                reads=[(G + ("kT" if src is kT else "vT"), h), "identb"], writes=[psk])
                P.copy(P.evac_eng(), dst[:, tt, :], psb[:, 0:512], reads=[psk], writes=[(G + nm, tt)])
        wz, wzk = load_w(w_in_ab, 2048, AB_IN, 512)
        wab, wabk = load_w(w_in_ab, 2560, AB_IN, 16)
        for tt in range(8):
            pst, psk = P.psum()
            for kc in range(8):
                P.mm(pst[:], hT[:, kc, tt * 128:(tt + 1) * 128], wz[:, kc, 0:512], kc == 0, kc == 7,
                     reads=[wzk, ("hT", tt // 4)], writes=[psk])
            mk.op("act", lambda e, pst=pst, tt=tt: e.activation(out=sz[:, tt, :], in_=pst[:], func=AF.Silu),
                  reads=[psk], writes=[(G + "sz", tt)])
            pst, psk = P.psum()
            for kc in range(8):
                P.mm(pst[:, 0:16], hT[:, kc, tt * 128:(tt + 1) * 128], wab[:, kc, 0:16], kc == 0, kc == 7,
                     reads=[wabk, ("hT", tt // 4)], writes=[psk])
            P.copy("dve", abt[:, tt, :], pst[:, 0:16], reads=[psk], writes=[G + "abt"])
        for d in range(2):
            a_v = rap(P.arena, abt.offset + d * 4, [[ASIZE, 128], [16, 8], [1, 4]])
            b_v = rap(P.arena, abt.offset + 8 + d * 4, [[ASIZE, 128], [16, 8], [1, 4]])
            dtb = rap(P.arena, prm.offset + 8 + d * 4, [[ASIZE, 128], [0, 8], [1, 4]])
            nA = rap(P.arena, prm.offset + 16 + d * 4, [[ASIZE, 128], [0, 8], [1, 4]])
            g3 = gbuf[:, d, :].rearrange("p (t h) -> p t h", h=4)
            b3 = beta[:, d, :].rearrange("p (t h) -> p t h", h=4)
            mk.op("dve", lambda e, g3=g3, a_v=a_v, dtb=dtb: e.tensor_tensor(out=g3, in0=a_v, in1=dtb, op=ALU.add),
                  reads=[G + "abt", G + "prm"], writes=[G + "gbuf"])
            mk.op("act", lambda e, g3=g3: e.activation(out=g3, in_=g3, func=AF.Exp), reads=[G + "gbuf"], writes=[G + "gbuf"])
            mk.op("act", lambda e, g3=g3: e.activation(out=g3, in_=g3, func=AF.Ln, bias=1.0), reads=[G + "gbuf"], writes=[G + "gbuf"])
            mk.op("dve", lambda e, g3=g3, nA=nA: e.tensor_tensor(out=g3, in0=g3, in1=nA, op=ALU.mult),
                  reads=[G + "gbuf", G + "prm"], writes=[G + "gbuf"])
            mk.op("act", lambda e, b3=b3, b_v=b_v: e.activation(out=b3, in_=b_v, func=AF.Sigmoid), reads=[G + "abt"], writes=[G + "beta"])
        mk.op("dve", lambda e: e.tensor_scalar(out=nbeta, in0=beta, scalar1=-1.0, scalar2=None, op0=ALU.mult),
              reads=[G + "beta"], writes=[G + "nbeta"])
        for d in range(2):
            pst, psk = P.psum()
            P.mm(pst[:, 0:32], tri[:, d, :], gbuf[:, d, :], True, True, reads=[G + "tri", G + "gbuf"], writes=[psk])
            P.copy("dve", gc[:, d, :], pst[:, 0:32], reads=[psk], writes=[G + "gc"])
        for d in range(2):
            for c in range(2):
                pst, psk = P.psum()
                P.mm(pst[:, 0:32], sel[:, 2 * d + c, :], gc[:, d, :], True, True, reads=[G + "sel", G + "gc"], writes=[psk])
                P.copy("dve", glb[:, d, c, :], pst[:, 0:32], reads=[psk], writes=[G + "glb"])
        mk.op("act", lambda e: e.activation(out=egc, in_=gc, func=AF.Exp), reads=[G + "gc"], writes=[G + "egc"])
        mk.op("act", lambda e: e.activation(out=egl, in_=glb, func=AF.Exp), reads=[G + "glb"], writes=[G + "egl"])
        mk.op("dve", lambda e: e.tensor_tensor(out=bgc, in0=beta, in1=egc, op=ALU.mult), reads=[G + "beta", G + "egc"], writes=[G + "bgc"])
        for c in range(2):
            rows = slice(c * 64, (c + 1) * 64)
            mk.op("dve", lambda e, c=c, rows=rows: e.tensor_tensor(out=ekd[rows, :, :], in0=glb[rows, :, c, :], in1=gc[rows, :, :], op=ALU.subtract),
                  reads=[G + "glb", G + "gc"], writes=[G + "ekd"])
        mk.op("act", lambda e: e.activation(out=ekd, in_=ekd, func=AF.Exp), reads=[G + "ekd"], writes=[G + "ekd"])
        mk.op("pool", lambda e: e.memset(o_acc, 0.0), writes=[G + "o_acc"])
        mk.fence()
        P.atop = mark
        def mkset():
            b = {}
            b["Rg"] = P.aalloc([128, 512]); b["ub"] = b["Rg"]
            b["Dm"] = P.aalloc([128, 512]); b["otmp"] = b["Dm"]
            b["DmT"] = P.aalloc([128, 512])
            dmt_b = b["DmT"].bitcast(BF16)
            b["wT"] = dmt_b[:, 0:512]
            b["vn"] = dmt_b[:, 512:1024]
            b["X"] = P.aalloc([128, 512]); b["XT"] = P.aalloc([128, 512]); b["R"] = P.aalloc([128, 512])
            for nm in ("TTb", "ATm", "vb", "kbg", "kd"):
                b[nm] = P.aalloc([128, 512], BF16)
            return b
        sets = [mkset(), mkset()]

        def bc_h(buf, d, tt, inner):
            return rap(P.arena, buf.offset + d * 32 + tt * 4, [[ASIZE, 128], [1, 4], [0, inner]])

        def v4(t_):
            return t_.rearrange("p (h x) -> p h x", x=128)

        def hsl(h):
            return slice(h * 128, (h + 1) * 128)

        def run_chain(ci, d, tiles, sq_):
            B = sets[d]
            K = G + "w%d_" % d
            Rg, Dm, DmT, X, XT, R_ = B["Rg"], B["Dm"], B["DmT"], B["X"], B["XT"], B["R"]
            TTb, ATm, vb, kbg, kd, ub, wT, vn, otmp = B["TTb"], B["ATm"], B["vb"], B["kbg"], B["kd"], B["ub"], B["wT"], B["vn"], B["otmp"]
            kRg, kDm, kDmT, kX, kXT, kR = K + "Rg", K + "Dm", K + "DmT", K + "X", K + "XT", K + "R"
            kub, kotmp, kwT, kvn = kRg, kDm, kDmT, kDmT
            S, Sk = Sst[d], G + "S%d" % d
            Sb, Sbk = Sbf[d], G + "Sb%d" % d
            if v == 0:
                mk.op("pool", lambda e, S=S: e.memset(S, 0.0), writes=[Sk])
            else:
                mk.dma(S.rearrange("p (h x) -> p h x", x=128), rap(gstate, d * 4 * 128 * 128, [[128, 128], [128 * 128, 4], [1, 128]]), writes=[Sk])
            mk.op("act", lambda e, S=S, Sb=Sb: e.activation(out=Sb, in_=S, func=AF.Copy), reads=[Sk], writes=[Sbk])
            triD = tri[:, d, :]
            ntri = tri[:, 2 + d, :]
            mS = tri[:, 4 + d, :]
            ntri_b = rap(P.arena, ntri.offset, [[ASIZE, 128], [0, 4], [1, 128]])
            mS_b = rap(P.arena, mS.offset, [[ASIZE, 128], [0, 4], [1, 128]])
            tri_b = rap(P.arena, triD.offset, [[ASIZE, 128], [0, 4], [1, 128]])
            id_b = rap(ident, 0, [[128, 128], [0, 4], [1, 128]])
            yield
            for tt in tiles:
                tsl = slice(tt * 128, (tt + 1) * 128)
                g_b = bc_h(gbuf, d, tt, 128)
                mk.op("dve", lambda e, g_b=g_b: e.tensor_tensor(out=v4(Rg), in0=ntri_b, in1=g_b, op=ALU.mult),
                      reads=[G + "tri", G + "gbuf"], writes=[kRg])
                pL, pLk = P.psum()
                P.mm(pL[:], triD, Rg, True, True, reads=[G + "tri", kRg], writes=[pLk])
                mk.op("act", lambda e, pL=pL: e.activation(out=Dm, in_=pL[:], func=AF.Exp), reads=[pLk], writes=[kDm])
                pLT, pLTk = P.psum()
                for h in range(4):
                    P.mm(pLT[:, hsl(h)], Rg[:, hsl(h)], triD, True, True, reads=[G + "tri", kRg], writes=[pLTk])
                mk.op("act", lambda e, pLT=pLT: e.activation(out=DmT, in_=pLT[:], func=AF.Exp), reads=[pLTk], writes=[kDmT])
                yield
                nb_b = bc_h(nbeta, d, tt, 128)
                mk.op("pool", lambda e: e.tensor_tensor(out=v4(Dm), in0=v4(Dm), in1=mS_b, op=ALU.mult), reads=[kDm, G + "tri"], writes=[kDm])
                mk.op("pool", lambda e, nb_b=nb_b: e.tensor_tensor(out=v4(Dm), in0=v4(Dm), in1=nb_b, op=ALU.mult), reads=[kDm, G + "nbeta"], writes=[kDm])
                mk.op("pool", lambda e: e.tensor_tensor(out=v4(DmT), in0=v4(DmT), in1=tri_b, op=ALU.mult), reads=[kDmT, G + "tri"], writes=[kDmT])
                pG, pGk = P.psum()
                pA, pAk = P.psum()
                for h in range(4):
                    P.mm(pG[:, hsl(h)], kT[:, h, tsl], kT[:, h, tsl], True, True, reads=[(G + "kT", h)], writes=[pGk])
                for h in range(4):
                    P.mm(pA[:, hsl(h)], kT[:, h, tsl], qT[:, h, tsl], True, True, reads=[(G + "kT", h), (G + "qT", h)], writes=[pAk])
                mk.op("dve", lambda e, pG=pG: e.tensor_tensor(out=X, in0=pG[:], in1=Dm, op=ALU.mult), reads=[pGk, kDm], writes=[kX])
                mk.op("dve", lambda e, pA=pA: e.tensor_tensor(out=ATm, in0=pA[:], in1=DmT, op=ALU.mult), reads=[pAk, kDmT], writes=[K + "ATm"])
                yield
                pX, pXk = P.psum()
                for h in range(4):
                    P.transpose(pX[:, hsl(h)], X[:, hsl(h)], ident[:], reads=[kX, "ident"], writes=[pXk])
                P.copy("act", XT, pX[:, 0:512], reads=[pXk], writes=[kXT])
                mk.op("dve", lambda e: e.tensor_tensor(out=v4(R_), in0=v4(XT), in1=id_b, op=ALU.add), reads=[kXT, "ident"], writes=[kR])
                yield
                for it in range(1, 6):
                    p1, p1k = P.psum()
                    for h in range(4):
                        P.mm(p1[:, hsl(h)], XT[:, hsl(h)], X[:, hsl(h)], True, True, reads=[kXT, kX], writes=[p1k])
                    if it < 5:
                        p2, p2k = P.psum()
                        for h in range(4):
                            P.mm(p2[:, hsl(h)], X[:, hsl(h)], XT[:, hsl(h)], True, True, reads=[kXT, kX], writes=[p2k])
                    P.copy("act", X, p1[:], reads=[p1k], writes=[kX])
                    if it < 5:
                        P.copy("dve", XT, p2[:], reads=[p2k], writes=[kXT])
                    p3, p3k = P.psum()
                    for h in range(4):
                        P.mm(p3[:, hsl(h)], X[:, hsl(h)], R_[:, hsl(h)], True, True, reads=[kX, kR], writes=[p3k])
                    mk.op("dve", lambda e, p3=p3: e.tensor_tensor(out=R_, in0=p3[:], in1=R_, op=ALU.add), reads=[p3k, kR], writes=[kR])
                    yield
                mk.op("act", lambda e: e.activation(out=TTb, in_=R_, func=AF.Copy), reads=[kR], writes=[K + "TTb"])
                be_b, bg_b, ek_b = bc_h(beta, d, tt, 128), bc_h(bgc, d, tt, 128), bc_h(ekd, d, tt, 128)
                mk.op("pool", lambda e, tt=tt, be_b=be_b: e.tensor_tensor(out=v4(vb), in0=v4(v_tok[:, tt, :]), in1=be_b, op=ALU.mult),
                      reads=[(G + "v_tok", tt), G + "beta"], writes=[K + "vb"])
                mk.op("pool", lambda e, tt=tt, bg_b=bg_b: e.tensor_tensor(out=v4(kbg), in0=v4(k_tok[:, tt, :]), in1=bg_b, op=ALU.mult),
                      reads=[(G + "k_tok", tt), G + "bgc"], writes=[K + "kbg"])
                mk.op("pool", lambda e, tt=tt, ek_b=ek_b: e.tensor_tensor(out=v4(kd), in0=v4(k_tok[:, tt, :]), in1=ek_b, op=ALU.mult),
                      reads=[(G + "k_tok", tt), G + "ekd"], writes=[K + "kd"])
                pu, puk = P.psum()
                pw, pwk = P.psum()
                for h in range(4):
                    P.mm(pu[:, hsl(h)], TTb[:, hsl(h)], vb[:, hsl(h)], True, True, reads=[K + "TTb", K + "vb"], writes=[puk])
                for h in range(4):
                    P.mm(pw[:, hsl(h)], kbg[:, hsl(h)], TTb[:, hsl(h)], True, True, reads=[K + "TTb", K + "kbg"], writes=[pwk])
                P.copy("act", ub, pu[:], reads=[puk], writes=[kub])
                P.copy("dve", wT, pw[:], reads=[pwk], writes=[kwT])
                yield
                for c in ((0, 1) if d == 0 else (1, 0)):
                    rows = slice(c * 64, (c + 1) * 64)
                    ccols = slice(tt * 128 + c * 64, tt * 128 + (c + 1) * 64)
                    p1, p1k = P.psum()
                    for h in range(4):
                        P.mm(p1[rows, hsl(h)], wT[:, h * 128 + c * 64:h * 128 + (c + 1) * 64], Sb[:, hsl(h)], True, True,
                             reads=[kwT, Sbk], writes=[p1k])
                    mk.op("dve", lambda e, p1=p1, rows=rows: e.tensor_tensor(out=vn[rows, :], in0=ub[rows, :], in1=p1[rows, :], op=ALU.subtract),
                          reads=[p1k, kub], writes=[kvn])
                    p2, p2k = P.psum()
                    for h in range(4):
                        P.mm(p2[rows, hsl(h)], qT[:, h, ccols], Sb[:, hsl(h)], True, True, reads=[(G + "qT", h), Sbk], writes=[p2k])
                    yield
                    p3, p3k = P.psum()
                    for h in range(4):
                        P.mm(p3[rows, hsl(h)], ATm[rows, h * 128 + c * 64:h * 128 + (c + 1) * 64], vn[rows, hsl(h)], True, True,
                             reads=[K + "ATm", kvn], writes=[p3k])
                    p4, p4k = P.psum()
                    for h in range(4):
                        P.mm(p4[:, hsl(h)], kd[rows, hsl(h)], vn[rows, hsl(h)], True, True, reads=[K + "kd", kvn], writes=[p4k])
                    mk.op("dve", lambda e, p3=p3, rows=rows, tt=tt: e.tensor_tensor(out=o_acc[rows, tt, :], in0=p3[rows, :], in1=o_acc[rows, tt, :], op=ALU.add),
                          reads=[p3k, (G + "o_acc", tt)], writes=[(G + "o_acc", tt)])
                    egb = rap(P.arena, egc.offset + c * 64 * ASIZE + d * 32 + tt * 4, [[ASIZE, 64], [1, 4], [0, 128]])
                    mk.op("dve", lambda e, p2=p2, rows=rows, egb=egb: e.tensor_tensor(out=v4(otmp[rows, :]), in0=v4(p2[rows, :]), in1=egb, op=ALU.mult),
                          reads=[p2k, G + "egc"], writes=[kotmp])
                    mk.op("pool", lambda e, rows=rows, tt=tt: e.tensor_tensor(out=o_acc[rows, tt, :], in0=otmp[rows, :], in1=o_acc[rows, tt, :], op=ALU.add),
                          reads=[kotmp, (G + "o_acc", tt)], writes=[(G + "o_acc", tt)])
                    for h in range(4):
                        col = egl[:, d, c, tt * 4 + h:tt * 4 + h + 1]
                        mk.op("dve", lambda e, S=S, p4=p4, h=h, col=col: e.scalar_tensor_tensor(
                            out=S[:, hsl(h)], in0=S[:, hsl(h)], scalar=col, in1=p4[:, hsl(h)], op0=ALU.mult, op1=ALU.add),
                            reads=[p4k, Sk, G + "egl"], writes=[Sk])
                    mk.op("act", lambda e, S=S, Sb=Sb: e.activation(out=Sb, in_=S, func=AF.Copy), reads=[Sk], writes=[Sbk])
                    yield
            if v == 0:
                ok = ("o_gdn", sq_, d)
                mk.dma(rap(o_gdn, (sq_ * 2 + d) * 4 * 128 * 128, [[128, 128], [128 * 128, 4], [1, 128]]),
                       S.rearrange("p (h x) -> p h x", x=128), reads=[Sk], writes=[ok])
                P.outs.append(ok)

        for c0 in range(0, len(chains), 2):
            gens = [run_chain(c0 + i, *chains[c0 + i]) for i in range(2)]
            while gens:
                for g_ in list(gens):
                    try:
                        next(g_)
                    except StopIteration:
                        gens.remove(g_)
        Rg, otmp, vb = sets[0]["Rg"], sets[0]["otmp"], sets[0]["vb"]
        for tt in range(8):
            o3 = o_acc[:, tt, :].rearrange("p (h x) -> p h x", x=128)
            mk.op("act", lambda e, tt=tt: e.activation(out=otmp, in_=o_acc[:, tt, :], func=AF.Square), reads=[(G + "o_acc", tt)], writes=[G + "w0_Dm"])
            mk.op("dve", lambda e: e.tensor_reduce(out=Rg[:, 0:4], in_=v4(otmp), axis=AX.X, op=ALU.add), reads=[G + "w0_Dm"], writes=[G + "w0_Rg"])
            mk.op("act", lambda e: e.activation(out=Rg[:, 0:4], in_=Rg[:, 0:4], func=AF.Sqrt, scale=1.0 / 128, bias=EPS), reads=[G + "w0_Rg"], writes=[G + "w0_Rg"])
            mk.op("dve", lambda e: e.reciprocal(out=Rg[:, 0:4], in_=Rg[:, 0:4]), reads=[G + "w0_Rg"], writes=[G + "w0_Rg"])
            mk.op("dve", lambda e, o3=o3: e.tensor_tensor(out=v4(otmp), in0=o3, in1=rap(P.arena, Rg.offset, [[ASIZE, 128], [1, 4], [0, 128]]), op=ALU.mult),
                  reads=[(G + "o_acc", tt), G + "w0_Rg", G + "w0_Dm"], writes=[G + "w0_Dm"])
            mk.op("pool", lambda e: e.tensor_tensor(out=v4(otmp), in0=v4(otmp), in1=rap(P.arena, gng.offset, [[ASIZE, 128], [0, 4], [1, 128]]), op=ALU.mult),
                  reads=[G + "w0_Dm", G + "gng"], writes=[G + "w0_Dm"])
            mk.op("dve", lambda e, tt=tt: e.tensor_tensor(out=vb, in0=otmp, in1=sz[:, tt, :], op=ALU.mult),
                  reads=[G + "w0_Dm", (G + "sz", tt)], writes=[G + "w0_vb"])
            pst, psk = P.psum()
            psb = pst[:].bitcast(BF16)
            for h in range(4):
                P.transpose(psb[:, h * 128:(h + 1) * 128], vb[:, h * 128:(h + 1) * 128], identb[:], reads=[G + "w0_vb", "identb"], writes=[psk])
            P.copy(P.evac_eng(), mixT[:, 4:8, tt * 128:(tt + 1) * 128], psb[:, 0:512].rearrange("p (c t) -> p c t", t=128),
                   reads=[psk], writes=["mixT"])
        mk.fence()
        P.atop = 0

    MAGIC = 12582912.0
    TWO_PI = 2.0 * math.pi

    def s5_part(v, mode="run", base=0):
        Z = ("s5%d_" % v) if mode == "run" else "s5p_"
        P.atop = base
        NS = 4
        u_tok = P.aalloc([128, 32, 8, 16]) if mode == "run" else None
        dB = P.aalloc([128, 512])
        sc = {}
        for nm in ("lre", "lim", "dtv", "lr", "li", "ar", "ai", "fr", "fi", "den", "t0", "t1", "t2", "rho8", "li8",
                   "rfr", "rfi", "h0r", "h0i", "fsr", "fsi"):
            sc[nm] = P.aalloc([128, 32])
        kk = P.aalloc([128, 8])
        arow = P.aalloc([128, 33])
        hl = P.aalloc([128, 2, 4, 32])
        natl = P.aalloc([128, 64])
        mfb = P.aalloc([128, 2, 128])
        mark0 = P.atop

        def sincos(x, n, cos_out, sin_out, tmp, rk, wk, turns=False):
            for (dst, shift) in ((sin_out, 0.0), (cos_out, 0.25)):
                mk.op("dve", lambda e, dst=dst, shift=shift: e.tensor_scalar(out=dst, in0=x, scalar1=(1.0 if turns else 1.0 / TWO_PI), scalar2=shift, op0=ALU.mult, op1=ALU.add),
                      reads=rk, writes=wk)
                mk.op("dve", lambda e, dst=dst: e.tensor_scalar(out=tmp, in0=dst, scalar1=MAGIC, scalar2=MAGIC, op0=ALU.add, op1=ALU.subtract),
                      reads=wk, writes=wk)
                mk.op("dve", lambda e, dst=dst: e.tensor_tensor(out=dst, in0=dst, in1=tmp, op=ALU.subtract), reads=wk, writes=wk)
                mk.op("act", lambda e, dst=dst: e.activation(out=dst, in_=dst, func=AF.Sin, scale=TWO_PI), reads=wk, writes=wk)

        SK = [Z + "sc"]
        mk.dma(kk, c_kk.ap(), writes=SK)
        mk.dma(arow, c_arow.ap(), writes=SK)
        mk.dma(mfb, rap(c_mfb, 0, [[128, 128], [128 * 128, 2], [1, 128]]), writes=[Z + "mfb"])
        mk.dma(dB, rap(s5_d, 0, [[0, 128], [1, 512]]), writes=[Z + "dB"])
        for (src, dst) in ((s5_lam_re, "lre"), (s5_lam_im, "lim")) + (((s5h_re, "h0r"), (s5h_im, "h0i")) if v == 1 else ()):
            mk.dma(natl[0:64, :], rap(src, 0, [[64, 64], [1, 64]]), writes=[Z + "natl"])
            pst, psk = P.psum()
            for d in range(2):
                P.mm(pst[d * 64:(d + 1) * 64, 0:32], natl[d * 32:(d + 1) * 32, :], ident[d * 32:(d + 1) * 32, d * 32:(d + 1) * 32],
                     True, True, reads=[Z + "natl", "ident"], writes=[psk])
            P.copy("dve", sc[dst], pst[:, 0:32], reads=[psk], writes=SK)
        for d in range(2):
            mk.dma(sc["dtv"][d * 64:(d + 1) * 64, :], rap(s5_log_dt, d * 32, [[0, 64], [1, 32]]), writes=SK)
        S = sc
        mk.op("act", lambda e: e.activation(out=S["dtv"], in_=S["dtv"], func=AF.Exp), reads=SK, writes=SK)
        mk.op("dve", lambda e: e.tensor_tensor(out=S["lr"], in0=S["lre"], in1=S["dtv"], op=ALU.mult), reads=SK, writes=SK)
        mk.op("dve", lambda e: e.tensor_tensor(out=S["li"], in0=S["lim"], in1=S["dtv"], op=ALU.mult), reads=SK, writes=SK)
        sincos(S["li"], 32, S["ar"], S["ai"], S["t0"], SK, SK)
        mk.op("act", lambda e: e.activation(out=S["t1"], in_=S["lr"], func=AF.Exp), reads=SK, writes=SK)
        mk.op("dve", lambda e: e.tensor_tensor(out=S["ar"], in0=S["ar"], in1=S["t1"], op=ALU.mult), reads=SK, writes=SK)
        mk.op("dve", lambda e: e.tensor_tensor(out=S["ai"], in0=S["ai"], in1=S["t1"], op=ALU.mult), reads=SK, writes=SK)
        mk.op("dve", lambda e: e.tensor_tensor(out=S["den"], in0=S["lre"], in1=S["lre"], op=ALU.mult), reads=SK, writes=SK)
        mk.op("dve", lambda e: e.tensor_tensor(out=S["t0"], in0=S["lim"], in1=S["lim"], op=ALU.mult), reads=SK, writes=SK)
        mk.op("dve", lambda e: e.tensor_tensor(out=S["den"], in0=S["den"], in1=S["t0"], op=ALU.add), reads=SK, writes=SK)
        mk.op("dve", lambda e: e.reciprocal(out=S["den"], in_=S["den"]), reads=SK, writes=SK)
        mk.op("dve", lambda e: e.tensor_scalar(out=S["t2"], in0=S["ar"], scalar1=-1.0, scalar2=None, op0=ALU.add), reads=SK, writes=SK)
        mk.op("dve", lambda e: e.tensor_tensor(out=S["fr"], in0=S["t2"], in1=S["lre"], op=ALU.mult), reads=SK, writes=SK)
        mk.op("dve", lambda e: e.tensor_tensor(out=S["t0"], in0=S["ai"], in1=S["lim"], op=ALU.mult), reads=SK, writes=SK)
        mk.op("dve", lambda e: e.tensor_tensor(out=S["fr"], in0=S["fr"], in1=S["t0"], op=ALU.add), reads=SK, writes=SK)
        mk.op("dve", lambda e: e.tensor_tensor(out=S["fr"], in0=S["fr"], in1=S["den"], op=ALU.mult), reads=SK, writes=SK)
        mk.op("dve", lambda e: e.tensor_tensor(out=S["fi"], in0=S["ai"], in1=S["lre"], op=ALU.mult), reads=SK, writes=SK)
        mk.op("dve", lambda e: e.tensor_tensor(out=S["t0"], in0=S["t2"], in1=S["lim"], op=ALU.mult), reads=SK, writes=SK)
        mk.op("dve", lambda e: e.tensor_tensor(out=S["fi"], in0=S["fi"], in1=S["t0"], op=ALU.subtract), reads=SK, writes=SK)
        mk.op("dve", lambda e: e.tensor_tensor(out=S["fi"], in0=S["fi"], in1=S["den"], op=ALU.mult), reads=SK, writes=SK)
        mk.op("act", lambda e: e.activation(out=S["rho8"], in_=S["lr"], func=AF.Exp, scale=8.0), reads=SK, writes=SK)
        mk.op("dve", lambda e: e.tensor_scalar(out=S["li8"], in0=S["li"], scalar1=8.0 / TWO_PI, scalar2=None, op0=ALU.mult), reads=SK, writes=SK)
        mk.op("dve", lambda e: e.tensor_scalar(out=S["t0"], in0=S["li8"], scalar1=MAGIC, scalar2=MAGIC, op0=ALU.add, op1=ALU.subtract), reads=SK, writes=SK)
        mk.op("dve", lambda e: e.tensor_tensor(out=S["li8"], in0=S["li8"], in1=S["t0"], op=ALU.subtract), reads=SK, writes=SK)
        mk.op("dve", lambda e: e.tensor_scalar(out=S["t2"], in0=S["li"], scalar1=1.0 / TWO_PI, scalar2=None, op0=ALU.mult), reads=SK, writes=SK)
        mk.op("dve", lambda e: e.tensor_scalar(out=S["t0"], in0=S["t2"], scalar1=MAGIC, scalar2=MAGIC, op0=ALU.add, op1=ALU.subtract), reads=SK, writes=SK)
        mk.op("dve", lambda e: e.tensor_tensor(out=S["t2"], in0=S["t2"], in1=S["t0"], op=ALU.subtract), reads=SK, writes=SK)
        mk.op("dve", lambda e: e.tensor_scalar(out=S["t2"], in0=S["t2"], scalar1=255.0, scalar2=None, op0=ALU.mult), reads=SK, writes=SK)
        sincos(S["t2"], 32, S["rfr"], S["rfi"], S["t0"], SK, SK, turns=True)
        mk.op("act", lambda e: e.activation(out=S["t1"], in_=S["lr"], func=AF.Exp, scale=7.0), reads=SK, writes=SK)
        mk.op("dve", lambda e: e.tensor_tensor(out=S["rfr"], in0=S["rfr"], in1=S["t1"], op=ALU.mult), reads=SK, writes=SK)
        mk.op("dve", lambda e: e.tensor_tensor(out=S["rfi"], in0=S["rfi"], in1=S["t1"], op=ALU.mult), reads=SK, writes=SK)
        if v == 1:
            mk.op("dve", lambda e: e.tensor_tensor(out=S["fsr"], in0=S["ar"], in1=S["h0r"], op=ALU.mult), reads=SK, writes=SK)
            mk.op("dve", lambda e: e.tensor_tensor(out=S["t0"], in0=S["ai"], in1=S["h0i"], op=ALU.mult), reads=SK, writes=SK)
            mk.op("dve", lambda e: e.tensor_tensor(out=S["fsr"], in0=S["fsr"], in1=S["t0"], op=ALU.subtract), reads=SK, writes=SK)
            mk.op("dve", lambda e: e.tensor_tensor(out=S["fsi"], in0=S["ar"], in1=S["h0i"], op=ALU.mult), reads=SK, writes=SK)
            mk.op("dve", lambda e: e.tensor_tensor(out=S["t0"], in0=S["ai"], in1=S["h0r"], op=ALU.mult), reads=SK, writes=SK)
            mk.op("dve", lambda e: e.tensor_tensor(out=S["fsi"], in0=S["fsi"], in1=S["t0"], op=ALU.add), reads=SK, writes=SK)
        yield
        if mode == "run":
            wu, wuk = load_w(w_in_ab, 0, AB_IN, 512)
            for b in range(8):
                pst, psk = P.psum()
                for kc in range(8):
                    P.mm(pst[:], hT[:, kc, b:T:8], wu[:, kc, 0:512], kc == 0, kc == 7, reads=[wuk, "hT"], writes=[psk])
                P.copy(P.evac_eng(), u_tok[:, :, b, :], pst[:].rearrange("p (g c) -> p g c", c=16), reads=[psk], writes=[Z + "u_tok"])

        def off(view):
            return view.offset

        for gh in range(2):
            g0 = gh * 16
            P.atop = mark0
            H = (Z + "h%d_" % gh) if mode == "run" else (Z + "h_")
            markp = P.atop
            PTre = P.aalloc([128, 16, 128], BF16)
            PTim = P.aalloc([128, 16, 128], BF16)
            Qre_b = P.aalloc([128, 16, 128], BF16)
            Qsim_b = P.aalloc([128, 16, 128], BF16)
            WT = P.aalloc([128, 16, 128], BF16)
            Rr = P.aalloc([128, 16, 33])
            Ri = P.aalloc([128, 16, 33])
            R2r = P.aalloc([128, 16, 33])
            R2i = P.aalloc([128, 16, 33])
            D0 = P.aalloc([128, 16, 32])
            mark1 = P.atop
            assert mark1 - markp == 7744, (mark1, markp)
            def bc_g(nm, inner):
                return rap(P.arena, off(sc[nm]) + g0, [[ASIZE, 128], [1, 16], [0, inner]])
            PE2 = "pool" if mode == "run" else "dve"
            def cmul(out_r, out_i, ar_, ai_, br_, bi_, tmp, keys, neg_im=False):
                mk.op("dve", lambda e: e.tensor_tensor(out=out_r, in0=ar_, in1=br_, op=ALU.mult), reads=keys, writes=keys)
                mk.op(PE2, lambda e: e.tensor_tensor(out=tmp, in0=ai_, in1=bi_, op=ALU.mult), reads=keys, writes=keys)
                mk.op("dve", lambda e: e.tensor_tensor(out=out_r, in0=out_r, in1=tmp, op=ALU.subtract), reads=keys, writes=keys)
                mk.op("dve", lambda e: e.tensor_tensor(out=out_i, in0=ar_, in1=bi_, op=ALU.mult), reads=keys, writes=keys)
                mk.op(PE2, lambda e: e.tensor_tensor(out=tmp, in0=ai_, in1=br_, op=ALU.mult), reads=keys, writes=keys)
                if neg_im:
                    mk.op("dve", lambda e: e.scalar_tensor_tensor(out=out_i, in0=out_i, scalar=-1.0, in1=tmp, op0=ALU.mult, op1=ALU.subtract),
                          reads=keys, writes=keys)
                else:
                    mk.op("dve", lambda e: e.tensor_tensor(out=out_i, in0=out_i, in1=tmp, op=ALU.add), reads=keys, writes=keys)
            if mode == "pre":
                Bn = [P.aalloc([128, 16, 16]) for _ in range(2)]
                Bb = [P.aalloc([128, 16, 16]) for _ in range(2)]
                Cn = [P.aalloc([128, 16, 16]) for _ in range(2)]
                TPm = [P.aalloc([128, 16, 8]) for _ in range(4)]
                KL = P.aalloc([128, 16, 8])
                KI = P.aalloc([128, 16, 8])
                tmpk = P.aalloc([128, 16, 8])
                Pre = P.aalloc([128, 16, 8, 16])
                Pim = P.aalloc([128, 16, 8, 16])
                Qre = P.aalloc([128, 16, 8, 16])
                Qsim = P.aalloc([128, 16, 8, 16])
                tA = P.aalloc([128, 16, 8, 16])
                cnat = P.aalloc([128, 64])
                ang = P.aalloc([128, 16, 33])
                tng = P.aalloc([128, 16, 33])
                HK = [H + "pre"]
                for ri, (bsrc, csrc) in enumerate(((s5_b_re, s5_c_re), (s5_b_im, s5_c_im))):
                    for d in range(2):
                        mk.dma(Bn[ri][d * 64:(d + 1) * 64, :, :], rap(bsrc, (d * 32 + g0) * 1024, [[16, 64], [1024, 16], [1, 16]]), writes=HK)
                        for gq in range(2):
                            mk.dma(cnat, rap(csrc, (d * 32 + g0 + gq * 8) * 1024, [[64, 128], [1, 64]]), writes=[H + "cnat"])
                            pst, psk = P.psum()
                            P.mm(pst[d * 64:(d + 1) * 64, 0:128], cnat, ident[:], True, True, reads=[H + "cnat", "ident"], writes=[psk])
                            P.copy("dve", Cn[ri][d * 64:(d + 1) * 64, gq * 8:(gq + 1) * 8, :],
                                   pst[d * 64:(d + 1) * 64, 0:128].rearrange("p (g c) -> p g c", c=16), reads=[psk], writes=HK)
                def bc_g(nm, inner):
                    return rap(P.arena, off(sc[nm]) + g0, [[ASIZE, 128], [1, 16], [0, inner]])
                def cmul(out_r, out_i, ar_, ai_, br_, bi_, tmp, keys, neg_im=False):
                    mk.op("dve", lambda e: e.tensor_tensor(out=out_r, in0=ar_, in1=br_, op=ALU.mult), reads=keys, writes=keys)
                    mk.op("dve", lambda e: e.tensor_tensor(out=tmp, in0=ai_, in1=bi_, op=ALU.mult), reads=keys, writes=keys)
                    mk.op("dve", lambda e: e.tensor_tensor(out=out_r, in0=out_r, in1=tmp, op=ALU.subtract), reads=keys, writes=keys)
                    mk.op("dve", lambda e: e.tensor_tensor(out=out_i, in0=ar_, in1=bi_, op=ALU.mult), reads=keys, writes=keys)
                    mk.op("dve", lambda e: e.tensor_tensor(out=tmp, in0=ai_, in1=br_, op=ALU.mult), reads=keys, writes=keys)
                    if neg_im:
                        mk.op("dve", lambda e: e.scalar_tensor_tensor(out=out_i, in0=out_i, scalar=-1.0, in1=tmp, op0=ALU.mult, op1=ALU.subtract),
                              reads=keys, writes=keys)
                    else:
                        mk.op("dve", lambda e: e.tensor_tensor(out=out_i, in0=out_i, in1=tmp, op=ALU.add), reads=keys, writes=keys)
                AK = HK + SK
                cmul(Bb[0], Bb[1], bc_g("fr", 16), bc_g("fi", 16), Bn[0], Bn[1], tA[:, :, 0, :], AK)
                kkb = rap(P.arena, off(kk), [[ASIZE, 128], [0, 16], [1, 8]])
                lr8b, li8b = bc_g("lr", 8), bc_g("li", 8)
                mk.op("dve", lambda e, lr8b=lr8b, KL=KL, kkb=kkb: e.tensor_tensor(out=KL, in0=kkb, in1=lr8b, op=ALU.mult), reads=AK, writes=HK)
                mk.op("dve", lambda e, li8b=li8b, KI=KI, kkb=kkb: e.tensor_tensor(out=KI, in0=kkb, in1=li8b, op=ALU.mult), reads=AK, writes=HK)
                sincos(KI, 128, TPm[0], TPm[1], tmpk, HK, HK)
                mk.op("act", lambda e: e.activation(out=tmpk, in_=KL, func=AF.Exp, scale=-1.0), reads=HK, writes=HK)
                mk.op("dve", lambda e: e.tensor_tensor(out=TPm[2], in0=TPm[0], in1=tmpk, op=ALU.mult), reads=HK, writes=HK)
                mk.op("dve", lambda e: e.scalar_tensor_tensor(out=TPm[3], in0=TPm[1], scalar=-1.0, in1=tmpk, op0=ALU.mult, op1=ALU.mult), reads=HK, writes=HK)
                mk.op("act", lambda e: e.activation(out=tmpk, in_=KL, func=AF.Exp), reads=HK, writes=HK)
                mk.op("dve", lambda e: e.tensor_tensor(out=TPm[0], in0=TPm[0], in1=tmpk, op=ALU.mult), reads=HK, writes=HK)
                mk.op("dve", lambda e: e.tensor_tensor(out=TPm[1], in0=TPm[1], in1=tmpk, op=ALU.mult), reads=HK, writes=HK)
                def tb(t_):
                    return rap(P.arena, off(t_), [[ASIZE, 128], [8, 16], [1, 8], [0, 16]])
                def vb_(t_):
                    return rap(P.arena, off(t_), [[ASIZE, 128], [16, 16], [0, 8], [1, 16]])
                yield
                cmul(Pre, Pim, tb(TPm[2]), tb(TPm[3]), vb_(Bb[0]), vb_(Bb[1]), tA, HK)
                yield
                cmul(Qre, Qsim, tb(TPm[0]), tb(TPm[1]), vb_(Cn[0]), vb_(Cn[1]), tA, HK, neg_im=True)
                mk.op("act", lambda e: e.activation(out=Qre_b, in_=Qre.rearrange("p g b c -> p g (b c)"), func=AF.Copy), reads=HK, writes=[H + "Qb"])
                mk.op("act", lambda e: e.activation(out=Qsim_b, in_=Qsim.rearrange("p g b c -> p g (b c)"), func=AF.Copy), reads=HK, writes=[H + "Qb"])
                yield
                for g4 in range(4):
                    yield
                    for (src, dst, nm) in ((Pre, PTre, "PTre"), (Pim, PTim, "PTim")):
                        pst, psk = P.psum()
                        for gg in range(4):
                            g = g4 * 4 + gg
                            P.transpose(pst[:, gg * 128:(gg + 1) * 128], src[:, g, :, :].rearrange("p b c -> p (b c)"), ident[:],
                                        reads=HK + ["ident"], writes=[psk])
                        P.copy(P.evac_eng(), dst[:, g4 * 4:(g4 + 1) * 4, :], pst[:].rearrange("p (g x) -> p g x", x=128), reads=[psk], writes=[H + nm])
                    pf, pfk = P.psum()
                    pb_, pbk = P.psum()
                    for gg in range(4):
                        g = g4 * 4 + gg
                        for (pp, ppk, rows) in ((pf, pfk, slice(0, 64)), (pb_, pbk, slice(64, 128))):
                            P.mm(pp[:, gg * 128:(gg + 1) * 128], Pre[rows, g, :, :].rearrange("p b c -> p (b c)"),
                                 Qre[rows, g, :, :].rearrange("p b c -> p (b c)"), True, False, reads=HK, writes=[ppk])
                            P.mm(pp[:, gg * 128:(gg + 1) * 128], Pim[rows, g, :, :].rearrange("p b c -> p (b c)"),
                                 Qsim[rows, g, :, :].rearrange("p b c -> p (b c)"), False, True, reads=HK, writes=[ppk])
                    mfv = rap(P.arena, off(mfb), [[ASIZE, 128], [0, 4], [1, 128]])
                    mbv = rap(P.arena, off(mfb) + 128, [[ASIZE, 128], [0, 4], [1, 128]])
                    t4 = tA[:, 0:4, :, :].rearrange("p g b c -> p g (b c)")
                    mk.op("dve", lambda e, pf=pf, mfv=mfv, t4=t4: e.tensor_tensor(out=t4, in0=pf[:].rearrange("p (g x) -> p g x", x=128), in1=mfv, op=ALU.mult),
                          reads=[pfk, Z + "mfb"] + HK, writes=HK)
                    mk.op("dve", lambda e, pb_=pb_, mbv=mbv, g4=g4: e.tensor_tensor(out=WT[:, g4 * 4:(g4 + 1) * 4, :], in0=pb_[:].rearrange("p (g x) -> p g x", x=128), in1=mbv, op=ALU.mult),
                          reads=[pbk, Z + "mfb"], writes=[H + "WT"])
                    mk.op("dve", lambda e, t4=t4, g4=g4: e.tensor_tensor(out=WT[:, g4 * 4:(g4 + 1) * 4, :], in0=WT[:, g4 * 4:(g4 + 1) * 4, :], in1=t4, op=ALU.add),
                          reads=HK + [H + "WT"], writes=[H + "WT"])
                arb = rap(P.arena, off(arow), [[ASIZE, 128], [0, 16], [1, 33]])
                li8_33, rho33, rho32 = bc_g("li8", 33), bc_g("rho8", 33), bc_g("rho8", 32)
                mk.op("dve", lambda e, li8_33=li8_33, ang=ang, arb=arb: e.tensor_tensor(out=ang, in0=arb, in1=li8_33, op=ALU.mult), reads=AK, writes=HK)
                sincos(ang, 528, Rr, R2i, tng, HK, [H + "tab"], turns=True)
                mk.op("dve", lambda e: e.tensor_scalar(out=Ri, in0=R2i, scalar1=-1.0, scalar2=None, op0=ALU.mult), reads=[H + "tab"], writes=[H + "tab"])
                mk.op("dve", lambda e, rho33=rho33, R2r=R2r, Rr=Rr: e.tensor_tensor(out=R2r, in0=Rr, in1=rho33, op=ALU.mult), reads=[H + "tab"] + SK, writes=[H + "tab"])
                mk.op("dve", lambda e, rho33=rho33, R2i=R2i: e.tensor_tensor(out=R2i, in0=R2i, in1=rho33, op=ALU.mult), reads=[H + "tab"] + SK, writes=[H + "tab"])
                mk.op("dve", lambda e, D0=D0: e.memset(D0, 1.0), writes=[H + "tab"])
                mk.op("dve", lambda e, rho32=rho32, D0=D0: e.tensor_tensor(out=D0, in0=D0, in1=rho32, op=ALU.mult), reads=SK + [H + "tab"], writes=[H + "tab"])
                mk.op("dve", lambda e: e.memset(D0[:, :, 0:1], 0.0), reads=[H + "tab"], writes=[H + "tab"])
                mk.dma(s5scr.ap()[gh], P.arena[:, markp:mark1], reads=[H + "PTre", H + "PTim", H + "Qb", H + "WT", H + "tab"], writes=[("s5scr", gh)])
                yield
                continue
            else:
                mk.dma(P.arena[:, markp:mark1], s5scr.ap()[gh], reads=[("s5scr", gh)], writes=[H + "PTre", H + "PTim", H + "Qb", H + "WT", H + "tab"])
            mk.fence()
            P.atop = mark1
            U2T = P.aalloc([128, 16, 128], BF16)
            Gr = P.aalloc([128, NS, 16, 32])
            Gi = P.aalloc([128, NS, 16, 32])
            Wr = P.aalloc([128, NS, 16, 32])
            Wi = P.aalloc([128, NS, 16, 32])
            t1_ = P.aalloc([128, NS, 16, 32])
            Fr = P.aalloc([128, 16, 128], BF16)
            Fi = P.aalloc([128, 16, 128], BF16)
            cr = P.aalloc([128, 16])
            ci_ = P.aalloc([128, 16])
            ct = P.aalloc([128, 16])
            RK = [H + "run"]
            for g4 in range(4):
                pst, psk = P.psum()
                for gg in range(4):
                    g = g0 + g4 * 4 + gg
                    P.transpose(pst[:, gg * 128:(gg + 1) * 128], u_tok[:, g, :, :].rearrange("p b c -> p (b c)"), ident[:],
                                reads=[Z + "u_tok", "ident"], writes=[psk])
                P.copy(P.evac_eng(), U2T[:, g4 * 4:(g4 + 1) * 4, :], pst[:].rearrange("p (g x) -> p g x", x=128), reads=[psk], writes=[H + "U2T"])
            for g4 in range(4):
                for (PT_, Gd, nm) in ((PTre, Gr, "Gr"), (PTim, Gi, "Gi")):
                    pst, psk = P.psum()
                    for gg in range(4):
                        g = g4 * 4 + gg
                        P.mm(pst[:, gg * 128:(gg + 1) * 128], PT_[:, g, :], U2T[:, g, :], True, True,
                             reads=[H + "PTre", H + "PTim", H + "U2T"], writes=[psk])
                    o_f = rap(P.arena, off(Gd) + g4 * 4 * 32, [[ASIZE, 64], [32, 4], [512, NS], [1, 32]])
                    i_f = rap(pst, 0, [[512, 64], [128, 4], [32, NS], [1, 32]])
                    mk.op("act", lambda e, o_f=o_f, i_f=i_f: e.activation(out=o_f, in_=i_f, func=AF.Copy), reads=[psk], writes=RK)
                    o_b = rap(P.arena, off(Gd) + 64 * ASIZE + g4 * 4 * 32 + (NS - 1) * 512 + 31, [[ASIZE, 64], [32, 4], [-512, NS], [-1, 32]])
                    i_b = rap(pst, 64 * 512, [[512, 64], [128, 4], [32, NS], [1, 32]])
                    mk.op("dve", lambda e, o_b=o_b, i_b=i_b: e.tensor_copy(out=o_b, in_=i_b), reads=[psk], writes=RK)
            def rtab(t_, a0, n):
                return rap(P.arena, off(t_) + a0, [[ASIZE, 128], [0, NS], [33, 16], [1, n]])
            TK = [H + "tab"]
            mk.op("dve", lambda e: e.tensor_tensor(out=Wr, in0=Gr, in1=rtab(Rr, 0, 32), op=ALU.mult), reads=RK + TK, writes=RK)
            mk.op("pool", lambda e: e.tensor_tensor(out=t1_, in0=Gi, in1=rtab(Ri, 0, 32), op=ALU.mult), reads=RK + TK, writes=RK)
            mk.op("dve", lambda e: e.tensor_tensor(out=Wr, in0=Wr, in1=t1_, op=ALU.subtract), reads=RK, writes=RK)
            mk.op("dve", lambda e: e.tensor_tensor(out=Wi, in0=Gr, in1=rtab(Ri, 0, 32), op=ALU.mult), reads=RK + TK, writes=RK)
            mk.op("pool", lambda e: e.tensor_tensor(out=t1_, in0=Gi, in1=rtab(Rr, 0, 32), op=ALU.mult), reads=RK + TK, writes=RK)
            mk.op("dve", lambda e: e.tensor_tensor(out=Wi, in0=Wi, in1=t1_, op=ALU.add), reads=RK, writes=RK)
            d0flat = D0.rearrange("p g a -> p (g a)")
            for s in range(NS):
                if v == 1:
                    srcr = bc_g("fsr", 1) if s == 0 else cr
                    srci = bc_g("fsi", 1) if s == 0 else ci_
                    srcr = rap(P.arena, off(sc["fsr"]) + g0, [[ASIZE, 128], [1, 16]]) if s == 0 else cr
                    srci = rap(P.arena, off(sc["fsi"]) + g0, [[ASIZE, 128], [1, 16]]) if s == 0 else ci_
                    mk.op("dve", lambda e, s=s, srcr=srcr: e.tensor_tensor(out=Wr[:, s, :, 0], in0=Wr[:, s, :, 0], in1=srcr, op=ALU.add), reads=RK + SK, writes=RK)
                    mk.op("dve", lambda e, s=s, srci=srci: e.tensor_tensor(out=Wi[:, s, :, 0], in0=Wi[:, s, :, 0], in1=srci, op=ALU.add), reads=RK + SK, writes=RK)
                for (src, dst) in ((Wr, Gr), (Wi, Gi)):
                    mk.op("dve", lambda e, s=s, src=src, dst=dst: e.tensor_tensor_scan(
                        out=dst[:, s, :, :].rearrange("p g a -> p (g a)"), data0=d0flat, data1=src[:, s, :, :].rearrange("p g a -> p (g a)"),
                        initial=0.0, op0=ALU.mult, op1=ALU.add), reads=RK + TK, writes=RK)
                if v == 1 and s < NS - 1:
                    r2r = rap(P.arena, off(R2r) + 32, [[ASIZE, 128], [33, 16]])
                    r2i = rap(P.arena, off(R2i) + 32, [[ASIZE, 128], [33, 16]])
                    wr_ = Gr[:, s, :, 31]
                    wi_ = Gi[:, s, :, 31]
                    cmul(cr, ci_, r2r, r2i, wr_, wi_, ct, RK + TK)
            if v == 0:
                for s in range(NS):
                    rfr = rap(P.arena, off(sc["rfr"]) + g0, [[ASIZE, 128], [1, 16]])
                    rfi = rap(P.arena, off(sc["rfi"]) + g0, [[ASIZE, 128], [1, 16]])
                    for (rows, ss) in ((slice(0, 64), s), (slice(64, 128), NS - 1 - s)):
                        p0 = rows.start
                        rfr_ = rap(P.arena, off(sc["rfr"]) + p0 * ASIZE + g0, [[ASIZE, 64], [1, 16]])
                        rfi_ = rap(P.arena, off(sc["rfi"]) + p0 * ASIZE + g0, [[ASIZE, 64], [1, 16]])
                        cmul(hl[rows, 0, s, g0:g0 + 16], hl[rows, 1, s, g0:g0 + 16], rfr_, rfi_, Gr[rows, ss, :, 31], Gi[rows, ss, :, 31],
                             ct[rows, :], RK + SK + [Z + "hl"])
            def sh(t_, a0, n):
                return rap(P.arena, off(t_) + a0, [[ASIZE, 128], [512, NS], [32, 16], [1, n]])
            mk.op("dve", lambda e: e.tensor_tensor(out=sh(Wr, 1, 31), in0=sh(Gr, 0, 31), in1=rtab(R2r, 1, 31), op=ALU.mult), reads=RK + TK, writes=RK)
            mk.op("pool", lambda e: e.tensor_tensor(out=sh(t1_, 1, 31), in0=sh(Gi, 0, 31), in1=rtab(R2i, 1, 31), op=ALU.mult), reads=RK + TK, writes=RK)
            mk.op("dve", lambda e: e.tensor_tensor(out=sh(Wr, 1, 31), in0=sh(Wr, 1, 31), in1=sh(t1_, 1, 31), op=ALU.subtract), reads=RK, writes=RK)
            mk.op("dve", lambda e: e.tensor_tensor(out=sh(Wi, 1, 31), in0=sh(Gr, 0, 31), in1=rtab(R2i, 1, 31), op=ALU.mult), reads=RK + TK, writes=RK)
            mk.op("pool", lambda e: e.tensor_tensor(out=sh(t1_, 1, 31), in0=sh(Gi, 0, 31), in1=rtab(R2r, 1, 31), op=ALU.mult), reads=RK + TK, writes=RK)
            mk.op("dve", lambda e: e.tensor_tensor(out=sh(Wi, 1, 31), in0=sh(Wi, 1, 31), in1=sh(t1_, 1, 31), op=ALU.add), reads=RK, writes=RK)
            for s in range(NS):
                if v == 0:
                    mk.op("pool", lambda e, s=s: e.memset(Wr[:, s, :, 0], 0.0), reads=RK, writes=RK)
                    mk.op("pool", lambda e, s=s: e.memset(Wi[:, s, :, 0], 0.0), reads=RK, writes=RK)
            if v == 1:
                for s in range(NS):
                    if s == 0:
                        mk.op("dve", lambda e: e.tensor_copy(out=Wr[:, 0, :, 0], in_=rap(P.arena, off(sc["fsr"]) + g0, [[ASIZE, 128], [1, 16]])), reads=RK + SK, writes=RK)
                        mk.op("dve", lambda e: e.tensor_copy(out=Wi[:, 0, :, 0], in_=rap(P.arena, off(sc["fsi"]) + g0, [[ASIZE, 128], [1, 16]])), reads=RK + SK, writes=RK)
                    else:
                        r2r = rap(P.arena, off(R2r) + 32, [[ASIZE, 128], [33, 16]])
                        r2i = rap(P.arena, off(R2i) + 32, [[ASIZE, 128], [33, 16]])
                        cmul(Wr[:, s, :, 0], Wi[:, s, :, 0], r2r, r2i, Gr[:, s - 1, :, 31], Gi[:, s - 1, :, 31], ct, RK + TK)
            for (src, dst) in ((Wr, Fr), (Wi, Fi)):
                mk.op("act", lambda e, src=src, dst=dst: e.activation(
                    out=dst[0:64, :, :].rearrange("p g (s a) -> p s g a", a=32), in_=src[0:64, :, :, :], func=AF.Copy), reads=RK, writes=[H + "F"])
                i_b = rap(P.arena, off(src) + 64 * ASIZE + (NS - 1) * 512 + 31, [[ASIZE, 64], [-512, NS], [32, 16], [-1, 32]])
                mk.op("dve", lambda e, dst=dst, i_b=i_b: e.tensor_copy(out=dst[64:128, :, :].rearrange("p g (s a) -> p s g a", a=32), in_=i_b),
                      reads=RK, writes=[H + "F"])
            for g4 in range(4):
                pst, psk = P.psum()
                for gg in range(4):
                    g = g4 * 4 + gg
                    cs = slice(gg * 128, (gg + 1) * 128)
                    P.mm(pst[:, cs], U2T[:, g, :], WT[:, g, :], True, False, reads=[H + "U2T", H + "WT"], writes=[psk])
                    P.mm(pst[:, cs], Fr[:, g, :], Qre_b[:, g, :], False, False, reads=[H + "F", H + "Qb"], writes=[psk])
                    P.mm(pst[:, cs], Fi[:, g, :], Qsim_b[:, g, :], False, True, reads=[H + "F", H + "Qb"], writes=[psk])
                ga = g0 + g4 * 4
                uv = u_tok[:, ga:ga + 4, :, :]
                dbv = rap(P.arena, off(dB) + ga * 16, [[ASIZE, 128], [16, 4], [0, 8], [1, 16]])
                mk.op("pool", lambda e, uv=uv, dbv=dbv: e.tensor_tensor(out=uv, in0=uv, in1=dbv, op=ALU.mult), reads=[Z + "u_tok", Z + "dB", H + "U2T"], writes=[Z + "u_tok"])
                mk.op("dve", lambda e, uv=uv, pst=pst: e.tensor_tensor(out=uv, in0=uv, in1=pst[:].rearrange("p (g b c) -> p g b c", b=8, c=16), op=ALU.add),
                      reads=[psk, Z + "u_tok"], writes=[Z + "u_tok"])
            mk.fence()
        if mode == "pre":
            return
        P.atop = mark0
        if v == 0:
            hst = P.aalloc([128, 2, 8, 64])
            for ri in range(2):
                for s in range(4):
                    for d in range(2):
                        pst, psk = P.psum()
                        P.mm(pst[0:32, 0:64], hl[d * 64:(d + 1) * 64, ri, s, :], ident[d * 64:(d + 1) * 64, d * 64:(d + 1) * 64], True, True,
                             reads=[Z + "hl", "ident"], writes=[psk])
                        P.copy(P.evac_eng(), hst[0:32, ri, s * 2 + d, :], pst[0:32, 0:64], reads=[psk], writes=[Z + "hst"])
                ok = ("o_s5", ri)
                dst = o_s5r if ri == 0 else o_s5i
                mk.dma(rap(dst, 0, [[64, 32], [2048, 8], [1, 64]]), hst[0:32, ri, :, :], reads=[Z + "hst"], writes=[ok])
                P.outs.append(ok)
        ga_ = P.aalloc([128, 32, 8, 16])
        g_tok = P.aalloc([128, 8, 512], BF16)
        gT = P.aalloc([128, 4, T], BF16)
        bglu = P.aalloc([128, 4])
        UK = [Z + "u_tok"]
        mk.op("dve", lambda e: e.tensor_tensor(out=ga_, in0=u_tok, in1=u_tok, op=ALU.mult), reads=UK, writes=[Z + "ga"])
        mk.op("dve", lambda e: e.tensor_scalar(out=ga_, in0=ga_, scalar1=0.044715, scalar2=1.0, op0=ALU.mult, op1=ALU.add), reads=[Z + "ga"], writes=[Z + "ga"])
        mk.op("dve", lambda e: e.tensor_tensor(out=ga_, in0=ga_, in1=u_tok, op=ALU.mult), reads=UK + [Z + "ga"], writes=[Z + "ga"])
        mk.op("act", lambda e: e.activation(out=ga_, in_=ga_, func=AF.Sigmoid, scale=2.0 * math.sqrt(2.0 / math.pi)), reads=[Z + "ga"], writes=[Z + "ga"])
        mk.op("dve", lambda e: e.tensor_tensor(out=g_tok.rearrange("p b (g c) -> p g b c", c=16), in0=ga_, in1=u_tok, op=ALU.mult),
              reads=UK + [Z + "ga"], writes=[Z + "g_tok"])
        P.load_T(bglu, s5_b_glu, 0, 4, Z + "bglu")
        for b in range(8):
            pst, psk = P.psum()
            psb = pst[:].bitcast(BF16)
            for cc in range(4):
                P.transpose(psb[:, cc * 128:(cc + 1) * 128], g_tok[:, b, cc * 128:(cc + 1) * 128], identb[:], reads=[Z + "g_tok", "identb"], writes=[psk])
            P.copy(P.evac_eng(), gT[:, :, b:T:8], psb[:, 0:512].rearrange("p (c t) -> p c t", t=128), reads=[psk], writes=[Z + "gT"])
        i = state["w"] % 2
        state["w"] += 1
        wgl, wglk = wbuf[i], "wbuf%d" % i
        mk.dma(wgl[:, 0:4, :], rap(s5_w_glu, 0, [[512, 128], [128 * 512, 4], [1, 512]]), writes=[wglk], q="pool")
        for m in range(4):
            for th in range(2):
                ts = slice(th * 512, (th + 1) * 512)
                pst, psk = P.psum()
                for kc in range(4):
                    P.mm(pst[:], wgl[:, kc, m * 128:(m + 1) * 128], gT[:, kc, ts], kc == 0, kc == 3, reads=[wglk, Z + "gT"], writes=[psk])
                s_ = sq[(m * 2 + th) % 2]
                sk_ = "sq%d" % ((m * 2 + th) % 2)
                mk.op("act", lambda e, pst=pst, s_=s_, m=m: e.activation(out=s_[:], in_=pst[:], func=AF.Sigmoid, bias=bglu[:, m:m + 1]),
                      reads=[psk, Z + "bglu"], writes=[sk_])
                mk.op("dve", lambda e, s_=s_, m=m, ts=ts: e.tensor_tensor(out=mixT[:, m, ts], in0=gT[:, m, ts], in1=s_[:], op=ALU.mult),
                      reads=[sk_, Z + "gT"], writes=["mixT"])
        mk.fence()
        P.atop = 0

    def ab_layer(v):
        if v == 0:
            chains = []
            for s in range(4):
                chains.append((0, [2 * s, 2 * s + 1], s))
                chains.append((1, [2 * s + 1, 2 * s], s))
        else:
            chains = [(0, list(range(8)), 0), (1, list(range(7, -1, -1)), 0)]
        for _ in s5_part(v):
            pass
        gdn_part(v, chains)

    def xload_gen(v, xtok):
        n = len(xtok)
        for tt in range(8):
            xt_, xk = xtok[tt % n], "xtok%d" % (tt % n)
            mk.dma(xt_, xin.ap()[v, tt * 128:(tt + 1) * 128, :], writes=[xk])
            for cg in range(2):
                pst, psk = P.psum()
                for cc in range(4):
                    c = cg * 4 + cc
                    P.transpose(pst[:, cc * 128:(cc + 1) * 128], xt_[:, c * 128:(c + 1) * 128], ident[:],
                                reads=[xk, "ident"], writes=[psk])
                P.copy(P.evac_eng(), xT[:, cg * 4:(cg + 1) * 4, tt * 128:(tt + 1) * 128],
                       pst[:].rearrange("p (c t) -> p c t", t=128), reads=[psk], writes=[("xT", tt // 4)])
            yield

    P.psmod = 7
    P.atop = adaln_top
    ga, gothers = adaln_gen(), [s5_part(0, "pre", P.atop)]
    alive = True
    while alive or gothers:
        for _ in range(2):
            if alive:
                try:
                    next(ga)
                except StopIteration:
                    alive = False
        for g_ in list(gothers):
            try:
                next(g_)
            except StopIteration:
                gothers.remove(g_)
    P.psmod = 8
    adaln_finish()

    for v in range(2):
        seqlen = 256 if v == 0 else 1024
        P.atop = 0
        for _ in xload_gen(v, [P.aalloc([128, D]) for _ in range(4)]):
            pass
        mk.fence()
        for l in range(2):
            norm_mod(l, 0, v)
            if l == 0:
                ab_layer(v)
                out_proj(w_out_ab, l, v)
            else:
                attn_layer(v)
                out_proj(w_out_cd, l, v)
            norm_mod(l, 1, v)
            ffn(l, v, seqlen)
        P.atop = 0
        xtok = [P.aalloc([128, D]) for _ in range(4)]
        for tt in range(8):
            xt_, xk = xtok[tt % 4], "xtok%d" % (tt % 4)
            for cg in range(2):
                pst, psk = P.psum()
                for cc in range(4):
                    c = cg * 4 + cc
                    P.transpose(pst[:, cc * 128:(cc + 1) * 128], xT[:, c, tt * 128:(tt + 1) * 128], ident[:],
                                reads=[("xT", tt // 4), "ident"], writes=[psk])
                P.copy(P.evac_eng(), xt_[:, cg * 512:(cg + 1) * 512], pst[:], reads=[psk], writes=[xk])
            ok = ("out_y", v, tt)
            mk.dma(yout.ap()[v, tt * 128:(tt + 1) * 128, :], xt_, reads=[xk], writes=[ok])
            P.outs.append(ok)
        if v == 1:
            mk.fence()

    mk.emit(final_keys=P.outs)
    P.es.close()
    return nc


_CACHE = {}


def _consts():
    c = {}
    c["c_ident"] = np.eye(128, dtype=np.float32)
    n = T
    row = np.repeat(np.arange(n // 64), 64).astype(np.float32)
    col = np.tile(np.arange(64), n // 64).astype(np.float32)
    inv = (10000.0 ** (-np.arange(16, dtype=np.float32) / 16)).astype(np.float32)
    ar = row[:, None] * inv[None, :]
    ac = col[:, None] * inv[None, :]
    c["c_rope"] = np.stack([np.cos(ar), np.sin(ar), np.cos(ac), np.sin(ac)]).astype(np.float32)
    q = np.arange(128)[:, None]
    k = np.arange(128)[None, :]
    NEG = -30000.0
    m1 = np.where(k >= q, 0.0, NEG)
    m3 = np.where(k <= q, 0.0, NEG)
    c["c_wmask"] = np.concatenate([m1.T, np.zeros((128, 128)), m3.T], axis=1).astype(np.float32)
    i = np.arange(128)[:, None]
    j = np.arange(128)[None, :]
    same = (i // 64) == (j // 64)
    LE = ((i <= j) & same).astype(np.float32)
    GE = ((i >= j) & same).astype(np.float32)
    eye = np.eye(128, dtype=np.float32)
    c["c_tri"] = np.stack([LE, GE, 1 - LE, 1 - GE, GE - eye, LE - eye]).astype(np.float32)
    sel = np.zeros((4, 128, 128), np.float32)
    for n_, r_ in enumerate((63, 127, 0, 64)):
        sel[n_, r_, :] = 1.0
    c["c_sel"] = sel
    kkt = np.zeros((128, 8), np.float32)
    kkt[0:64, :] = np.arange(8)[None, :]
    kkt[64:128, :] = 7 - np.arange(8)[None, :]
    c["c_kk"] = kkt
    c["c_arow"] = np.tile(np.arange(33, dtype=np.float32)[None, :], (128, 1))
    bq = (np.arange(128) // 16)
    c["c_mfb"] = np.stack([(bq[:, None] <= bq[None, :]), (bq[:, None] >= bq[None, :])]).astype(np.float32)
    return c


SHARED = ("w_mod", "b_mod", "norm1_g", "norm2_g", "ffn_up", "ffn_conv_w", "ffn_conv_b", "ffn_down",
          "w_out_ab", "c_sink", "d_subln")


def make_in_map(inp, r, consts):
    b = r // 4
    m = {k: inp[k].reshape(inp[k].shape[1:]) if k in ("w_out_ab", "c_sink", "d_subln") else inp[k] for k in SHARED}
    m.update(consts)
    m["w_in_cd"] = inp["w_in_cd"][0]
    m["w_out_cd"] = inp["w_out_cd"][0]
    m["gcat"] = np.stack([inp["c_qn"][0], inp["c_kn"][0], inp["d_qn"][0], inp["d_kn"][0]])
    m["lqk"] = np.stack([inp["d_lq1"][0], inp["d_lk1"][0], inp["d_lq2"][0], inp["d_lk2"][0]])
    m["ccat"] = np.concatenate([inp["cache_c_k"][b, 0].reshape(512, 128), inp["cache_c_v"][b, 0].reshape(512, 128),
                                inp["cache_d_k"][b, 0].reshape(512, 512), inp["cache_d_v"][b, 0].reshape(512, 512)], axis=1)
    m["w_in_ab"] = inp["w_in_ab"][0]
    m["gdn_conv_w"] = inp["gdn_conv_w"][0]
    m["gdn_a_log"] = inp["gdn_a_log"][0].reshape(8)
    m["gdn_dt_bias"] = inp["gdn_dt_bias"][0].reshape(8)
    m["gdn_norm_g"] = inp["gdn_norm_g"][0]
    m["gstate"] = inp["state_gdn"][b, 0]
    for k_ in ("s5_lam_re", "s5_lam_im"):
        m[k_] = inp[k_][0].reshape(64, 64)
    m["s5_log_dt"] = inp["s5_log_dt"][0].reshape(64)
    for k_ in ("s5_b_re", "s5_b_im"):
        m[k_] = inp[k_][0].reshape(64, 64, 16)
    for k_ in ("s5_c_re", "s5_c_im"):
        m[k_] = inp[k_][0].reshape(64, 16, 64)
    m["s5_d"] = inp["s5_d"][0]
    m["s5_w_glu"] = inp["s5_w_glu"][0]
    m["s5_b_glu"] = inp["s5_b_glu"][0]
    m["s5h_re"] = inp["state_s5_re"][b, 0].reshape(64, 64)
    m["s5h_im"] = inp["state_s5_im"][b, 0].reshape(64, 64)
    m["xin"] = np.stack([inp["x_prompt"][4 * r:4 * r + 4].reshape(T, D), inp["x_sample"][b]])
    m["cvec"] = np.stack([inp["c_ctx"], inp["c"][b]])
    return {k: np.ascontiguousarray(v, dtype=np.float32) for k, v in m.items()}


def kernel(**inp):
    inp = {k: np.asarray(v) for k, v in inp.items()}
    ncores = 8
    if "nc" not in _CACHE:
        nc = bass.Bass("TRN2", target_bir_lowering=False)
        build(nc)
        _CACHE["nc"] = nc
    nc = _CACHE["nc"]
    consts = _consts()
    in_maps = [make_in_map(inp, r, consts) for r in range(ncores)]
    res = run_bass_kernel_spmd(nc, in_maps, core_ids=list(range(ncores)))
    rs = res.results
    y_prompt = np.concatenate([rs[r]["yout"][0].reshape(4, 256, D) for r in range(ncores)], axis=0)
    y_sample = np.stack([rs[0]["yout"][1], rs[4]["yout"][1]])
    new_c_k = np.concatenate([rs[r]["o_ck"].reshape(4, 1, 256, 2, 64) for r in range(ncores)], axis=0)
    new_c_v = np.concatenate([rs[r]["o_cv"].reshape(4, 1, 256, 2, 64) for r in range(ncores)], axis=0)
    new_d_k = np.concatenate([rs[r]["o_dk"].reshape(4, 1, 256, 4, 2, 64) for r in range(ncores)], axis=0)
    new_d_v = np.concatenate([rs[r]["o_dv"].reshape(4, 1, 256, 4, 128) for r in range(ncores)], axis=0)
    new_gdn = np.concatenate([rs[r]["o_gdn"].reshape(4, 1, 2, 4, 128, 128) for r in range(ncores)], axis=0)
    new_s5_re = np.concatenate([rs[r]["o_s5r"].reshape(4, 1, 2, 32, 64) for r in range(ncores)], axis=0)
    new_s5_im = np.concatenate([rs[r]["o_s5i"].reshape(4, 1, 2, 32, 64) for r in range(ncores)], axis=0)
    return y_prompt, y_sample, new_s5_re, new_s5_im, new_gdn, new_c_k, new_c_v, new_d_k, new_d_v
```

```python
import contextlib
import math
import numpy as np
import concourse.bass as bass
import concourse.mybir as mybir
from concourse.bass_utils import run_bass_kernel_spmd

F32 = mybir.dt.float32
BF16 = mybir.dt.bfloat16
ALU = mybir.AluOpType
AF = mybir.ActivationFunctionType
AX = mybir.AxisListType

COMPUTE = ("pe", "act", "dve", "pool")
NDMASEM = 64

D = 1024
T = 1024
DFF = 2816
NJ = 22
EPS = 1e-6


class MK:
    def __init__(self, nc):
        self.nc = nc
        self.ops = []

    def op(self, eng, fn, reads=(), writes=(), dma=False):
        self.ops.append((eng, fn, tuple(reads), tuple(writes), dma, False))

    def dma(self, out, in_, reads=(), writes=(), q="sp", **kw):
        self.ops.append((q, lambda e: e.dma_start(out=out, in_=in_, **kw), tuple(reads), tuple(writes), True, False))

    def fence(self):
        self.ops.append(("pool", None, (), (), False, True))

    @staticmethod
    def _norm(k):
        if isinstance(k, tuple):
            return (k[0], k[1:] if len(k) > 1 else None)
        return (k, None)

    def emit(self, final_keys=()):
        ops = list(self.ops)
        ops.append(("sp", None, tuple(final_keys), (), False, False))
        n = len(ops)
        st = {}
        seqcnt = 0
        deps_of = [None] * n
        dma_sem_of = [None] * n
        dma_val_of = [0] * n
        dma_sem_last = [None] * NDMASEM
        dma_sem_cnt = [0] * NDMASEM
        ndma = 0
        last_fence = None
        all_since_fence = []

        def entries(name, sub):
            d = st.setdefault(name, {})
            if sub is None:
                return list(d.values())
            out = []
            if sub in d:
                out.append(d[sub])
            if None in d:
                out.append(d[None])
            return out

        for i, (eng, fn, reads, writes, isdma, isfence) in enumerate(ops):
            deps = set()
            if isfence:
                deps.update(all_since_fence)
                if last_fence is not None:
                    deps.add(last_fence)
                all_since_fence = []
                last_fence = i
                deps_of[i] = deps
                continue
            if last_fence is not None:
                deps.add(last_fence)
            for k in reads:
                name, sub = self._norm(k)
                for e in entries(name, sub):
                    if e[0] is not None:
                        deps.add(e[0])
            for k in writes:
                name, sub = self._norm(k)
                for e in entries(name, sub):
                    if e[0] is not None:
                        deps.add(e[0])
                    deps.update(e[1])
            if isdma:
                s = ndma % NDMASEM
                ndma += 1
                if dma_sem_last[s] is not None:
                    deps.add(dma_sem_last[s])
                dma_sem_last[s] = i
                dma_sem_cnt[s] += 16
                dma_sem_of[i] = s
                dma_val_of[i] = dma_sem_cnt[s]
            deps.discard(i)
            deps_of[i] = deps
            all_since_fence.append(i)
            for k in reads:
                name, sub = self._norm(k)
                d = st.setdefault(name, {})
                e = d.setdefault(sub, [None, []])
                if not isdma:
                    e[1] = [r for r in e[1] if ops[r][4] or ops[r][0] != eng]
                e[1].append(i)
            for k in writes:
                name, sub = self._norm(k)
                d = st.setdefault(name, {})
                if sub is None:
                    d.clear()
                d[sub] = [i, []]
        need_inc = [False] * n
        for i in range(n):
            for dd in deps_of[i]:
                if not ops[dd][4]:
                    need_inc[dd] = True
        sig_cnt = {}
        sig_val = [0] * n
        for i in range(n):
            eng = ops[i][0]
            if not ops[i][4] and need_inc[i]:
                sig_cnt[eng] = sig_cnt.get(eng, 0) + 1
                sig_val[i] = sig_cnt[eng]
        known = {}
        known_dma = {}
        streams = {}
        for i in range(n):
            eng = ops[i][0]
            kn = known.setdefault(eng, {})
            kd = known_dma.setdefault(eng, set())
            waits = []
            emax = {}
            dmax = {}
            for dd in deps_of[i]:
                deng = ops[dd][0]
                if ops[dd][4]:
                    if dd in kd:
                        continue
                    kd.add(dd)
                    sl = dma_sem_of[dd]
                    if dma_val_of[dd] > dmax.get(sl, 0):
                        dmax[sl] = dma_val_of[dd]
                else:
                    if deng == eng and eng == "pe":
                        continue
                    v = sig_val[dd]
                    if v > emax.get(deng, 0):
                        emax[deng] = v
            for deng, v in emax.items():
                if kn.get(deng, 0) >= v:
                    continue
                kn[deng] = v
                waits.append(("eng", deng, v))
            for sl, v in dmax.items():
                waits.append(("dma", sl, v))
            streams.setdefault(eng, []).append((i, waits))
        ptr = {e: 0 for e in streams}
        sval = {}
        progress = True
        while progress:
            progress = False
            for e, lst in streams.items():
                while ptr[e] < len(lst):
                    i, waits = lst[ptr[e]]
                    ok = True
                    for w in waits:
                        if sval.get((w[0], w[1]), 0) < w[2]:
                            ok = False
                            break
                    if not ok:
                        break
                    if ops[i][4]:
                        sval[("dma", dma_sem_of[i])] = sval.get(("dma", dma_sem_of[i]), 0) + 16
                    elif need_inc[i]:
                        sval[("eng", ops[i][0])] = sval.get(("eng", ops[i][0]), 0) + 1
                    ptr[e] += 1
                    progress = True
        for e, lst in streams.items():
            if ptr[e] < len(lst):
                raise RuntimeError("deadlock in sync plan: engine %s stuck at op %d waits %s" % (e, lst[ptr[e]][0], lst[ptr[e]][1]))
        self.stats = {e: len(l) for e, l in streams.items()}
        nc = self.nc
        with contextlib.ExitStack() as es:
            esem = {e: es.enter_context(nc.semaphore("s_" + e)) for e in COMPUTE}
            dsem = [es.enter_context(nc.semaphore("d%d" % k)) for k in range(NDMASEM)]
            block = es.enter_context(nc.Block())

            def run(engname):
                def body(e):
                    for (i, waits) in streams.get(engname, []):
                        for w in waits:
                            if w[0] == "dma":
                                e.wait_ge(dsem[w[1]], w[2])
                            else:
                                e.wait_ge(esem[w[1]], w[2])
                        eng, fn, reads, writes, isdma, isfence = ops[i]
                        if fn is None:
                            if need_inc[i]:
                                e.nop().then_inc(esem[eng], 1)
                            continue
                        ins = fn(e)
                        if isdma:
                            ins.then_inc(dsem[dma_sem_of[i]], 16)
                        elif need_inc[i]:
                            ins.then_inc(esem[eng], 1)
                return body

            block.tensor(run("pe"))
            block.scalar(run("act"))
            block.vector(run("dve"))
            block.gpsimd(run("pool"))
            block.sync(run("sp"))
        return n


def rap(t, offset, pattern):
    th = t.tensor if hasattr(t, "tensor") else t
    return bass.AP(th, offset, [list(p) for p in pattern])


class Prog:
    def __init__(self, nc):
        self.nc = nc
        self.mk = MK(nc)
        self.es = contextlib.ExitStack()
        self.dram = {}
        self.psn = 0
        self.evn = 0
        self.outs = []
        self.stn = 0
        self.atop = 0
        self.psmod = 8

    def din(self, name, shape):
        a = self.nc.dram_tensor(name, list(shape), F32, kind="ExternalInput")
        self.dram[name] = a
        return a

    def dout(self, name, shape):
        a = self.nc.dram_tensor(name, list(shape), F32, kind="ExternalOutput")
        self.dram[name] = a
        return a

    def sb(self, name, shape, dt=F32):
        return self.es.enter_context(self.nc.sbuf_tensor(name, list(shape), dt))

    def psum(self):
        k = self.psn % self.psmod
        self.psn += 1
        return self.ps[k], "ps%d" % k

    def evac_eng(self):
        self.evn += 1
        return "dve" if self.evn % 2 else "act"

    def copy(self, eng, out, in_, reads, writes):
        if eng == "act":
            self.mk.op("act", lambda e: e.activation(out=out, in_=in_, func=AF.Copy), reads, writes)
        else:
            self.mk.op(eng, lambda e: e.tensor_copy(out=out, in_=in_), reads, writes)

    def mm(self, out, lhsT, rhs, start, stop, reads, writes):
        self.mk.op("pe", lambda e: e.matmul(out, lhsT=lhsT, rhs=rhs, start=start, stop=stop), reads, writes)

    def load_T(self, dst, dram_t, offset, n, wkey):
        i = self.stn % 2
        self.stn += 1
        stg, sk = self.stg[i], "stg%d" % i
        self.mk.dma(stg[0:n, :], rap(dram_t, offset, [[128, n], [1, 128]]), writes=[sk])
        pst, psk = self.psum()
        self.transpose(pst[:, 0:n], stg[0:n, :], self.ident[0:n, 0:n], reads=[sk, "ident"], writes=[psk])
        self.copy("dve", dst, pst[:, 0:n], reads=[psk], writes=[wkey])

    def aalloc(self, shape, dt=F32):
        n = 1
        for x in shape[1:]:
            n *= x
        words = n if dt == F32 else (n + 1) // 2
        words += words % 2
        off = self.atop
        self.atop += words
        assert self.atop <= self.asize, ("arena overflow", self.atop, self.asize)
        v = self.arena[:, off:off + words]
        if dt != F32:
            v = v.bitcast(dt)[:, 0:n]
        else:
            v = v[:, 0:n]
        if len(shape) > 2:
            names = ["d%d" % i for i in range(len(shape) - 1)]
            kw = {names[i]: shape[i + 1] for i in range(1, len(names))}
            v = v.rearrange("p (" + " ".join(names) + ") -> p " + " ".join(names), **kw)
        return v

    def transpose(self, out, in_, ident, reads, writes):
        self.mk.op("pe", lambda e: e.transpose(out, in_, ident), reads, writes)


ATTN_SCALE = 64 ** -0.5
LAM_INIT1 = 0.8 - 0.6 * math.exp(-0.3 * 1)
CD_IN = 2304
AB_IN = 2576
ASIZE = 28160


def build(nc):
    P = Prog(nc)
    mk = P.mk
    xin = P.din("xin", [2, T, D])
    cvec = P.din("cvec", [2, D])
    w_mod = P.din("w_mod", [2, D, 6 * D])
    b_mod = P.din("b_mod", [2, 6 * D])
    norm1_g = P.din("norm1_g", [2, D])
    norm2_g = P.din("norm2_g", [2, D])
    ffn_up = P.din("ffn_up", [2, D, 2 * DFF])
    ffn_conv_w = P.din("ffn_conv_w", [2, 3, DFF])
    ffn_conv_b = P.din("ffn_conv_b", [2, DFF])
    ffn_down = P.din("ffn_down", [2, DFF, D])
    w_in_cd = P.din("w_in_cd", [D, CD_IN])
    w_out_cd = P.din("w_out_cd", [D, D])
    w_out_ab = P.din("w_out_ab", [D, D])
    gcat = P.din("gcat", [4, 64])
    c_sink = P.din("c_sink", [8])
    d_subln = P.din("d_subln", [128])
    lqk = P.din("lqk", [4, 64])
    ccat = P.din("ccat", [512, 1280])
    c_ident = P.din("c_ident", [128, 128])
    w_in_ab = P.din("w_in_ab", [D, AB_IN])
    gdn_conv_w = P.din("gdn_conv_w", [3, 1536])
    gdn_a_log = P.din("gdn_a_log", [8])
    gdn_dt_bias = P.din("gdn_dt_bias", [8])
    gdn_norm_g = P.din("gdn_norm_g", [128])
    gstate = P.din("gstate", [2, 4, 128, 128])
    c_tri = P.din("c_tri", [6, 128, 128])
    s5_lam_re = P.din("s5_lam_re", [64, 64])
    s5_lam_im = P.din("s5_lam_im", [64, 64])
    s5_log_dt = P.din("s5_log_dt", [64])
    s5_b_re = P.din("s5_b_re", [64, 64, 16])
    s5_b_im = P.din("s5_b_im", [64, 64, 16])
    s5_c_re = P.din("s5_c_re", [64, 16, 64])
    s5_c_im = P.din("s5_c_im", [64, 16, 64])
    s5_d = P.din("s5_d", [512])
    s5_w_glu = P.din("s5_w_glu", [512, 512])
    s5_b_glu = P.din("s5_b_glu", [512])
    s5h_re = P.din("s5h_re", [64, 64])
    s5h_im = P.din("s5h_im", [64, 64])
    c_kk = P.din("c_kk", [128, 8])
    c_arow = P.din("c_arow", [128, 33])
    c_mfb = P.din("c_mfb", [2, 128, 128])
    c_sel = P.din("c_sel", [4, 128, 128])
    c_rope = P.din("c_rope", [4, T, 16])
    c_wmask = P.din("c_wmask", [128, 384])
    yout = P.dout("yout", [2, T, D])
    s5scr = nc.dram_tensor("s5scr", [2, 128, 7744], F32, kind="Internal")
    o_ck = P.dout("o_ck", [T, 128])
    o_cv = P.dout("o_cv", [T, 128])
    o_dk = P.dout("o_dk", [T, 512])
    o_dv = P.dout("o_dv", [T, 512])
    o_gdn = P.dout("o_gdn", [4, 2, 4, 128, 128])
    o_s5r = P.dout("o_s5r", [4, 2, 32, 64])
    o_s5i = P.dout("o_s5i", [4, 2, 32, 64])

    P.ps = [P.es.enter_context(nc.psum_tensor("ps%d" % k, [128, 512], F32)) for k in range(8)]
    ident = P.sb("ident", [128, 128])
    identb = P.sb("identb", [128, 128], BF16)
    P.ident = ident
    P.stg = [P.sb("stg%d" % i, [128, 128]) for i in range(2)]
    onesD = P.sb("onesD", [128, 128])
    epsc = P.sb("epsc", [128, 1])
    xT = P.sb("xT", [128, 8, T])
    hT = P.sb("hT", [128, 8, T], BF16)
    mixT = P.sb("mixT", [128, 8, T], BF16)
    wbuf = [P.sb("wbuf%d" % i, [128, 8, 512], BF16) for i in range(2)]
    scv = P.sb("scv", [128, 8, 2])
    modv = P.sb("modv", [128, 2, 48, 2])
    bmod = P.sb("bmod", [128, 2, 48])
    ng = P.sb("ng", [128, 2, 2, 8])
    gs = P.sb("gs", [128, 2, 2, 2, 8])
    cw = P.sb("cw", [128, 2, 3, NJ])
    cb = P.sb("cb", [128, 2, NJ])
    sq = [P.sb("sq%d" % i, [128, 512]) for i in range(2)]
    rstd = P.sb("rstd", [128, 512])
    tmpn = [P.sb("tmpn%d" % i, [128, 512]) for i in range(2)]
    P.arena = P.sb("arena", [128, ASIZE])
    P.asize = ASIZE

    mk.dma(ident[:], c_ident.ap(), writes=["ident"])
    mk.op("dve", lambda e: e.tensor_copy(out=identb[:], in_=ident[:]), reads=["ident"], writes=["identb"])
    mk.op("pool", lambda e: e.memset(onesD[:], 1.0 / D), writes=["onesD"])
    mk.op("pool", lambda e: e.memset(epsc[:], EPS), writes=["epsc"])
    for v in range(2):
        P.load_T(scv[:, :, v], cvec, v * D, 8, ("scv", v))
    for l in range(2):
        P.load_T(bmod[:, l, :], b_mod, l * 6 * D, 48, ("bmod", l))
        P.load_T(ng[:, 0, l, :], norm1_g, l * D, 8, ("ng", 0, l))
        P.load_T(ng[:, 1, l, :], norm2_g, l * D, 8, ("ng", 1, l))
        for tap in range(3):
            P.load_T(cw[:, l, tap, :], ffn_conv_w, (l * 3 + tap) * DFF, NJ, ("cw", l, tap))
        P.load_T(cb[:, l, :], ffn_conv_b, l * DFF, NJ, ("cb", l))
    mk.op("act", lambda e: e.activation(out=scv[:], in_=scv[:], func=AF.Silu), reads=["scv"], writes=["scv"])

    wmb = [P.aalloc([128, 8, 512], BF16) for _ in range(2)]
    scvb = P.aalloc([128, 8, 2], BF16)
    adaln_top = P.atop
    mk.op("dve", lambda e: e.tensor_copy(out=scvb, in_=scv[:]), reads=["scv"], writes=["scvb"])

    def adaln_gen():
        wn = 0
        for l in range(2):
            pst, psk = P.ps[7], "ps7"
            for jg in range(12):
                wb = wmb[wn % 2]
                wk = "wmb%d" % (wn % 2)
                wn += 1
                mk.dma(wb, rap(w_mod, l * D * 6 * D + jg * 512, [[6 * D, 128], [128 * 6 * D, 8], [1, 512]]), writes=[wk], q="pool")
                for jj in range(4):
                    j = jg * 4 + jj
                    for kc in range(8):
                        P.mm(pst[:, 2 * j:2 * j + 2], wb[:, kc, jj * 128:(jj + 1) * 128], scvb[:, kc, :],
                             kc == 0, kc == 7, reads=[wk, "scvb"], writes=[psk])
                yield
            mk.op("dve", lambda e, l=l, pst=pst: e.tensor_tensor(
                out=modv[:, l, :, :], in0=pst[:, 0:96].rearrange("p (j v) -> p j v", v=2),
                in1=rap(bmod, l * 48, [[96, 128], [1, 48], [0, 2]]), op=ALU.add),
                reads=[psk, ("bmod", l)], writes=[("modv", l)])
            yield
    def adaln_finish():
        for l in range(2):
            for which in range(2):
                for v in range(2):
                    sc_ap = rap(modv, (l * 48 + (1 + 3 * which) * 8) * 2 + v, [[192, 128], [2, 8]])
                    mk.op("dve", lambda e, l=l, which=which, v=v, sc_ap=sc_ap: e.scalar_tensor_tensor(
                        out=gs[:, l, which, v, :], in0=sc_ap, scalar=1.0, in1=ng[:, which, l, :],
                        op0=ALU.add, op1=ALU.mult),
                        reads=[("modv", l), ("ng", which, l)], writes=[("gs", l, which, v)])
        mk.fence()
        P.atop = 0

    def mod_col(l, split, v, c):
        return rap(modv, (l * 48 + split * 8 + c) * 2 + v, [[192, 128], [1, 1]])

    state = {"w": 0, "wd": 0, "sq": 0, "tn": 0}

    def load_w(dram_t, base_off, rowlen, ncols):
        i = state["w"] % 2
        state["w"] += 1
        mk.dma(wbuf[i][:, :, 0:ncols], rap(dram_t, base_off, [[rowlen, 128], [128 * rowlen, 8], [1, ncols]]),
               writes=["wbuf%d" % i], q="pool")
        return wbuf[i], "wbuf%d" % i

    def norm_mod(l, which, v):
        for th in range(2):
            ts = slice(th * 512, (th + 1) * 512)
            pst, psk = P.psum()
            for c in range(8):
                s = state["sq"] % 2
                state["sq"] += 1
                mk.op("act", lambda e, s=s, c=c, ts=ts: e.activation(out=sq[s][:], in_=xT[:, c, ts], func=AF.Square),
                      reads=[("xT", th)], writes=["sq%d" % s])
                P.mm(pst[:], onesD[:], sq[s][:], c == 0, c == 7, reads=["onesD", "sq%d" % s], writes=[psk])
            mk.op("act", lambda e, pst=pst: e.activation(out=rstd[:], in_=pst[:], func=AF.Ln, bias=epsc[:, 0:1]),
                  reads=[psk, "epsc"], writes=["rstd"])
            mk.op("act", lambda e: e.activation(out=rstd[:], in_=rstd[:], func=AF.Exp, scale=-0.5), reads=["rstd"], writes=["rstd"])
            for c in range(8):
                s = state["tn"] % 2
                state["tn"] += 1
                mk.op("dve", lambda e, s=s, c=c, ts=ts: e.tensor_tensor(out=tmpn[s][:], in0=xT[:, c, ts], in1=rstd[:], op=ALU.mult),
                      reads=[("xT", th), "rstd"], writes=["tmpn%d" % s])
                g_ap = rap(gs, (((l * 2 + which) * 2 + v) * 8 + c), [[64, 128], [1, 1]])
                sh_ap = mod_col(l, 3 * which, v, c)
                mk.op("act", lambda e, s=s, c=c, ts=ts, g_ap=g_ap, sh_ap=sh_ap: e.activation(
                    out=hT[:, c, ts], in_=tmpn[s][:], func=AF.Identity, scale=g_ap, bias=sh_ap),
                    reads=["tmpn%d" % s, ("gs", l, which, v), ("modv", l)], writes=[("hT", th)])

    pre_out = [None]

    def out_proj(w_dram, l, v):
        for mg in range(2):
            if mg == 0 and pre_out[0] is not None:
                w, wk = pre_out[0]
                pre_out[0] = None
            else:
                w, wk = load_w(w_dram, mg * 512, D, 512)
            for mm_ in range(4):
                m = mg * 4 + mm_
                for th in range(2):
                    ts = slice(th * 512, (th + 1) * 512)
                    po, pok = P.psum()
                    for kc in range(8):
                        P.mm(po[:], w[:, kc, mm_ * 128:(mm_ + 1) * 128], mixT[:, kc, ts], kc == 0, kc == 7,
                             reads=[wk, "mixT"], writes=[pok])
                    gate = mod_col(l, 2, v, m)
                    mk.op("dve", lambda e, po=po, gate=gate, m=m, ts=ts: e.scalar_tensor_tensor(
                        out=xT[:, m, ts], in0=po[:], scalar=gate, in1=xT[:, m, ts], op0=ALU.mult, op1=ALU.add),
                        reads=[pok, ("modv", l), ("xT", th)], writes=[("xT", th)])

    def ffn(l, v, seqlen):
        P.atop = 0
        hid = P.aalloc([128, NJ, T], BF16)
        wdn = [P.aalloc([128, NJ, 256], BF16) for _ in range(2)]
        abuf = [P.aalloc([128, T]) for _ in range(2)]
        cbuf = [P.aalloc([128, T]) for _ in range(2)]
        wx = [P.aalloc([128, 8, 512], BF16) for _ in range(2)]
        ring = [(wbuf[0], "wbuf0"), (wbuf[1], "wbuf1"), (wx[0], "ffn_wx0"), (wx[1], "ffn_wx1")]
        rstate = {"n": 0}

        def load_up(jg):
            nj = 4 if jg < 5 else 2
            res = []
            for part in range(2):
                w_, wk_ = ring[rstate["n"] % 4]
                rstate["n"] += 1
                mk.dma(w_[:, :, 0:nj * 128], rap(ffn_up, l * D * 2 * DFF + part * DFF + jg * 512, [[2 * DFF, 128], [128 * 2 * DFF, 8], [1, nj * 128]]),
                       writes=[wk_], q="pool")
                res.append((w_, wk_))
            return res

        def load_dn(mg):
            i = state["wd"] % 2
            state["wd"] += 1
            wd, wdk = wdn[i], "wdn%d" % i
            for jh in range(2):
                mk.dma(wd[:, jh * 11:(jh + 1) * 11, :],
                       rap(ffn_down, l * DFF * D + jh * 11 * 128 * D + mg * 256, [[D, 128], [128 * D, 11], [1, 256]]),
                       writes=[(wdk, jh)], q="pool")
            return wd, wdk

        pend = [load_up(0), load_up(1)]
        dn_pend = [load_dn(0)]
        for jg in range(6):
            nj = 4 if jg < 5 else 2
            (wa, wak), (wb_, wbk) = pend.pop(0)
            if jg == 1:
                dn_pend.append(load_dn(1))
            for jj in range(nj):
                j = jg * 4 + jj
                s = j % 2
                ab, abk = abuf[s], "abuf%d" % s
                cbf, cbk = cbuf[s], "cbuf%d" % s
                pbs = []
                for th in range(2):
                    ts = slice(th * 512, (th + 1) * 512)
                    pa, pak = P.psum()
                    pb, pbk = P.psum()
                    pbs.append((pb, pbk))
                    for kc in range(8):
                        P.mm(pa[:], wa[:, kc, jj * 128:(jj + 1) * 128], hT[:, kc, ts], kc == 0, kc == 7,
                             reads=[wak, ("hT", th)], writes=[pak])
                    for kc in range(8):
                        P.mm(pb[:], wb_[:, kc, jj * 128:(jj + 1) * 128], hT[:, kc, ts], kc == 0, kc == 7,
                             reads=[wbk, ("hT", th)], writes=[pbk])
                    mk.op("act", lambda e, ab=ab, pa=pa, th=th: e.activation(
                        out=ab[:, th * 512:(th + 1) * 512], in_=pa[:], func=AF.Copy),
                        reads=[pak], writes=[(abk, th)])
                w0 = rap(cw, (l * 3 + 0) * NJ + j, [[2 * 3 * NJ, 128], [1, 1]])
                w1 = rap(cw, (l * 3 + 1) * NJ + j, [[2 * 3 * NJ, 128], [1, 1]])
                w2 = rap(cw, (l * 3 + 2) * NJ + j, [[2 * 3 * NJ, 128], [1, 1]])
                bb = rap(cb, l * NJ + j, [[2 * NJ, 128], [1, 1]])
                mk.op("dve", lambda e, cbf=cbf, ab=ab, w1=w1, bb=bb: e.tensor_scalar(
                    out=cbf, in0=ab, scalar1=w1, scalar2=bb, op0=ALU.mult, op1=ALU.add),
                    reads=[abk, ("cw", l, 1), ("cb", l)], writes=[cbk])
                c3 = cbf.rearrange("p (s t) -> p s t", t=seqlen)
                a3 = ab.rearrange("p (s t) -> p s t", t=seqlen)
                mk.op("dve", lambda e, c3=c3, a3=a3, w0=w0: e.scalar_tensor_tensor(
                    out=c3[:, :, 1:seqlen], in0=a3[:, :, 0:seqlen - 1], scalar=w0, in1=c3[:, :, 1:seqlen],
                    op0=ALU.mult, op1=ALU.add), reads=[abk, cbk, ("cw", l, 0)], writes=[cbk])
                mk.op("dve", lambda e, c3=c3, a3=a3, w2=w2: e.scalar_tensor_tensor(
                    out=c3[:, :, 0:seqlen - 1], in0=a3[:, :, 1:seqlen], scalar=w2, in1=c3[:, :, 0:seqlen - 1],
                    op0=ALU.mult, op1=ALU.add), reads=[abk, cbk, ("cw", l, 2)], writes=[cbk])
                mk.op("act", lambda e, cbf=cbf: e.activation(out=cbf, in_=cbf, func=AF.Silu),
                      reads=[cbk], writes=[cbk])
                for th in range(2):
                    ts = slice(th * 512, (th + 1) * 512)
                    pb, pbk = pbs[th]
                    mk.op("dve", lambda e, cbf=cbf, pb=pb, j=j, ts=ts: e.tensor_tensor(
                        out=hid[:, j, ts], in0=cbf[:, ts], in1=pb[:], op=ALU.mult),
                        reads=[cbk, pbk], writes=[("hid", j, th)])
            if jg + 2 < 6:
                pend.append(load_up(jg + 2))
        for mg in range(4):
            wd, wdk = dn_pend.pop(0)
            for mm_ in range(2):
                m = mg * 2 + mm_
                for th in range(2):
                    ts = slice(th * 512, (th + 1) * 512)
                    po, pok = P.psum()
                    for j in range(NJ):
                        P.mm(po[:], wd[:, j, mm_ * 128:(mm_ + 1) * 128], hid[:, j, ts], j == 0, j == NJ - 1,
                             reads=[(wdk, j // 11), ("hid", j, th)], writes=[pok])
                    gate = mod_col(l, 5, v, m)
                    mk.op("dve", lambda e, po=po, gate=gate, m=m, ts=ts: e.scalar_tensor_tensor(
                        out=xT[:, m, ts], in0=po[:], scalar=gate, in1=xT[:, m, ts], op0=ALU.mult, op1=ALU.add),
                        reads=[pok, ("modv", l), ("xT", th)], writes=[("xT", th)])
            if mg + 2 < 4:
                dn_pend.append(load_dn(mg + 2))
        mk.fence()
        P.atop = 0

    def attn_layer(v):
        l = 1
        P.atop = 0
        koff = 0 if v == 0 else 4
        NKT = 8 + koff
        NK = NKT * 128
        qcT = P.aalloc([128, 4, T], BF16)
        kcT2 = P.aalloc([128, 2, NK], BF16)
        qdT = P.aalloc([128, 4, T], BF16)
        kdT = P.aalloc([128, 4, NK], BF16)
        vc_b = P.aalloc([128, NKT, 2, 65], BF16)
        vd_b = P.aalloc([128, NKT, 4, 129], BF16)
        gain_t = P.aalloc([128, 4, 64])
        sink_t = P.aalloc([128, 8])
        dsubg = P.aalloc([128, 128])
        lqk_t = P.aalloc([128, 4, 64])
        lstat = P.aalloc([128, 8])
        rope_t = P.aalloc([128, 4, 8, 16])
        wmask = P.aalloc([128, 384])
        qn = P.aalloc([128, 8, 16])
        nbq = P.aalloc([128, 8, 16])
        kmx = P.aalloc([128, 128])
        ksq = P.aalloc([128, 128])
        kcol = P.aalloc([128, 2])
        mark = P.atop
        wcd = P.aalloc([128, 8, 1280], BF16)
        pjb = [P.aalloc([128, CD_IN]) for _ in range(2)]
        sqt = P.aalloc([128, 1024])
        ms26 = P.aalloc([128, 32])
        rtmp = [P.aalloc([128, 16, 16]) for _ in range(4)]
        A = "at%d_" % v

        mk.dma(gain_t, rap(gcat, 0, [[0, 128], [64, 4], [1, 64]]), writes=[A + "gain"])
        mk.dma(sink_t, rap(c_sink, 0, [[0, 128], [1, 8]]), writes=[A + "sink"])
        mk.dma(dsubg, rap(d_subln, 0, [[0, 128], [1, 128]]), writes=[A + "dsubg"])
        mk.dma(lqk_t, rap(lqk, 0, [[0, 128], [64, 4], [1, 64]]), writes=[A + "lqk"])
        mk.dma(wmask, c_wmask.ap(), writes=[A + "wmask"])
        if v == 1:
            for i in range(4):
                mk.dma(rope_t[:, i, :, :], rap(c_rope, i * T * 16, [[16, 128], [128 * 16, 8], [1, 16]]),
                       writes=[(A + "rope", i)])
        mk.op("pool", lambda e: e.tensor_scalar(out=dsubg, in0=dsubg, scalar1=1.0 - LAM_INIT1, scalar2=None, op0=ALU.mult),
              reads=[A + "dsubg"], writes=[A + "dsubg"])
        for i in range(2):
            mk.op("dve", lambda e, i=i: e.tensor_tensor(out=lqk_t[:, 2 * i, :], in0=lqk_t[:, 2 * i, :], in1=lqk_t[:, 2 * i + 1, :], op=ALU.mult),
                  reads=[A + "lqk"], writes=[A + "lqk"])
            mk.op("dve", lambda e, i=i: e.tensor_reduce(out=lstat[:, i:i + 1], in_=lqk_t[:, 2 * i, :], axis=AX.X, op=ALU.add),
                  reads=[A + "lqk"], writes=[A + "lstat"])
        mk.op("act", lambda e: e.activation(out=lstat[:, 0:2], in_=lstat[:, 0:2], func=AF.Exp), reads=[A + "lstat"], writes=[A + "lstat"])
        mk.op("dve", lambda e: e.tensor_tensor(out=lstat[:, 2:3], in0=lstat[:, 1:2], in1=lstat[:, 0:1], op=ALU.subtract),
              reads=[A + "lstat"], writes=[A + "lstat"])
        mk.op("dve", lambda e: e.tensor_scalar(out=lstat[:, 4:5], in0=lstat[:, 2:3], scalar1=-LAM_INIT1, scalar2=None, op0=ALU.add),
              reads=[A + "lstat"], writes=[A + "lstat"])
        neg_lam = lstat[:, 4:5]

        wA, wAk = load_w(w_in_cd, 0, CD_IN, 512)
        wB, wBk = load_w(w_in_cd, 512, CD_IN, 512)
        mk.dma(wcd, rap(w_in_cd, 1024, [[CD_IN, 128], [128 * CD_IN, 8], [1, 1280]]), writes=[A + "wcd"], q="pool")
        groups = [(wA, wAk, 0, 0, 512), (wB, wBk, 0, 512, 512), (wcd, A + "wcd", 0, 1024, 512),
                  (wcd, A + "wcd", 512, 1536, 512), (wcd, A + "wcd", 1024, 2048, 256)]

        kfirst = [True]
        mk.op("dve", lambda e: e.memset(kmx, 0.0), writes=[A + "kmx"])

        def build_operands(pj, pjk, kt, tt, with_q):
            mk.op("act", lambda e: e.activation(out=sqt[:, 0:128], in_=pj[:, 512:640], func=AF.Square), reads=[pjk], writes=[A + "sqt"])
            mk.op("act", lambda e: e.activation(out=sqt[:, 128:640], in_=pj[:, 1280:1792], func=AF.Square), reads=[pjk], writes=[A + "sqt"])
            mk.op("dve", lambda e: e.tensor_reduce(out=ksq[:, 0:10], in_=sqt[:, 0:640].rearrange("p (h d) -> p h d", d=64), axis=AX.X, op=ALU.add),
                  reads=[A + "sqt"], writes=[A + "ksq"])
            mk.op("dve", lambda e: e.tensor_tensor(out=kmx[:, 0:10], in0=kmx[:, 0:10], in1=ksq[:, 0:10], op=ALU.max), reads=[A + "kmx", A + "ksq"], writes=[A + "kmx"])
            if with_q:
                mk.op("act", lambda e: e.activation(out=sqt[:, 0:512], in_=pj[:, 0:512], func=AF.Square), reads=[pjk], writes=[A + "sqt"])
                mk.op("act", lambda e: e.activation(out=sqt[:, 512:1024], in_=pj[:, 768:1280], func=AF.Square), reads=[pjk], writes=[A + "sqt"])
                mk.op("dve", lambda e: e.tensor_reduce(out=qn[:, tt, :], in_=sqt[:, 0:1024].rearrange("p (h d) -> p h d", d=64), axis=AX.X, op=ALU.add),
                      reads=[A + "sqt"], writes=[A + "qn"])
            jobs = []
            if with_q:
                jobs.append((qcT, [pj[:, j * 128:(j + 1) * 128] for j in range(4)], tt, A + "qcT"))
                jobs.append((qdT, [pj[:, 768 + j * 128:768 + (j + 1) * 128] for j in range(4)], tt, A + "qdT"))
            pst, psk = P.psum()
            for g in range(2):
                for dd in range(2):
                    P.mm(pst[dd * 64:(dd + 1) * 64, g * 128:(g + 1) * 128], pj[:, 512 + g * 64:512 + (g + 1) * 64],
                         ident[:], True, True, reads=[pjk, "ident"], writes=[psk])
            P.copy(P.evac_eng(), kcT2[:, 0:2, kt * 128:(kt + 1) * 128],
                   pst[:, 0:256].rearrange("p (c t) -> p c t", t=128), reads=[psk], writes=[(A + "kcT2", kt)])
            jobs.append((kdT, [pj[:, 1280 + j * 128:1280 + (j + 1) * 128] for j in range(4)], kt, A + "kdT"))
            for dst, srcs, col, dk in jobs:
                pst, psk = P.psum()
                n = len(srcs)
                for j, sap in enumerate(srcs):
                    P.transpose(pst[:, j * 128:(j + 1) * 128], sap, ident[:], reads=[pjk, "ident"], writes=[psk])
                P.copy(P.evac_eng(), dst[:, 0:n, col * 128:(col + 1) * 128],
                       pst[:, 0:n * 128].rearrange("p (c t) -> p c t", t=128), reads=[psk], writes=[(dk, col)])
            mk.op("pool", lambda e: e.memset(vc_b[:, kt, :, 64:65], 1.0), writes=[(A + "vc_b", kt)])
            mk.op("pool", lambda e: e.memset(vd_b[:, kt, :, 128:129], 1.0), writes=[(A + "vd_b", kt)])
            mk.op("pool", lambda e: e.tensor_copy(out=vc_b[:, kt, :, 0:64], in_=pj[:, 640:768].rearrange("p (g d) -> p g d", d=64)), reads=[pjk], writes=[(A + "vc_b", kt)])
            mk.op("pool", lambda e: e.tensor_copy(out=vd_b[:, kt, :, 0:128], in_=pj[:, 1792:2304].rearrange("p (g d) -> p g d", d=128)), reads=[pjk], writes=[(A + "vd_b", kt)])

        pjn = 0
        if v == 1:
            for ct in range(4):
                pj, pjk = pjb[pjn % 2], A + "pj%d" % (pjn % 2)
                pjn += 1
                r0 = ct * 128
                mk.dma(pj[:, 512:768], rap(ccat, r0 * 1280, [[1280, 128], [1, 256]]), writes=[pjk])
                mk.dma(pj[:, 1280:2304], rap(ccat, r0 * 1280 + 256, [[1280, 128], [1, 1024]]), writes=[pjk])
                build_operands(pj, pjk, ct, None, False)
        for tt in range(8):
            pj, pjk = pjb[pjn % 2], A + "pj%d" % (pjn % 2)
            pjn += 1
            for (w, wk, wc0, c0, n) in groups:
                pst, psk = P.psum()
                for kc in range(8):
                    P.mm(pst[:, 0:n], hT[:, kc, tt * 128:(tt + 1) * 128], w[:, kc, wc0:wc0 + n], kc == 0, kc == 7,
                         reads=[wk, ("hT", tt // 4)], writes=[psk])
                P.copy(P.evac_eng(), pj[:, c0:c0 + n], pst[:, 0:n], reads=[psk], writes=[pjk])
            for (c0, n, h0, g0) in ((0, 640, 0, 0), (768, 1024, 10, 640)):
                H = n // 64
                mk.op("act", lambda e, c0=c0, n=n, pj=pj: e.activation(out=sqt[:, 0:n], in_=pj[:, c0:c0 + n], func=AF.Square),
                      reads=[pjk], writes=[A + "sqt"])
                mk.op("dve", lambda e, n=n, h0=h0, H=H: e.tensor_reduce(
                    out=ms26[:, h0:h0 + H], in_=sqt[:, 0:n].rearrange("p (h d) -> p h d", d=64), axis=AX.X, op=ALU.add),
                    reads=[A + "sqt"], writes=[A + "ms26"])
            mk.op("act", lambda e: e.activation(out=ms26[:, 0:26], in_=ms26[:, 0:26], func=AF.Sqrt, scale=1.0 / 64, bias=EPS),
                  reads=[A + "ms26"], writes=[A + "ms26"])
            mk.op("dve", lambda e: e.reciprocal(out=ms26[:, 0:26], in_=ms26[:, 0:26]), reads=[A + "ms26"], writes=[A + "ms26"])
            for (c0, n, h0, g0) in ((0, 640, 0, 0), (768, 1024, 10, 640)):
                H = n // 64
                bc = rap(P.arena, ms26.offset + h0, [[ASIZE, 128], [1, H], [0, 64]])
                mk.op("dve", lambda e, c0=c0, n=n, pj=pj, bc=bc: e.tensor_tensor(
                    out=pj[:, c0:c0 + n].rearrange("p (h d) -> p h d", d=64),
                    in0=pj[:, c0:c0 + n].rearrange("p (h d) -> p h d", d=64), in1=bc, op=ALU.mult),
                    reads=[pjk, A + "ms26"], writes=[pjk])
            for (c0, H, gi) in ((0, 8, 0), (512, 2, 1), (768, 8, 2), (1280, 8, 3)):
                gb = rap(P.arena, gain_t.offset + gi * 64, [[ASIZE, 128], [0, H], [1, 64]])
                mk.op("pool", lambda e, c0=c0, H=H, gb=gb, pj=pj: e.tensor_tensor(
                    out=pj[:, c0:c0 + H * 64].rearrange("p (h d) -> p h d", d=64),
                    in0=pj[:, c0:c0 + H * 64].rearrange("p (h d) -> p h d", d=64), in1=gb, op=ALU.mult),
                    reads=[pjk, A + "gain"], writes=[pjk])
            if v == 0:
                r = slice(tt * 128, (tt + 1) * 128)
                for (dst, c0, n, nm) in ((o_ck, 512, 128, "o_ck"), (o_cv, 640, 128, "o_cv"),
                                         (o_dk, 1280, 512, "o_dk"), (o_dv, 1792, 512, "o_dv")):
                    ok = (nm, tt)
                    mk.dma(dst.ap()[r, :], pj[:, c0:c0 + n], reads=[pjk], writes=[ok])
                    P.outs.append(ok)
            else:
                for (c0, H) in ((0, 10), (768, 16)):
                    for half in range(2):
                        cs = rap(P.arena, rope_t.offset + ((2 * half) * 8 + tt) * 16, [[ASIZE, 128], [0, H], [1, 16]])
                        sn = rap(P.arena, rope_t.offset + ((2 * half + 1) * 8 + tt) * 16, [[ASIZE, 128], [0, H], [1, 16]])
                        x1 = rap(P.arena, pj.offset + c0 + half * 32, [[ASIZE, 128], [64, H], [1, 16]])
                        x2 = rap(P.arena, pj.offset + c0 + half * 32 + 16, [[ASIZE, 128], [64, H], [1, 16]])
                        t = [rt[:, 0:H, :] for rt in rtmp]
                        rk = [A + "rtmp%d" % i for i in range(4)]
                        rp = [(A + "rope", 2 * half), (A + "rope", 2 * half + 1)]
                        mk.op("dve", lambda e, t=t, x1=x1, cs=cs: e.tensor_tensor(out=t[0], in0=x1, in1=cs, op=ALU.mult), reads=[pjk, rp[0]], writes=[rk[0]])
                        mk.op("pool", lambda e, t=t, x2=x2, sn=sn: e.tensor_tensor(out=t[1], in0=x2, in1=sn, op=ALU.mult), reads=[pjk, rp[1]], writes=[rk[1]])
                        mk.op("dve", lambda e, t=t, x2=x2, cs=cs: e.tensor_tensor(out=t[2], in0=x2, in1=cs, op=ALU.mult), reads=[pjk, rp[0]], writes=[rk[2]])
                        mk.op("pool", lambda e, t=t, x1=x1, sn=sn: e.tensor_tensor(out=t[3], in0=x1, in1=sn, op=ALU.mult), reads=[pjk, rp[1]], writes=[rk[3]])
                        mk.op("dve", lambda e, t=t, x1=x1: e.tensor_tensor(out=x1, in0=t[0], in1=t[1], op=ALU.subtract), reads=[rk[0], rk[1]], writes=[pjk])
                        mk.op("dve", lambda e, t=t, x2=x2: e.tensor_tensor(out=x2, in0=t[2], in1=t[3], op=ALU.add), reads=[rk[2], rk[3]], writes=[pjk])
            build_operands(pj, pjk, koff + tt, tt, True)
        pst, psk = P.psum()
        P.transpose(pst[:, 0:128], kmx, ident[:], reads=[A + "kmx", "ident"], writes=[psk])
        mk.op("dve", lambda e, pst=pst: e.tensor_reduce(out=kcol[:, 0:1], in_=pst[:, 0:128], axis=AX.X, op=ALU.max), reads=[psk], writes=[A + "kcol"])
        mk.op("dve", lambda e: e.tensor_scalar(out=ksq, in0=ident[:], scalar1=kcol[:, 0:1], scalar2=None, op0=ALU.mult),
              reads=[A + "kcol", "ident"], writes=[A + "ksq"])
        pst2, ps2k = P.psum()
        P.mm(pst2[:, 0:128], onesD[:], ksq, True, True, reads=["onesD", A + "ksq"], writes=[ps2k])
        mk.op("act", lambda e, pst2=pst2: e.activation(out=kmx, in_=pst2[:, 0:128], func=AF.Sqrt, scale=float(D)), reads=[ps2k], writes=[A + "kmx"])
        mk.op("act", lambda e: e.activation(out=qn, in_=qn, func=AF.Sqrt), reads=[A + "qn"], writes=[A + "qn"])
        mk.op("dve", lambda e: e.memset(ksq, 0.0), reads=[A + "ksq"], writes=[A + "ksq"])
        mk.op("dve", lambda e: e.tensor_reduce(out=ksq[:, 0:16], in_=qn.rearrange("p t h -> p h t"), axis=AX.X, op=ALU.max), reads=[A + "qn", A + "ksq"], writes=[A + "ksq"])
        pst3, ps3k = P.psum()
        P.transpose(pst3[:, 0:128], ksq, ident[:], reads=[A + "ksq", "ident"], writes=[ps3k])
        mk.op("dve", lambda e, pst3=pst3: e.tensor_reduce(out=kcol[:, 1:2], in_=pst3[:, 0:128], axis=AX.X, op=ALU.max), reads=[ps3k], writes=[A + "kcol"])
        mk.op("dve", lambda e: e.tensor_scalar(out=ksq, in0=ident[:], scalar1=kcol[:, 1:2], scalar2=None, op0=ALU.mult),
              reads=[A + "kcol", "ident"], writes=[A + "ksq"])
        pst4, ps4k = P.psum()
        P.mm(pst4[:, 0:128], onesD[:], ksq, True, True, reads=["onesD", A + "ksq"], writes=[ps4k])
        nbh = nbq[:, 0, :]
        for g in range(2):
            mk.op("dve", lambda e, g=g, pst4=pst4: e.tensor_scalar(out=nbh[:, g * 4:(g + 1) * 4], in0=pst4[:, g * 4:(g + 1) * 4], scalar1=kmx[:, g:g + 1],
                                                                 scalar2=-ATTN_SCALE * float(D), op0=ALU.mult, op1=ALU.mult), reads=[ps4k, A + "kmx"], writes=[A + "nbq"])
        for j in range(8):
            mk.op("dve", lambda e, j=j, pst4=pst4: e.tensor_scalar(out=nbh[:, 8 + j:9 + j], in0=pst4[:, 8 + j:9 + j], scalar1=kmx[:, 2 + j:3 + j],
                                                                 scalar2=-ATTN_SCALE * float(D), op0=ALU.mult, op1=ALU.mult), reads=[ps4k, A + "kmx"], writes=[A + "nbq"])
        mk.fence()
        P.atop = mark
        pre_out[0] = load_w(w_out_cd, 0, D, 512)
        NPT = 7
        PT = [P.aalloc([128, 12, 128], BF16) for _ in range(NPT)]
        msb = [P.aalloc([128, 384]) for _ in range(2)]
        Otok = [P.aalloc([128, 512]) for _ in range(2)]
        Onb = [P.aalloc([128, 512]) for _ in range(2)]
        sqs = P.aalloc([128, 512])
        tmpD = [P.aalloc([128, 128]) for _ in range(2)]
        stat = P.aalloc([128, 64, 16])
        cnt = {"pn": 0, "pt": 0, "st": 0, "ms": 0, "td": 0}
        qsel_i = [0]
        freeb = [0, 1, 2, 3, 4]

        def runs_of(ktiles):
            runs = []
            for kt in ktiles:
                if runs and runs[-1][-1] + 1 == kt and len(runs[-1]) < 4:
                    runs[-1].append(kt)
                else:
                    runs.append([kt])
            return runs

        def row_gen(q_ap, kbase, bias_ap, rk, kn, ktiles, moff, pv_fn, done_ctr=None):
            runs = runs_of(ktiles)
            nkt = len(ktiles)
            sidx = cnt["st"] % 64
            cnt["st"] += 1
            stk = (A + "stat", sidx)
            ti = cnt["pt"] % NPT
            cnt["pt"] += 1
            pt_, ptk = PT[ti], A + "PT%d" % ti
            blk = 0
            for ri, run in enumerate(runs):
                while not freeb:
                    yield
                bk = freeb.pop(0)
                pst, psk = P.ps[bk], "ps%d" % bk
                n = len(run) * 128
                for j, kt in enumerate(run):
                    P.mm(pst[:, j * 128:(j + 1) * 128], kbase[:, kt * 128:(kt + 1) * 128], q_ap, True, True,
                         reads=rk + [(kn, kt)], writes=[psk])
                src, srck = pst[:, 0:n], psk
                if moff is not None and ri == 1:
                    mi = cnt["ms"] % 2
                    cnt["ms"] += 1
                    mb, mbk = msb[mi], A + "msb%d" % mi
                    mk.op("dve", lambda e, mb=mb, n=n, pst=pst: e.tensor_tensor(out=mb[:, 0:n], in0=pst[:, 0:n], in1=wmask[:, moff:moff + n], op=ALU.add),
                          reads=[psk, A + "wmask"], writes=[mbk])
                    src, srck = mb[:, 0:n], mbk
                dstv = pt_[:, blk:blk + len(run), :].rearrange("p c t -> p (c t)")
                mk.op("act", lambda e, src=src, dstv=dstv: e.activation(out=dstv, in_=src, func=AF.Exp, scale=ATTN_SCALE, bias=bias_ap),
                      reads=[srck, A + "nbq"], writes=[ptk])
                freeb.append(bk)
                blk += len(run)
                yield
            pv_fn(pt_, ptk, sidx, stk)
            if done_ctr is not None:
                done_ctr[0] += 1
            yield

        def pipeline(gens, depth):
            active = []
            gens = list(gens)
            while gens or active:
                if gens and len(active) < depth:
                    active.append(gens.pop(0))
                for g_ in list(active):
                    try:
                        next(g_)
                    except StopIteration:
                        active.remove(g_)

        def qtile_rows(qt, ktilesC, moff, ktilesD):
            qs = slice(qt * 128, (qt + 1) * 128)
            oi = qsel_i[0] % 2
            qsel_i[0] += 1
            ot, otk = Otok[oi], A + "Otok%d" % oi
            On, onkey = Onb[oi], A + "On%d" % oi
            done = [0]
            gens = []
            cpo = {}
            nktC = len(ktilesC)
            for hq in range(8):
                pb = (hq % 2) * 64
                g = hq // 4

                def pvC(pt_, ptk, sidx, stk, hq=hq, g=g):
                    bank = hq // 4
                    po, pok = P.ps[5 + bank], "ps%d" % (5 + bank)
                    cs = slice((hq % 4) * 65, (hq % 4) * 65 + 65)
                    for j, kt in enumerate(ktilesC):
                        P.mm(po[:, cs], pt_[:, j, :], vc_b[:, kt, g, :], j == 0, j == nktC - 1, reads=[ptk, (A + "vc_b", kt)], writes=[pok])
                    st = stat[:, sidx, :]
                    mk.op("act", lambda e: e.activation(out=st[:, 6:7], in_=nbq[:, 0, hq:hq + 1], func=AF.Exp, bias=sink_t[:, hq:hq + 1]), reads=[A + "nbq", A + "sink"], writes=[stk])
                    mk.op("dve", lambda e: e.tensor_tensor(out=st[:, 7:8], in0=po[:, cs][:, 64:65], in1=st[:, 6:7], op=ALU.add), reads=[pok, stk], writes=[stk])
                    mk.op("dve", lambda e: e.reciprocal(out=st[:, 7:8], in_=st[:, 7:8]), reads=[stk], writes=[stk])
                    mk.op("dve", lambda e: e.tensor_scalar(out=ot[:, hq * 64:(hq + 1) * 64], in0=po[:, cs][:, 0:64], scalar1=st[:, 7:8], scalar2=None, op0=ALU.mult),
                          reads=[pok, stk], writes=[(otk, hq)])
                gens.append(row_gen(qcT[pb:pb + 64, hq // 2, qs], kcT2[pb:pb + 64, g, :], nbq[:, 0, hq:hq + 1], [(A + "qcT", qt)], A + "kcT2", ktilesC, moff, pvC, done))
            nktD = len(ktilesD)
            dpo = {}
            for h in range(4):
                for c in range(2):
                    pb = c * 64

                    def pvD(pt_, ptk, sidx, stk, h=h, c=c):
                        if c == 0:
                            dpo[h] = ((P.ps[7], "ps7"), sidx, stk)
                        (po, pok), sidx0, stk0 = dpo[h]
                        cs = slice(c * 129, c * 129 + 129)
                        for j, kt in enumerate(ktilesD):
                            P.mm(po[:, cs], pt_[:, j, :], vd_b[:, kt, h, :], j == 0, j == nktD - 1, reads=[ptk, (A + "vd_b", kt)], writes=[pok])
                        if c == 0:
                            return
                        st0 = stat[:, sidx0, :]
                        st1 = stat[:, sidx, :]
                        ti = cnt["td"] % 2
                        cnt["td"] += 1
                        td, tdk = tmpD[ti], A + "tmpD%d" % ti
                        mk.op("dve", lambda e: e.reciprocal(out=st0[:, 8:9], in_=po[:, 128:129]), reads=[pok], writes=[stk0])
                        mk.op("dve", lambda e: e.reciprocal(out=st1[:, 8:9], in_=po[:, 257:258]), reads=[pok], writes=[stk])
                        mk.op("dve", lambda e: e.tensor_tensor(out=st1[:, 9:10], in0=st1[:, 8:9], in1=neg_lam, op=ALU.mult), reads=[stk, A + "lstat"], writes=[stk])
                        mk.op("dve", lambda e: e.tensor_scalar(out=td, in0=po[:, 129:257], scalar1=st1[:, 9:10], scalar2=None, op0=ALU.mult), reads=[pok, stk], writes=[tdk])
                        mk.op("dve", lambda e: e.scalar_tensor_tensor(out=On[:, h * 128:(h + 1) * 128], in0=po[:, 0:128], scalar=st0[:, 8:9], in1=td, op0=ALU.mult, op1=ALU.add),
                              reads=[pok, stk0, tdk], writes=[(onkey, h)])
                    gens.append(row_gen(qdT[pb:pb + 64, h, qs], kdT[pb:pb + 64, h, :], nbq[:, 0, 8 + 2 * h + c:9 + 2 * h + c], [(A + "qdT", qt)], A + "kdT", ktilesD, None, pvD, done))

            def fin_gen():
                while done[0] < 16:
                    yield
                while not freeb:
                    yield
                bk = freeb.pop(0)
                pst, psk = P.ps[bk], "ps%d" % bk
                for cc in range(4):
                    P.transpose(pst[:, cc * 128:(cc + 1) * 128], ot[:, cc * 128:(cc + 1) * 128], ident[:], reads=[otk, "ident"], writes=[psk])
                P.copy(P.evac_eng(), mixT[:, 0:4, qs], pst[:].rearrange("p (c t) -> p c t", t=128), reads=[psk], writes=["mixT"])
                freeb.append(bk)
                yield
                sidx = cnt["st"] % 64
                cnt["st"] += 1
                stk = (A + "stat", sidx)
                st = stat[:, sidx, :]
                onk = onkey
                O3 = On.rearrange("p (h x) -> p h x", x=128)
                mk.op("act", lambda e: e.activation(out=sqs, in_=On, func=AF.Square), reads=[onk], writes=[A + "sqs"])
                mk.op("dve", lambda e: e.tensor_reduce(out=st[:, 0:4], in_=sqs.rearrange("p (h x) -> p h x", x=128), axis=AX.X, op=ALU.add),
                      reads=[A + "sqs"], writes=[stk])
                mk.op("act", lambda e: e.activation(out=st[:, 0:4], in_=st[:, 0:4], func=AF.Sqrt, scale=1.0 / 128, bias=EPS), reads=[stk], writes=[stk])
                mk.op("dve", lambda e: e.reciprocal(out=st[:, 0:4], in_=st[:, 0:4]), reads=[stk], writes=[stk])
                yield
                mk.op("dve", lambda e: e.tensor_tensor(out=O3, in0=O3, in1=rap(P.arena, stat.offset + sidx * 16, [[ASIZE, 128], [1, 4], [0, 128]]), op=ALU.mult),
                      reads=[onk, stk], writes=[onk])
                mk.op("pool", lambda e: e.tensor_tensor(out=O3, in0=O3, in1=rap(P.arena, dsubg.offset, [[ASIZE, 128], [0, 4], [1, 128]]), op=ALU.mult),
                      reads=[onk, A + "dsubg"], writes=[onk])
                yield
                while not freeb:
                    yield
                bk = freeb.pop(0)
                pt2, pt2k = P.ps[bk], "ps%d" % bk
                for r in range(4):
                    P.transpose(pt2[:, r * 128:(r + 1) * 128], On[:, r * 128:(r + 1) * 128], ident[:], reads=[onk, "ident"], writes=[pt2k])
                P.copy(P.evac_eng(), mixT[:, 4:8, qs], pt2[:].rearrange("p (c t) -> p c t", t=128), reads=[pt2k], writes=["mixT"])
                freeb.append(bk)
                yield
            return gens, fin_gen

        allgens = []
        def chain_q(gens, fin):
            return gens, fin
        if v == 0:
            plan = []
            for s in range(4):
                kts = [2 * s, 2 * s + 1]
                for qt in kts:
                    plan.append((qt, kts, None, kts))
        else:
            plan = []
            for qt in range(8):
                lo, hi = max(qt - 1, 0), min(qt + 2, 8)
                plan.append((qt, [0, 1, 2, 3] + [4 + t_ for t_ in range(lo, hi)], 128 if qt == 0 else 0, list(range(12))))
        P.psmod = 5
        allg = []
        for (qt, ktc, moff, ktd) in plan:
            gens, fin = qtile_rows(qt, ktc, moff, ktd)
            allg.extend(gens)
            allg.append(fin())
        pipeline(allg, 6)
        P.psmod = 8
        mk.fence()
        P.atop = 0

    def gdn_part(v, chains):
        G = "gd%d_" % v
        P.atop = 0
        qT = P.aalloc([128, 4, T], BF16)
        kT = P.aalloc([128, 4, T], BF16)
        k_tok = P.aalloc([128, 8, 512], BF16)
        v_tok = P.aalloc([128, 8, 512], BF16)
        sz = P.aalloc([128, 8, 512], BF16)
        o_acc = P.aalloc([128, 8, 512])
        gbuf = P.aalloc([128, 2, 32])
        gc = P.aalloc([128, 2, 32])
        egc = P.aalloc([128, 2, 32])
        beta = P.aalloc([128, 2, 32])
        nbeta = P.aalloc([128, 2, 32])
        bgc = P.aalloc([128, 2, 32])
        glb = P.aalloc([128, 2, 2, 32])
        egl = P.aalloc([128, 2, 2, 32])
        ekd = P.aalloc([128, 2, 32])
        abt = P.aalloc([128, 8, 16])
        prm = P.aalloc([128, 3, 8])
        gng = P.aalloc([128, 128])
        tri = P.aalloc([128, 6, 128])
        sel = P.aalloc([128, 4, 128])
        cwq = P.aalloc([128, 3, 12])
        onesk = P.aalloc([128, 128])
        Sst = [P.aalloc([128, 512]) for _ in range(2)]
        Sbf = [P.aalloc([128, 512], BF16) for _ in range(2)]
        mark = P.atop
        vT = P.aalloc([128, 4, T], BF16)
        cst = P.aalloc([128, T])
        cso = P.aalloc([128, T])
        sqb = P.aalloc([128, T])
        seqlen = 256 if v == 0 else 1024
        mk.dma(tri, rap(c_tri, 0, [[128, 128], [128 * 128, 6], [1, 128]]), writes=[G + "tri"])
        mk.dma(sel, rap(c_sel, 0, [[128, 128], [128 * 128, 4], [1, 128]]), writes=[G + "sel"])
        mk.dma(prm[:, 0, :], rap(gdn_a_log, 0, [[0, 128], [1, 8]]), writes=[G + "prm"])
        mk.dma(prm[:, 1, :], rap(gdn_dt_bias, 0, [[0, 128], [1, 8]]), writes=[G + "prm"])
        mk.dma(gng, rap(gdn_norm_g, 0, [[0, 128], [1, 128]]), writes=[G + "gng"])
        mk.op("pool", lambda e: e.memset(onesk, 1.0), writes=[G + "onesk"])
        for tap in range(3):
            P.load_T(cwq[:, tap, :], gdn_conv_w, tap * 1536, 12, (G + "cwq", tap))
        mk.op("act", lambda e: e.activation(out=prm[:, 2, :], in_=prm[:, 0, :], func=AF.Exp), reads=[G + "prm"], writes=[G + "prm"])
        mk.op("dve", lambda e: e.tensor_scalar(out=prm[:, 2, :], in0=prm[:, 2, :], scalar1=-1.0, scalar2=None, op0=ALU.mult),
              reads=[G + "prm"], writes=[G + "prm"])
        wg = None
        for ch in range(12):
            c0 = 512 + ch * 128
            if ch % 4 == 0:
                wg, wgk = load_w(w_in_ab, 512 + ch * 128, AB_IN, 512)
            for th in range(2):
                ts = slice(th * 512, (th + 1) * 512)
                pst, psk = P.psum()
                for kc in range(8):
                    P.mm(pst[:], wg[:, kc, (ch % 4) * 128:(ch % 4 + 1) * 128], hT[:, kc, ts], kc == 0, kc == 7,
                         reads=[wgk, ("hT", th)], writes=[psk])
                P.copy(P.evac_eng(), cst[:, ts], pst[:], reads=[psk], writes=[(G + "cst", th)])
            w0 = cwq[:, 0, ch:ch + 1]
            w1 = cwq[:, 1, ch:ch + 1]
            w2 = cwq[:, 2, ch:ch + 1]
            mk.op("dve", lambda e, w1=w1: e.tensor_scalar(out=cso, in0=cst, scalar1=w1, scalar2=None, op0=ALU.mult),
                  reads=[G + "cst", (G + "cwq", 1)], writes=[G + "cso"])
            c3 = cso.rearrange("p (s t) -> p s t", t=seqlen)
            a3 = cst.rearrange("p (s t) -> p s t", t=seqlen)
            mk.op("dve", lambda e, c3=c3, a3=a3, w0=w0: e.scalar_tensor_tensor(
                out=c3[:, :, 1:seqlen], in0=a3[:, :, 0:seqlen - 1], scalar=w0, in1=c3[:, :, 1:seqlen],
                op0=ALU.mult, op1=ALU.add), reads=[G + "cst", G + "cso", (G + "cwq", 0)], writes=[G + "cso"])
            mk.op("dve", lambda e, c3=c3, a3=a3, w2=w2: e.scalar_tensor_tensor(
                out=c3[:, :, 0:seqlen - 1], in0=a3[:, :, 1:seqlen], scalar=w2, in1=c3[:, :, 0:seqlen - 1],
                op0=ALU.mult, op1=ALU.add), reads=[G + "cst", G + "cso", (G + "cwq", 2)], writes=[G + "cso"])
            if ch >= 8:
                mk.op("act", lambda e, ch=ch: e.activation(out=vT[:, ch - 8, :], in_=cso, func=AF.Silu),
                      reads=[G + "cso"], writes=[(G + "vT", ch - 8)])
                continue
            mk.op("act", lambda e: e.activation(out=cso, in_=cso, func=AF.Silu), reads=[G + "cso"], writes=[G + "cso"])
            mk.op("act", lambda e: e.activation(out=sqb, in_=cso, func=AF.Square), reads=[G + "cso"], writes=[G + "sqb"])
            for th in range(2):
                ts = slice(th * 512, (th + 1) * 512)
                pst, psk = P.psum()
                P.mm(pst[:], onesk, sqb[:, ts], True, True, reads=[G + "onesk", G + "sqb"], writes=[psk])
                mk.op("act", lambda e, pst=pst, ts=ts: e.activation(out=cst[:, ts], in_=pst[:], func=AF.Ln, bias=epsc[:, 0:1]),
                      reads=[psk, "epsc"], writes=[(G + "cst", th)])
            mk.op("act", lambda e: e.activation(out=cst, in_=cst, func=AF.Exp, scale=-0.5), reads=[G + "cst"], writes=[G + "cst"])
            dst = qT[:, ch, :] if ch < 4 else kT[:, ch - 4, :]
            dk_ = (G + "qT", ch) if ch < 4 else (G + "kT", ch - 4)
            if ch < 4:
                mk.op("dve", lambda e, dst=dst: e.scalar_tensor_tensor(out=dst, in0=cso, scalar=128 ** -0.5, in1=cst, op0=ALU.mult, op1=ALU.mult),
                      reads=[G + "cso", G + "cst"], writes=[dk_])
            else:
                mk.op("dve", lambda e, dst=dst: e.tensor_tensor(out=dst, in0=cso, in1=cst, op=ALU.mult),
                      reads=[G + "cso", G + "cst"], writes=[dk_])
        for tt in range(8):
            for (src, dst, nm) in ((kT, k_tok, "k_tok"), (vT, v_tok, "v_tok")):
                pst, psk = P.psum()
                psb = pst[:].bitcast(BF16)
                for h in range(4):
                    P.transpose(psb[:, h * 128:(h + 1) * 128], src[:, h, tt * 128:(tt + 1) * 128], identb[:],
                                reads=[(G + ("kT" if src is kT else "vT"), h), "identb"], writes=[psk])
                P.copy(P.evac_eng(), dst[:, tt, :], psb[:, 0:512], reads=[psk], writes=[(G + nm, tt)])
        wz, wzk = load_w(w_in_ab, 2048, AB_IN, 512)
        wab, wabk = load_w(w_in_ab, 2560, AB_IN, 16)
        for tt in range(8):
            pst, psk = P.psum()
            for kc in range(8):
                P.mm(pst[:], hT[:, kc, tt * 128:(tt + 1) * 128], wz[:, kc, 0:512], kc == 0, kc == 7,
                     reads=[wzk, ("hT", tt // 4)], writes=[psk])
            mk.op("act", lambda e, pst=pst, tt=tt: e.activation(out=sz[:, tt, :], in_=pst[:], func=AF.Silu),
                  reads=[psk], writes=[(G + "sz", tt)])
            pst, psk = P.psum()
            for kc in range(8):
                P.mm(pst[:, 0:16], hT[:, kc, tt * 128:(tt + 1) * 128], wab[:, kc, 0:16], kc == 0, kc == 7,
                     reads=[wabk, ("hT", tt // 4)], writes=[psk])
            P.copy("dve", abt[:, tt, :], pst[:, 0:16], reads=[psk], writes=[G + "abt"])
        for d in range(2):
            a_v = rap(P.arena, abt.offset + d * 4, [[ASIZE, 128], [16, 8], [1, 4]])
            b_v = rap(P.arena, abt.offset + 8 + d * 4, [[ASIZE, 128], [16, 8], [1, 4]])
            dtb = rap(P.arena, prm.offset + 8 + d * 4, [[ASIZE, 128], [0, 8], [1, 4]])
            nA = rap(P.arena, prm.offset + 16 + d * 4, [[ASIZE, 128], [0, 8], [1, 4]])
            g3 = gbuf[:, d, :].rearrange("p (t h) -> p t h", h=4)
            b3 = beta[:, d, :].rearrange("p (t h) -> p t h", h=4)
            mk.op("dve", lambda e, g3=g3, a_v=a_v, dtb=dtb: e.tensor_tensor(out=g3, in0=a_v, in1=dtb, op=ALU.add),
                  reads=[G + "abt", G + "prm"], writes=[G + "gbuf"])
            mk.op("act", lambda e, g3=g3: e.activation(out=g3, in_=g3, func=AF.Exp), reads=[G + "gbuf"], writes=[G + "gbuf"])
            mk.op("act", lambda e, g3=g3: e.activation(out=g3, in_=g3, func=AF.Ln, bias=1.0), reads=[G + "gbuf"], writes=[G + "gbuf"])
            mk.op("dve", lambda e, g3=g3, nA=nA: e.tensor_tensor(out=g3, in0=g3, in1=nA, op=ALU.mult),
                  reads=[G + "gbuf", G + "prm"], writes=[G + "gbuf"])
            mk.op("act", lambda e, b3=b3, b_v=b_v: e.activation(out=b3, in_=b_v, func=AF.Sigmoid), reads=[G + "abt"], writes=[G + "beta"])
        mk.op("dve", lambda e: e.tensor_scalar(out=nbeta, in0=beta, scalar1=-1.0, scalar2=None, op0=ALU.mult),
              reads=[G + "beta"], writes=[G + "nbeta"])
        for d in range(2):
            pst, psk = P.psum()
            P.mm(pst[:, 0:32], tri[:, d, :], gbuf[:, d, :], True, True, reads=[G + "tri", G + "gbuf"], writes=[psk])
            P.copy("dve", gc[:, d, :], pst[:, 0:32], reads=[psk], writes=[G + "gc"])
        for d in range(2):
            for c in range(2):
                pst, psk = P.psum()
                P.mm(pst[:, 0:32], sel[:, 2 * d + c, :], gc[:, d, :], True, True, reads=[G + "sel", G + "gc"], writes=[psk])
                P.copy("dve", glb[:, d, c, :], pst[:, 0:32], reads=[psk], writes=[G + "glb"])
        mk.op("act", lambda e: e.activation(out=egc, in_=gc, func=AF.Exp), reads=[G + "gc"], writes=[G + "egc"])
        mk.op("act", lambda e: e.activation(out=egl, in_=glb, func=AF.Exp), reads=[G + "glb"], writes=[G + "egl"])
        mk.op("dve", lambda e: e.tensor_tensor(out=bgc, in0=beta, in1=egc, op=ALU.mult), reads=[G + "beta", G + "egc"], writes=[G + "bgc"])
        for c in range(2):
            rows = slice(c * 64, (c + 1) * 64)
            mk.op("dve", lambda e, c=c, rows=rows: e.tensor_tensor(out=ekd[rows, :, :], in0=glb[rows, :, c, :], in1=gc[rows, :, :], op=ALU.subtract),
                  reads=[G + "glb", G + "gc"], writes=[G + "ekd"])
        mk.op("act", lambda e: e.activation(out=ekd, in_=ekd, func=AF.Exp), reads=[G + "ekd"], writes=[G + "ekd"])
        mk.op("pool", lambda e: e.memset(o_acc, 0.0), writes=[G + "o_acc"])
        mk.fence()
        P.atop = mark
        pre_out[0] = load_w(w_out_ab, 0, D, 512)
        def mkset():
            b = {}
            b["Rg"] = P.aalloc([128, 512]); b["ub"] = b["Rg"]
            b["Dm"] = P.aalloc([128, 512]); b["otmp"] = b["Dm"]
            b["DmT"] = P.aalloc([128, 512])
            dmt_b = b["DmT"].bitcast(BF16)
            b["wT"] = dmt_b[:, 0:512]
            b["vn"] = dmt_b[:, 512:1024]
            b["X"] = P.aalloc([128, 512]); b["XT"] = P.aalloc([128, 512]); b["R"] = P.aalloc([128, 512])
            for nm in ("TTb", "ATm", "vb", "kbg", "kd"):
                b[nm] = P.aalloc([128, 512], BF16)
            return b
        sets = [mkset(), mkset()]

        def bc_h(buf, d, tt, inner):
            return rap(P.arena, buf.offset + d * 32 + tt * 4, [[ASIZE, 128], [1, 4], [0, inner]])

        def v4(t_):
            return t_.rearrange("p (h x) -> p h x", x=128)

        def hsl(h):
            return slice(h * 128, (h + 1) * 128)

        def run_chain(ci, d, tiles, sq_):
            B = sets[d]
            K = G + "w%d_" % d
            Rg, Dm, DmT, X, XT, R_ = B["Rg"], B["Dm"], B["DmT"], B["X"], B["XT"], B["R"]
            TTb, ATm, vb, kbg, kd, ub, wT, vn, otmp = B["TTb"], B["ATm"], B["vb"], B["kbg"], B["kd"], B["ub"], B["wT"], B["vn"], B["otmp"]
            kRg, kDm, kDmT, kX, kXT, kR = K + "Rg", K + "Dm", K + "DmT", K + "X", K + "XT", K + "R"
            kub, kotmp, kwT, kvn = kRg, kDm, kDmT, kDmT
            S, Sk = Sst[d], G + "S%d" % d
            Sb, Sbk = Sbf[d], G + "Sb%d" % d
            if v == 0:
                mk.op("pool", lambda e, S=S: e.memset(S, 0.0), writes=[Sk])
            else:
                mk.dma(S.rearrange("p (h x) -> p h x", x=128), rap(gstate, d * 4 * 128 * 128, [[128, 128], [128 * 128, 4], [1, 128]]), writes=[Sk])
            mk.op("act", lambda e, S=S, Sb=Sb: e.activation(out=Sb, in_=S, func=AF.Copy), reads=[Sk], writes=[Sbk])
            triD = tri[:, d, :]
            ntri = tri[:, 2 + d, :]
            mS = tri[:, 4 + d, :]
            ntri_b = rap(P.arena, ntri.offset, [[ASIZE, 128], [0, 4], [1, 128]])
            mS_b = rap(P.arena, mS.offset, [[ASIZE, 128], [0, 4], [1, 128]])
            tri_b = rap(P.arena, triD.offset, [[ASIZE, 128], [0, 4], [1, 128]])
            id_b = rap(ident, 0, [[128, 128], [0, 4], [1, 128]])
            yield
            for tt in tiles:
                tsl = slice(tt * 128, (tt + 1) * 128)
                g_b = bc_h(gbuf, d, tt, 128)
                mk.op("dve", lambda e, g_b=g_b: e.tensor_tensor(out=v4(Rg), in0=ntri_b, in1=g_b, op=ALU.mult),
                      reads=[G + "tri", G + "gbuf"], writes=[kRg])
                pL, pLk = P.psum()
                P.mm(pL[:], triD, Rg, True, True, reads=[G + "tri", kRg], writes=[pLk])
                mk.op("act", lambda e, pL=pL: e.activation(out=Dm, in_=pL[:], func=AF.Exp), reads=[pLk], writes=[kDm])
                pLT, pLTk = P.psum()
                for h in range(4):
                    P.mm(pLT[:, hsl(h)], Rg[:, hsl(h)], triD, True, True, reads=[G + "tri", kRg], writes=[pLTk])
                mk.op("act", lambda e, pLT=pLT: e.activation(out=DmT, in_=pLT[:], func=AF.Exp), reads=[pLTk], writes=[kDmT])
                yield
                nb_b = bc_h(nbeta, d, tt, 128)
                mk.op("pool", lambda e: e.tensor_tensor(out=v4(Dm), in0=v4(Dm), in1=mS_b, op=ALU.mult), reads=[kDm, G + "tri"], writes=[kDm])
                mk.op("pool", lambda e, nb_b=nb_b: e.tensor_tensor(out=v4(Dm), in0=v4(Dm), in1=nb_b, op=ALU.mult), reads=[kDm, G + "nbeta"], writes=[kDm])
                mk.op("pool", lambda e: e.tensor_tensor(out=v4(DmT), in0=v4(DmT), in1=tri_b, op=ALU.mult), reads=[kDmT, G + "tri"], writes=[kDmT])
                pG, pGk = P.psum()
                pA, pAk = P.psum()
                for h in range(4):
                    P.mm(pG[:, hsl(h)], kT[:, h, tsl], kT[:, h, tsl], True, True, reads=[(G + "kT", h)], writes=[pGk])
                for h in range(4):
                    P.mm(pA[:, hsl(h)], kT[:, h, tsl], qT[:, h, tsl], True, True, reads=[(G + "kT", h), (G + "qT", h)], writes=[pAk])
                mk.op("dve", lambda e, pG=pG: e.tensor_tensor(out=X, in0=pG[:], in1=Dm, op=ALU.mult), reads=[pGk, kDm], writes=[kX])
                mk.op("dve", lambda e, pA=pA: e.tensor_tensor(out=ATm, in0=pA[:], in1=DmT, op=ALU.mult), reads=[pAk, kDmT], writes=[K + "ATm"])
                yield
                pX, pXk = P.psum()
                for h in range(4):
                    P.transpose(pX[:, hsl(h)], X[:, hsl(h)], ident[:], reads=[kX, "ident"], writes=[pXk])
                P.copy("act", XT, pX[:, 0:512], reads=[pXk], writes=[kXT])
                mk.op("dve", lambda e: e.tensor_tensor(out=v4(R_), in0=v4(XT), in1=id_b, op=ALU.add), reads=[kXT, "ident"], writes=[kR])
                yield
                for it in range(1, 6):
                    p1, p1k = P.psum()
                    for h in range(4):
                        P.mm(p1[:, hsl(h)], XT[:, hsl(h)], X[:, hsl(h)], True, True, reads=[kXT, kX], writes=[p1k])
                    if it < 5:
                        p2, p2k = P.psum()
                        for h in range(4):
                            P.mm(p2[:, hsl(h)], X[:, hsl(h)], XT[:, hsl(h)], True, True, reads=[kXT, kX], writes=[p2k])
                    P.copy("act", X, p1[:], reads=[p1k], writes=[kX])
                    if it < 5:
                        P.copy("dve", XT, p2[:], reads=[p2k], writes=[kXT])
                    p3, p3k = P.psum()
                    for h in range(4):
                        P.mm(p3[:, hsl(h)], X[:, hsl(h)], R_[:, hsl(h)], True, True, reads=[kX, kR], writes=[p3k])
                    mk.op("dve", lambda e, p3=p3: e.tensor_tensor(out=R_, in0=p3[:], in1=R_, op=ALU.add), reads=[p3k, kR], writes=[kR])
                    yield
                mk.op("act", lambda e: e.activation(out=TTb, in_=R_, func=AF.Copy), reads=[kR], writes=[K + "TTb"])
                be_b, bg_b, ek_b = bc_h(beta, d, tt, 128), bc_h(bgc, d, tt, 128), bc_h(ekd, d, tt, 128)
                mk.op("pool", lambda e, tt=tt, be_b=be_b: e.tensor_tensor(out=v4(vb), in0=v4(v_tok[:, tt, :]), in1=be_b, op=ALU.mult),
                      reads=[(G + "v_tok", tt), G + "beta"], writes=[K + "vb"])
                mk.op("pool", lambda e, tt=tt, bg_b=bg_b: e.tensor_tensor(out=v4(kbg), in0=v4(k_tok[:, tt, :]), in1=bg_b, op=ALU.mult),
                      reads=[(G + "k_tok", tt), G + "bgc"], writes=[K + "kbg"])
                mk.op("pool", lambda e, tt=tt, ek_b=ek_b: e.tensor_tensor(out=v4(kd), in0=v4(k_tok[:, tt, :]), in1=ek_b, op=ALU.mult),
                      reads=[(G + "k_tok", tt), G + "ekd"], writes=[K + "kd"])
                pu, puk = P.psum()
                pw, pwk = P.psum()
                for h in range(4):
                    P.mm(pu[:, hsl(h)], TTb[:, hsl(h)], vb[:, hsl(h)], True, True, reads=[K + "TTb", K + "vb"], writes=[puk])
                for h in range(4):
                    P.mm(pw[:, hsl(h)], kbg[:, hsl(h)], TTb[:, hsl(h)], True, True, reads=[K + "TTb", K + "kbg"], writes=[pwk])
                P.copy("act", ub, pu[:], reads=[puk], writes=[kub])
                P.copy("dve", wT, pw[:], reads=[pwk], writes=[kwT])
                yield
                for c in ((0, 1) if d == 0 else (1, 0)):
                    rows = slice(c * 64, (c + 1) * 64)
                    ccols = slice(tt * 128 + c * 64, tt * 128 + (c + 1) * 64)
                    p1, p1k = P.psum()
                    for h in range(4):
                        P.mm(p1[rows, hsl(h)], wT[:, h * 128 + c * 64:h * 128 + (c + 1) * 64], Sb[:, hsl(h)], True, True,
                             reads=[kwT, Sbk], writes=[p1k])
                    mk.op("dve", lambda e, p1=p1, rows=rows: e.tensor_tensor(out=vn[rows, :], in0=ub[rows, :], in1=p1[rows, :], op=ALU.subtract),
                          reads=[p1k, kub], writes=[kvn])
                    p2, p2k = P.psum()
                    for h in range(4):
                        P.mm(p2[rows, hsl(h)], qT[:, h, ccols], Sb[:, hsl(h)], True, True, reads=[(G + "qT", h), Sbk], writes=[p2k])
                    yield
                    p3, p3k = P.psum()
                    for h in range(4):
                        P.mm(p3[rows, hsl(h)], ATm[rows, h * 128 + c * 64:h * 128 + (c + 1) * 64], vn[rows, hsl(h)], True, True,
                             reads=[K + "ATm", kvn], writes=[p3k])
                    p4, p4k = P.psum()
                    for h in range(4):
                        P.mm(p4[:, hsl(h)], kd[rows, hsl(h)], vn[rows, hsl(h)], True, True, reads=[K + "kd", kvn], writes=[p4k])
                    mk.op("dve", lambda e, p3=p3, rows=rows, tt=tt: e.tensor_tensor(out=o_acc[rows, tt, :], in0=p3[rows, :], in1=o_acc[rows, tt, :], op=ALU.add),
                          reads=[p3k, (G + "o_acc", tt)], writes=[(G + "o_acc", tt)])
                    egb = rap(P.arena, egc.offset + c * 64 * ASIZE + d * 32 + tt * 4, [[ASIZE, 64], [1, 4], [0, 128]])
                    mk.op("dve", lambda e, p2=p2, rows=rows, egb=egb: e.tensor_tensor(out=v4(otmp[rows, :]), in0=v4(p2[rows, :]), in1=egb, op=ALU.mult),
                          reads=[p2k, G + "egc"], writes=[kotmp])
                    mk.op("pool", lambda e, rows=rows, tt=tt: e.tensor_tensor(out=o_acc[rows, tt, :], in0=otmp[rows, :], in1=o_acc[rows, tt, :], op=ALU.add),
                          reads=[kotmp, (G + "o_acc", tt)], writes=[(G + "o_acc", tt)])
                    for h in range(4):
                        col = egl[:, d, c, tt * 4 + h:tt * 4 + h + 1]
                        mk.op("dve", lambda e, S=S, p4=p4, h=h, col=col: e.scalar_tensor_tensor(
                            out=S[:, hsl(h)], in0=S[:, hsl(h)], scalar=col, in1=p4[:, hsl(h)], op0=ALU.mult, op1=ALU.add),
                            reads=[p4k, Sk, G + "egl"], writes=[Sk])
                    mk.op("act", lambda e, S=S, Sb=Sb: e.activation(out=Sb, in_=S, func=AF.Copy), reads=[Sk], writes=[Sbk])
                    yield
            if v == 0:
                ok = ("o_gdn", sq_, d)
                mk.dma(rap(o_gdn, (sq_ * 2 + d) * 4 * 128 * 128, [[128, 128], [128 * 128, 4], [1, 128]]),
                       S.rearrange("p (h x) -> p h x", x=128), reads=[Sk], writes=[ok])
                P.outs.append(ok)

        for c0 in range(0, len(chains), 2):
            gens = [run_chain(c0 + i, *chains[c0 + i]) for i in range(2)]
            while gens:
                for g_ in list(gens):
                    try:
                        next(g_)
                    except StopIteration:
                        gens.remove(g_)
        Rg, otmp, vb = sets[0]["Rg"], sets[0]["otmp"], sets[0]["vb"]
        for tt in range(8):
            o3 = o_acc[:, tt, :].rearrange("p (h x) -> p h x", x=128)
            mk.op("act", lambda e, tt=tt: e.activation(out=otmp, in_=o_acc[:, tt, :], func=AF.Square), reads=[(G + "o_acc", tt)], writes=[G + "w0_Dm"])
            mk.op("dve", lambda e: e.tensor_reduce(out=Rg[:, 0:4], in_=v4(otmp), axis=AX.X, op=ALU.add), reads=[G + "w0_Dm"], writes=[G + "w0_Rg"])
            mk.op("act", lambda e: e.activation(out=Rg[:, 0:4], in_=Rg[:, 0:4], func=AF.Sqrt, scale=1.0 / 128, bias=EPS), reads=[G + "w0_Rg"], writes=[G + "w0_Rg"])
            mk.op("dve", lambda e: e.reciprocal(out=Rg[:, 0:4], in_=Rg[:, 0:4]), reads=[G + "w0_Rg"], writes=[G + "w0_Rg"])
            mk.op("dve", lambda e, o3=o3: e.tensor_tensor(out=v4(otmp), in0=o3, in1=rap(P.arena, Rg.offset, [[ASIZE, 128], [1, 4], [0, 128]]), op=ALU.mult),
                  reads=[(G + "o_acc", tt), G + "w0_Rg", G + "w0_Dm"], writes=[G + "w0_Dm"])
            mk.op("pool", lambda e: e.tensor_tensor(out=v4(otmp), in0=v4(otmp), in1=rap(P.arena, gng.offset, [[ASIZE, 128], [0, 4], [1, 128]]), op=ALU.mult),
                  reads=[G + "w0_Dm", G + "gng"], writes=[G + "w0_Dm"])
            mk.op("dve", lambda e, tt=tt: e.tensor_tensor(out=vb, in0=otmp, in1=sz[:, tt, :], op=ALU.mult),
                  reads=[G + "w0_Dm", (G + "sz", tt)], writes=[G + "w0_vb"])
            pst, psk = P.psum()
            psb = pst[:].bitcast(BF16)
            for h in range(4):
                P.transpose(psb[:, h * 128:(h + 1) * 128], vb[:, h * 128:(h + 1) * 128], identb[:], reads=[G + "w0_vb", "identb"], writes=[psk])
            P.copy(P.evac_eng(), mixT[:, 4:8, tt * 128:(tt + 1) * 128], psb[:, 0:512].rearrange("p (c t) -> p c t", t=128),
                   reads=[psk], writes=["mixT"])
        mk.fence()
        P.atop = 0

    MAGIC = 12582912.0
    TWO_PI = 2.0 * math.pi

    def s5_part(v, mode="run", base=0):
        Z = ("s5%d_" % v) if mode == "run" else "s5p_"
        P.atop = base
        NS = 4
        u_tok = P.aalloc([128, 32, 8, 16]) if mode == "run" else None
        dB = P.aalloc([128, 512])
        sc = {}
        for nm in ("lre", "lim", "dtv", "lr", "li", "ar", "ai", "fr", "fi", "den", "t0", "t1", "t2", "rho8", "li8",
                   "rfr", "rfi", "h0r", "h0i", "fsr", "fsi"):
            sc[nm] = P.aalloc([128, 32])
        kk = P.aalloc([128, 8])
        arow = P.aalloc([128, 33])
        hl = P.aalloc([128, 2, 4, 32])
        natl = P.aalloc([128, 64])
        mfb = P.aalloc([128, 2, 128])
        mark0 = P.atop

        def sincos(x, n, cos_out, sin_out, tmp, rk, wk, turns=False):
            for (dst, shift) in ((sin_out, 0.0), (cos_out, 0.25)):
                mk.op("dve", lambda e, dst=dst, shift=shift: e.tensor_scalar(out=dst, in0=x, scalar1=(1.0 if turns else 1.0 / TWO_PI), scalar2=shift, op0=ALU.mult, op1=ALU.add),
                      reads=rk, writes=wk)
                mk.op("dve", lambda e, dst=dst: e.tensor_scalar(out=tmp, in0=dst, scalar1=MAGIC, scalar2=MAGIC, op0=ALU.add, op1=ALU.subtract),
                      reads=wk, writes=wk)
                mk.op("dve", lambda e, dst=dst: e.tensor_tensor(out=dst, in0=dst, in1=tmp, op=ALU.subtract), reads=wk, writes=wk)
                mk.op("act", lambda e, dst=dst: e.activation(out=dst, in_=dst, func=AF.Sin, scale=TWO_PI), reads=wk, writes=wk)

        SK = [Z + "sc"]
        mk.dma(kk, c_kk.ap(), writes=SK)
        mk.dma(arow, c_arow.ap(), writes=SK)
        mk.dma(mfb, rap(c_mfb, 0, [[128, 128], [128 * 128, 2], [1, 128]]), writes=[Z + "mfb"])
        mk.dma(dB, rap(s5_d, 0, [[0, 128], [1, 512]]), writes=[Z + "dB"])
        for (src, dst) in ((s5_lam_re, "lre"), (s5_lam_im, "lim")) + (((s5h_re, "h0r"), (s5h_im, "h0i")) if v == 1 else ()):
            mk.dma(natl[0:64, :], rap(src, 0, [[64, 64], [1, 64]]), writes=[Z + "natl"])
            pst, psk = P.psum()
            for d in range(2):
                P.mm(pst[d * 64:(d + 1) * 64, 0:32], natl[d * 32:(d + 1) * 32, :], ident[d * 32:(d + 1) * 32, d * 32:(d + 1) * 32],
                     True, True, reads=[Z + "natl", "ident"], writes=[psk])
            P.copy("dve", sc[dst], pst[:, 0:32], reads=[psk], writes=SK)
        for d in range(2):
            mk.dma(sc["dtv"][d * 64:(d + 1) * 64, :], rap(s5_log_dt, d * 32, [[0, 64], [1, 32]]), writes=SK)
        S = sc
        mk.op("act", lambda e: e.activation(out=S["dtv"], in_=S["dtv"], func=AF.Exp), reads=SK, writes=SK)
        mk.op("dve", lambda e: e.tensor_tensor(out=S["lr"], in0=S["lre"], in1=S["dtv"], op=ALU.mult), reads=SK, writes=SK)
        mk.op("dve", lambda e: e.tensor_tensor(out=S["li"], in0=S["lim"], in1=S["dtv"], op=ALU.mult), reads=SK, writes=SK)
        sincos(S["li"], 32, S["ar"], S["ai"], S["t0"], SK, SK)
        mk.op("act", lambda e: e.activation(out=S["t1"], in_=S["lr"], func=AF.Exp), reads=SK, writes=SK)
        mk.op("dve", lambda e: e.tensor_tensor(out=S["ar"], in0=S["ar"], in1=S["t1"], op=ALU.mult), reads=SK, writes=SK)
        mk.op("dve", lambda e: e.tensor_tensor(out=S["ai"], in0=S["ai"], in1=S["t1"], op=ALU.mult), reads=SK, writes=SK)
        mk.op("dve", lambda e: e.tensor_tensor(out=S["den"], in0=S["lre"], in1=S["lre"], op=ALU.mult), reads=SK, writes=SK)
        mk.op("dve", lambda e: e.tensor_tensor(out=S["t0"], in0=S["lim"], in1=S["lim"], op=ALU.mult), reads=SK, writes=SK)
        mk.op("dve", lambda e: e.tensor_tensor(out=S["den"], in0=S["den"], in1=S["t0"], op=ALU.add), reads=SK, writes=SK)
        mk.op("dve", lambda e: e.reciprocal(out=S["den"], in_=S["den"]), reads=SK, writes=SK)
        mk.op("dve", lambda e: e.tensor_scalar(out=S["t2"], in0=S["ar"], scalar1=-1.0, scalar2=None, op0=ALU.add), reads=SK, writes=SK)
        mk.op("dve", lambda e: e.tensor_tensor(out=S["fr"], in0=S["t2"], in1=S["lre"], op=ALU.mult), reads=SK, writes=SK)
        mk.op("dve", lambda e: e.tensor_tensor(out=S["t0"], in0=S["ai"], in1=S["lim"], op=ALU.mult), reads=SK, writes=SK)
        mk.op("dve", lambda e: e.tensor_tensor(out=S["fr"], in0=S["fr"], in1=S["t0"], op=ALU.add), reads=SK, writes=SK)
        mk.op("dve", lambda e: e.tensor_tensor(out=S["fr"], in0=S["fr"], in1=S["den"], op=ALU.mult), reads=SK, writes=SK)
        mk.op("dve", lambda e: e.tensor_tensor(out=S["fi"], in0=S["ai"], in1=S["lre"], op=ALU.mult), reads=SK, writes=SK)
        mk.op("dve", lambda e: e.tensor_tensor(out=S["t0"], in0=S["t2"], in1=S["lim"], op=ALU.mult), reads=SK, writes=SK)
        mk.op("dve", lambda e: e.tensor_tensor(out=S["fi"], in0=S["fi"], in1=S["t0"], op=ALU.subtract), reads=SK, writes=SK)
        mk.op("dve", lambda e: e.tensor_tensor(out=S["fi"], in0=S["fi"], in1=S["den"], op=ALU.mult), reads=SK, writes=SK)
        mk.op("act", lambda e: e.activation(out=S["rho8"], in_=S["lr"], func=AF.Exp, scale=8.0), reads=SK, writes=SK)
        mk.op("dve", lambda e: e.tensor_scalar(out=S["li8"], in0=S["li"], scalar1=8.0 / TWO_PI, scalar2=None, op0=ALU.mult), reads=SK, writes=SK)
        mk.op("dve", lambda e: e.tensor_scalar(out=S["t0"], in0=S["li8"], scalar1=MAGIC, scalar2=MAGIC, op0=ALU.add, op1=ALU.subtract), reads=SK, writes=SK)
        mk.op("dve", lambda e: e.tensor_tensor(out=S["li8"], in0=S["li8"], in1=S["t0"], op=ALU.subtract), reads=SK, writes=SK)
        mk.op("dve", lambda e: e.tensor_scalar(out=S["t2"], in0=S["li"], scalar1=1.0 / TWO_PI, scalar2=None, op0=ALU.mult), reads=SK, writes=SK)
        mk.op("dve", lambda e: e.tensor_scalar(out=S["t0"], in0=S["t2"], scalar1=MAGIC, scalar2=MAGIC, op0=ALU.add, op1=ALU.subtract), reads=SK, writes=SK)
        mk.op("dve", lambda e: e.tensor_tensor(out=S["t2"], in0=S["t2"], in1=S["t0"], op=ALU.subtract), reads=SK, writes=SK)
        mk.op("dve", lambda e: e.tensor_scalar(out=S["t2"], in0=S["t2"], scalar1=255.0, scalar2=None, op0=ALU.mult), reads=SK, writes=SK)
        sincos(S["t2"], 32, S["rfr"], S["rfi"], S["t0"], SK, SK, turns=True)
        mk.op("act", lambda e: e.activation(out=S["t1"], in_=S["lr"], func=AF.Exp, scale=7.0), reads=SK, writes=SK)
        mk.op("dve", lambda e: e.tensor_tensor(out=S["rfr"], in0=S["rfr"], in1=S["t1"], op=ALU.mult), reads=SK, writes=SK)
        mk.op("dve", lambda e: e.tensor_tensor(out=S["rfi"], in0=S["rfi"], in1=S["t1"], op=ALU.mult), reads=SK, writes=SK)
        if v == 1:
            mk.op("dve", lambda e: e.tensor_tensor(out=S["fsr"], in0=S["ar"], in1=S["h0r"], op=ALU.mult), reads=SK, writes=SK)
            mk.op("dve", lambda e: e.tensor_tensor(out=S["t0"], in0=S["ai"], in1=S["h0i"], op=ALU.mult), reads=SK, writes=SK)
            mk.op("dve", lambda e: e.tensor_tensor(out=S["fsr"], in0=S["fsr"], in1=S["t0"], op=ALU.subtract), reads=SK, writes=SK)
            mk.op("dve", lambda e: e.tensor_tensor(out=S["fsi"], in0=S["ar"], in1=S["h0i"], op=ALU.mult), reads=SK, writes=SK)
            mk.op("dve", lambda e: e.tensor_tensor(out=S["t0"], in0=S["ai"], in1=S["h0r"], op=ALU.mult), reads=SK, writes=SK)
            mk.op("dve", lambda e: e.tensor_tensor(out=S["fsi"], in0=S["fsi"], in1=S["t0"], op=ALU.add), reads=SK, writes=SK)
        yield
        if mode == "run":
            wu, wuk = load_w(w_in_ab, 0, AB_IN, 512)
            for b in range(8):
                pst, psk = P.psum()
                for kc in range(8):
                    P.mm(pst[:], hT[:, kc, b:T:8], wu[:, kc, 0:512], kc == 0, kc == 7, reads=[wuk, "hT"], writes=[psk])
                P.copy(P.evac_eng(), u_tok[:, :, b, :], pst[:].rearrange("p (g c) -> p g c", c=16), reads=[psk], writes=[Z + "u_tok"])

        def off(view):
            return view.offset

        for gh in range(2):
            g0 = gh * 16
            P.atop = mark0
            H = (Z + "h%d_" % gh) if mode == "run" else (Z + "h_")
            markp = P.atop
            PTre = P.aalloc([128, 16, 128], BF16)
            PTim = P.aalloc([128, 16, 128], BF16)
            Qre_b = P.aalloc([128, 16, 128], BF16)
            Qsim_b = P.aalloc([128, 16, 128], BF16)
            WT = P.aalloc([128, 16, 128], BF16)
            Rr = P.aalloc([128, 16, 33])
            Ri = P.aalloc([128, 16, 33])
            R2r = P.aalloc([128, 16, 33])
            R2i = P.aalloc([128, 16, 33])
            D0 = P.aalloc([128, 16, 32])
            mark1 = P.atop
            assert mark1 - markp == 7744, (mark1, markp)
            def bc_g(nm, inner):
                return rap(P.arena, off(sc[nm]) + g0, [[ASIZE, 128], [1, 16], [0, inner]])
            PE2 = "pool" if mode == "run" else "dve"
            def cmul(out_r, out_i, ar_, ai_, br_, bi_, tmp, keys, neg_im=False):
                mk.op("dve", lambda e: e.tensor_tensor(out=out_r, in0=ar_, in1=br_, op=ALU.mult), reads=keys, writes=keys)
                mk.op(PE2, lambda e: e.tensor_tensor(out=tmp, in0=ai_, in1=bi_, op=ALU.mult), reads=keys, writes=keys)
                mk.op("dve", lambda e: e.tensor_tensor(out=out_r, in0=out_r, in1=tmp, op=ALU.subtract), reads=keys, writes=keys)
                mk.op("dve", lambda e: e.tensor_tensor(out=out_i, in0=ar_, in1=bi_, op=ALU.mult), reads=keys, writes=keys)
                mk.op(PE2, lambda e: e.tensor_tensor(out=tmp, in0=ai_, in1=br_, op=ALU.mult), reads=keys, writes=keys)
                if neg_im:
                    mk.op("dve", lambda e: e.scalar_tensor_tensor(out=out_i, in0=out_i, scalar=-1.0, in1=tmp, op0=ALU.mult, op1=ALU.subtract),
                          reads=keys, writes=keys)
                else:
                    mk.op("dve", lambda e: e.tensor_tensor(out=out_i, in0=out_i, in1=tmp, op=ALU.add), reads=keys, writes=keys)
            if mode == "pre":
                Bn = [P.aalloc([128, 16, 16]) for _ in range(2)]
                Bb = [P.aalloc([128, 16, 16]) for _ in range(2)]
                Cn = [P.aalloc([128, 16, 16]) for _ in range(2)]
                TPm = [P.aalloc([128, 16, 8]) for _ in range(4)]
                KL = P.aalloc([128, 16, 8])
                KI = P.aalloc([128, 16, 8])
                tmpk = P.aalloc([128, 16, 8])
                Pre = P.aalloc([128, 16, 8, 16])
                Pim = P.aalloc([128, 16, 8, 16])
                Qre = P.aalloc([128, 16, 8, 16])
                Qsim = P.aalloc([128, 16, 8, 16])
                tA = P.aalloc([128, 16, 8, 16])
                cnat = P.aalloc([128, 64])
                ang = P.aalloc([128, 16, 33])
                tng = P.aalloc([128, 16, 33])
                HK = [H + "pre"]
                for ri, (bsrc, csrc) in enumerate(((s5_b_re, s5_c_re), (s5_b_im, s5_c_im))):
                    for d in range(2):
                        mk.dma(Bn[ri][d * 64:(d + 1) * 64, :, :], rap(bsrc, (d * 32 + g0) * 1024, [[16, 64], [1024, 16], [1, 16]]), writes=HK)
                        for gq in range(2):
                            mk.dma(cnat, rap(csrc, (d * 32 + g0 + gq * 8) * 1024, [[64, 128], [1, 64]]), writes=[H + "cnat"])
                            pst, psk = P.psum()
                            P.mm(pst[d * 64:(d + 1) * 64, 0:128], cnat, ident[:], True, True, reads=[H + "cnat", "ident"], writes=[psk])
                            P.copy("dve", Cn[ri][d * 64:(d + 1) * 64, gq * 8:(gq + 1) * 8, :],
                                   pst[d * 64:(d + 1) * 64, 0:128].rearrange("p (g c) -> p g c", c=16), reads=[psk], writes=HK)
                def bc_g(nm, inner):
                    return rap(P.arena, off(sc[nm]) + g0, [[ASIZE, 128], [1, 16], [0, inner]])
                def cmul(out_r, out_i, ar_, ai_, br_, bi_, tmp, keys, neg_im=False):
                    mk.op("dve", lambda e: e.tensor_tensor(out=out_r, in0=ar_, in1=br_, op=ALU.mult), reads=keys, writes=keys)
                    mk.op("dve", lambda e: e.tensor_tensor(out=tmp, in0=ai_, in1=bi_, op=ALU.mult), reads=keys, writes=keys)
                    mk.op("dve", lambda e: e.tensor_tensor(out=out_r, in0=out_r, in1=tmp, op=ALU.subtract), reads=keys, writes=keys)
                    mk.op("dve", lambda e: e.tensor_tensor(out=out_i, in0=ar_, in1=bi_, op=ALU.mult), reads=keys, writes=keys)
                    mk.op("dve", lambda e: e.tensor_tensor(out=tmp, in0=ai_, in1=br_, op=ALU.mult), reads=keys, writes=keys)
                    if neg_im:
                        mk.op("dve", lambda e: e.scalar_tensor_tensor(out=out_i, in0=out_i, scalar=-1.0, in1=tmp, op0=ALU.mult, op1=ALU.subtract),
                              reads=keys, writes=keys)
                    else:
                        mk.op("dve", lambda e: e.tensor_tensor(out=out_i, in0=out_i, in1=tmp, op=ALU.add), reads=keys, writes=keys)
                AK = HK + SK
                cmul(Bb[0], Bb[1], bc_g("fr", 16), bc_g("fi", 16), Bn[0], Bn[1], tA[:, :, 0, :], AK)
                kkb = rap(P.arena, off(kk), [[ASIZE, 128], [0, 16], [1, 8]])
                lr8b, li8b = bc_g("lr", 8), bc_g("li", 8)
                mk.op("dve", lambda e, lr8b=lr8b, KL=KL, kkb=kkb: e.tensor_tensor(out=KL, in0=kkb, in1=lr8b, op=ALU.mult), reads=AK, writes=HK)
                mk.op("dve", lambda e, li8b=li8b, KI=KI, kkb=kkb: e.tensor_tensor(out=KI, in0=kkb, in1=li8b, op=ALU.mult), reads=AK, writes=HK)
                sincos(KI, 128, TPm[0], TPm[1], tmpk, HK, HK)
                mk.op("act", lambda e: e.activation(out=tmpk, in_=KL, func=AF.Exp, scale=-1.0), reads=HK, writes=HK)
                mk.op("dve", lambda e: e.tensor_tensor(out=TPm[2], in0=TPm[0], in1=tmpk, op=ALU.mult), reads=HK, writes=HK)
                mk.op("dve", lambda e: e.scalar_tensor_tensor(out=TPm[3], in0=TPm[1], scalar=-1.0, in1=tmpk, op0=ALU.mult, op1=ALU.mult), reads=HK, writes=HK)
                mk.op("act", lambda e: e.activation(out=tmpk, in_=KL, func=AF.Exp), reads=HK, writes=HK)
                mk.op("dve", lambda e: e.tensor_tensor(out=TPm[0], in0=TPm[0], in1=tmpk, op=ALU.mult), reads=HK, writes=HK)
                mk.op("dve", lambda e: e.tensor_tensor(out=TPm[1], in0=TPm[1], in1=tmpk, op=ALU.mult), reads=HK, writes=HK)
                def tb(t_):
                    return rap(P.arena, off(t_), [[ASIZE, 128], [8, 16], [1, 8], [0, 16]])
                def vb_(t_):
                    return rap(P.arena, off(t_), [[ASIZE, 128], [16, 16], [0, 8], [1, 16]])
                yield
                cmul(Pre, Pim, tb(TPm[2]), tb(TPm[3]), vb_(Bb[0]), vb_(Bb[1]), tA, HK)
                yield
                cmul(Qre, Qsim, tb(TPm[0]), tb(TPm[1]), vb_(Cn[0]), vb_(Cn[1]), tA, HK, neg_im=True)
                mk.op("act", lambda e: e.activation(out=Qre_b, in_=Qre.rearrange("p g b c -> p g (b c)"), func=AF.Copy), reads=HK, writes=[H + "Qb"])
                mk.op("act", lambda e: e.activation(out=Qsim_b, in_=Qsim.rearrange("p g b c -> p g (b c)"), func=AF.Copy), reads=HK, writes=[H + "Qb"])
                yield
                for g4 in range(4):
                    yield
                    for (src, dst, nm) in ((Pre, PTre, "PTre"), (Pim, PTim, "PTim")):
                        pst, psk = P.psum()
                        for gg in range(4):
                            g = g4 * 4 + gg
                            P.transpose(pst[:, gg * 128:(gg + 1) * 128], src[:, g, :, :].rearrange("p b c -> p (b c)"), ident[:],
                                        reads=HK + ["ident"], writes=[psk])
                        P.copy(P.evac_eng(), dst[:, g4 * 4:(g4 + 1) * 4, :], pst[:].rearrange("p (g x) -> p g x", x=128), reads=[psk], writes=[H + nm])
                    pf, pfk = P.psum()
                    pb_, pbk = P.psum()
                    for gg in range(4):
                        g = g4 * 4 + gg
                        for (pp, ppk, rows) in ((pf, pfk, slice(0, 64)), (pb_, pbk, slice(64, 128))):
                            P.mm(pp[:, gg * 128:(gg + 1) * 128], Pre[rows, g, :, :].rearrange("p b c -> p (b c)"),
                                 Qre[rows, g, :, :].rearrange("p b c -> p (b c)"), True, False, reads=HK, writes=[ppk])
                            P.mm(pp[:, gg * 128:(gg + 1) * 128], Pim[rows, g, :, :].rearrange("p b c -> p (b c)"),
                                 Qsim[rows, g, :, :].rearrange("p b c -> p (b c)"), False, True, reads=HK, writes=[ppk])
                    mfv = rap(P.arena, off(mfb), [[ASIZE, 128], [0, 4], [1, 128]])
                    mbv = rap(P.arena, off(mfb) + 128, [[ASIZE, 128], [0, 4], [1, 128]])
                    t4 = tA[:, 0:4, :, :].rearrange("p g b c -> p g (b c)")
                    mk.op("dve", lambda e, pf=pf, mfv=mfv, t4=t4: e.tensor_tensor(out=t4, in0=pf[:].rearrange("p (g x) -> p g x", x=128), in1=mfv, op=ALU.mult),
                          reads=[pfk, Z + "mfb"] + HK, writes=HK)
                    mk.op("dve", lambda e, pb_=pb_, mbv=mbv, g4=g4: e.tensor_tensor(out=WT[:, g4 * 4:(g4 + 1) * 4, :], in0=pb_[:].rearrange("p (g x) -> p g x", x=128), in1=mbv, op=ALU.mult),
                          reads=[pbk, Z + "mfb"], writes=[H + "WT"])
                    mk.op("dve", lambda e, t4=t4, g4=g4: e.tensor_tensor(out=WT[:, g4 * 4:(g4 + 1) * 4, :], in0=WT[:, g4 * 4:(g4 + 1) * 4, :], in1=t4, op=ALU.add),
                          reads=HK + [H + "WT"], writes=[H + "WT"])
                arb = rap(P.arena, off(arow), [[ASIZE, 128], [0, 16], [1, 33]])
                li8_33, rho33, rho32 = bc_g("li8", 33), bc_g("rho8", 33), bc_g("rho8", 32)
                mk.op("dve", lambda e, li8_33=li8_33, ang=ang, arb=arb: e.tensor_tensor(out=ang, in0=arb, in1=li8_33, op=ALU.mult), reads=AK, writes=HK)
                sincos(ang, 528, Rr, R2i, tng, HK, [H + "tab"], turns=True)
                mk.op("dve", lambda e: e.tensor_scalar(out=Ri, in0=R2i, scalar1=-1.0, scalar2=None, op0=ALU.mult), reads=[H + "tab"], writes=[H + "tab"])
                mk.op("dve", lambda e, rho33=rho33, R2r=R2r, Rr=Rr: e.tensor_tensor(out=R2r, in0=Rr, in1=rho33, op=ALU.mult), reads=[H + "tab"] + SK, writes=[H + "tab"])
                mk.op("dve", lambda e, rho33=rho33, R2i=R2i: e.tensor_tensor(out=R2i, in0=R2i, in1=rho33, op=ALU.mult), reads=[H + "tab"] + SK, writes=[H + "tab"])
                mk.op("dve", lambda e, D0=D0: e.memset(D0, 1.0), writes=[H + "tab"])
                mk.op("dve", lambda e, rho32=rho32, D0=D0: e.tensor_tensor(out=D0, in0=D0, in1=rho32, op=ALU.mult), reads=SK + [H + "tab"], writes=[H + "tab"])
                mk.op("dve", lambda e: e.memset(D0[:, :, 0:1], 0.0), reads=[H + "tab"], writes=[H + "tab"])
                mk.dma(s5scr.ap()[gh], P.arena[:, markp:mark1], reads=[H + "PTre", H + "PTim", H + "Qb", H + "WT", H + "tab"], writes=[("s5scr", gh)])
                yield
                continue
            else:
                mk.dma(P.arena[:, markp:mark1], s5scr.ap()[gh], reads=[("s5scr", gh)], writes=[H + "PTre", H + "PTim", H + "Qb", H + "WT", H + "tab"])
            mk.fence()
            P.atop = mark1
            U2T = P.aalloc([128, 16, 128], BF16)
            Gr = P.aalloc([128, NS, 16, 32])
            Gi = P.aalloc([128, NS, 16, 32])
            Wr = P.aalloc([128, NS, 16, 32])
            Wi = P.aalloc([128, NS, 16, 32])
            t1_ = P.aalloc([128, NS, 16, 32])
            Fr = P.aalloc([128, 16, 128], BF16)
            Fi = P.aalloc([128, 16, 128], BF16)
            cr = P.aalloc([128, 16])
            ci_ = P.aalloc([128, 16])
            ct = P.aalloc([128, 16])
            RK = [H + "run"]
            for g4 in range(4):
                pst, psk = P.psum()
                for gg in range(4):
                    g = g0 + g4 * 4 + gg
                    P.transpose(pst[:, gg * 128:(gg + 1) * 128], u_tok[:, g, :, :].rearrange("p b c -> p (b c)"), ident[:],
                                reads=[Z + "u_tok", "ident"], writes=[psk])
                P.copy(P.evac_eng(), U2T[:, g4 * 4:(g4 + 1) * 4, :], pst[:].rearrange("p (g x) -> p g x", x=128), reads=[psk], writes=[H + "U2T"])
            for g4 in range(4):
                for (PT_, Gd, nm) in ((PTre, Gr, "Gr"), (PTim, Gi, "Gi")):
                    pst, psk = P.psum()
                    for gg in range(4):
                        g = g4 * 4 + gg
                        P.mm(pst[:, gg * 128:(gg + 1) * 128], PT_[:, g, :], U2T[:, g, :], True, True,
                             reads=[H + "PTre", H + "PTim", H + "U2T"], writes=[psk])
                    o_f = rap(P.arena, off(Gd) + g4 * 4 * 32, [[ASIZE, 64], [32, 4], [512, NS], [1, 32]])
                    i_f = rap(pst, 0, [[512, 64], [128, 4], [32, NS], [1, 32]])
                    mk.op("act", lambda e, o_f=o_f, i_f=i_f: e.activation(out=o_f, in_=i_f, func=AF.Copy), reads=[psk], writes=RK)
                    o_b = rap(P.arena, off(Gd) + 64 * ASIZE + g4 * 4 * 32 + (NS - 1) * 512 + 31, [[ASIZE, 64], [32, 4], [-512, NS], [-1, 32]])
                    i_b = rap(pst, 64 * 512, [[512, 64], [128, 4], [32, NS], [1, 32]])
                    mk.op("dve", lambda e, o_b=o_b, i_b=i_b: e.tensor_copy(out=o_b, in_=i_b), reads=[psk], writes=RK)
            def rtab(t_, a0, n):
                return rap(P.arena, off(t_) + a0, [[ASIZE, 128], [0, NS], [33, 16], [1, n]])
            TK = [H + "tab"]
            mk.op("dve", lambda e: e.tensor_tensor(out=Wr, in0=Gr, in1=rtab(Rr, 0, 32), op=ALU.mult), reads=RK + TK, writes=RK)
            mk.op("pool", lambda e: e.tensor_tensor(out=t1_, in0=Gi, in1=rtab(Ri, 0, 32), op=ALU.mult), reads=RK + TK, writes=RK)
            mk.op("dve", lambda e: e.tensor_tensor(out=Wr, in0=Wr, in1=t1_, op=ALU.subtract), reads=RK, writes=RK)
            mk.op("dve", lambda e: e.tensor_tensor(out=Wi, in0=Gr, in1=rtab(Ri, 0, 32), op=ALU.mult), reads=RK + TK, writes=RK)
            mk.op("pool", lambda e: e.tensor_tensor(out=t1_, in0=Gi, in1=rtab(Rr, 0, 32), op=ALU.mult), reads=RK + TK, writes=RK)
            mk.op("dve", lambda e: e.tensor_tensor(out=Wi, in0=Wi, in1=t1_, op=ALU.add), reads=RK, writes=RK)
            d0flat = D0.rearrange("p g a -> p (g a)")
            for s in range(NS):
                if v == 1:
                    srcr = bc_g("fsr", 1) if s == 0 else cr
                    srci = bc_g("fsi", 1) if s == 0 else ci_
                    srcr = rap(P.arena, off(sc["fsr"]) + g0, [[ASIZE, 128], [1, 16]]) if s == 0 else cr
                    srci = rap(P.arena, off(sc["fsi"]) + g0, [[ASIZE, 128], [1, 16]]) if s == 0 else ci_
                    mk.op("dve", lambda e, s=s, srcr=srcr: e.tensor_tensor(out=Wr[:, s, :, 0], in0=Wr[:, s, :, 0], in1=srcr, op=ALU.add), reads=RK + SK, writes=RK)
                    mk.op("dve", lambda e, s=s, srci=srci: e.tensor_tensor(out=Wi[:, s, :, 0], in0=Wi[:, s, :, 0], in1=srci, op=ALU.add), reads=RK + SK, writes=RK)
                for (src, dst) in ((Wr, Gr), (Wi, Gi)):
                    mk.op("dve", lambda e, s=s, src=src, dst=dst: e.tensor_tensor_scan(
                        out=dst[:, s, :, :].rearrange("p g a -> p (g a)"), data0=d0flat, data1=src[:, s, :, :].rearrange("p g a -> p (g a)"),
                        initial=0.0, op0=ALU.mult, op1=ALU.add), reads=RK + TK, writes=RK)
                if v == 1 and s < NS - 1:
                    r2r = rap(P.arena, off(R2r) + 32, [[ASIZE, 128], [33, 16]])
                    r2i = rap(P.arena, off(R2i) + 32, [[ASIZE, 128], [33, 16]])
                    wr_ = Gr[:, s, :, 31]
                    wi_ = Gi[:, s, :, 31]
                    cmul(cr, ci_, r2r, r2i, wr_, wi_, ct, RK + TK)
            if v == 0:
                for s in range(NS):
                    rfr = rap(P.arena, off(sc["rfr"]) + g0, [[ASIZE, 128], [1, 16]])
                    rfi = rap(P.arena, off(sc["rfi"]) + g0, [[ASIZE, 128], [1, 16]])
                    for (rows, ss) in ((slice(0, 64), s), (slice(64, 128), NS - 1 - s)):
                        p0 = rows.start
                        rfr_ = rap(P.arena, off(sc["rfr"]) + p0 * ASIZE + g0, [[ASIZE, 64], [1, 16]])
                        rfi_ = rap(P.arena, off(sc["rfi"]) + p0 * ASIZE + g0, [[ASIZE, 64], [1, 16]])
                        cmul(hl[rows, 0, s, g0:g0 + 16], hl[rows, 1, s, g0:g0 + 16], rfr_, rfi_, Gr[rows, ss, :, 31], Gi[rows, ss, :, 31],
                             ct[rows, :], RK + SK + [Z + "hl"])
            def sh(t_, a0, n):
                return rap(P.arena, off(t_) + a0, [[ASIZE, 128], [512, NS], [32, 16], [1, n]])
            mk.op("dve", lambda e: e.tensor_tensor(out=sh(Wr, 1, 31), in0=sh(Gr, 0, 31), in1=rtab(R2r, 1, 31), op=ALU.mult), reads=RK + TK, writes=RK)
            mk.op("pool", lambda e: e.tensor_tensor(out=sh(t1_, 1, 31), in0=sh(Gi, 0, 31), in1=rtab(R2i, 1, 31), op=ALU.mult), reads=RK + TK, writes=RK)
            mk.op("dve", lambda e: e.tensor_tensor(out=sh(Wr, 1, 31), in0=sh(Wr, 1, 31), in1=sh(t1_, 1, 31), op=ALU.subtract), reads=RK, writes=RK)
            mk.op("dve", lambda e: e.tensor_tensor(out=sh(Wi, 1, 31), in0=sh(Gr, 0, 31), in1=rtab(R2i, 1, 31), op=ALU.mult), reads=RK + TK, writes=RK)
            mk.op("pool", lambda e: e.tensor_tensor(out=sh(t1_, 1, 31), in0=sh(Gi, 0, 31), in1=rtab(R2r, 1, 31), op=ALU.mult), reads=RK + TK, writes=RK)
            mk.op("dve", lambda e: e.tensor_tensor(out=sh(Wi, 1, 31), in0=sh(Wi, 1, 31), in1=sh(t1_, 1, 31), op=ALU.add), reads=RK, writes=RK)
            for s in range(NS):
                if v == 0:
                    mk.op("pool", lambda e, s=s: e.memset(Wr[:, s, :, 0], 0.0), reads=RK, writes=RK)
                    mk.op("pool", lambda e, s=s: e.memset(Wi[:, s, :, 0], 0.0), reads=RK, writes=RK)
            if v == 1:
                for s in range(NS):
                    if s == 0:
                        mk.op("dve", lambda e: e.tensor_copy(out=Wr[:, 0, :, 0], in_=rap(P.arena, off(sc["fsr"]) + g0, [[ASIZE, 128], [1, 16]])), reads=RK + SK, writes=RK)
                        mk.op("dve", lambda e: e.tensor_copy(out=Wi[:, 0, :, 0], in_=rap(P.arena, off(sc["fsi"]) + g0, [[ASIZE, 128], [1, 16]])), reads=RK + SK, writes=RK)
                    else:
                        r2r = rap(P.arena, off(R2r) + 32, [[ASIZE, 128], [33, 16]])
                        r2i = rap(P.arena, off(R2i) + 32, [[ASIZE, 128], [33, 16]])
                        cmul(Wr[:, s, :, 0], Wi[:, s, :, 0], r2r, r2i, Gr[:, s - 1, :, 31], Gi[:, s - 1, :, 31], ct, RK + TK)
            for (src, dst) in ((Wr, Fr), (Wi, Fi)):
                mk.op("act", lambda e, src=src, dst=dst: e.activation(
                    out=dst[0:64, :, :].rearrange("p g (s a) -> p s g a", a=32), in_=src[0:64, :, :, :], func=AF.Copy), reads=RK, writes=[H + "F"])
                i_b = rap(P.arena, off(src) + 64 * ASIZE + (NS - 1) * 512 + 31, [[ASIZE, 64], [-512, NS], [32, 16], [-1, 32]])
                mk.op("dve", lambda e, dst=dst, i_b=i_b: e.tensor_copy(out=dst[64:128, :, :].rearrange("p g (s a) -> p s g a", a=32), in_=i_b),
                      reads=RK, writes=[H + "F"])
            for g4 in range(4):
                pst, psk = P.psum()
                for gg in range(4):
                    g = g4 * 4 + gg
                    cs = slice(gg * 128, (gg + 1) * 128)
                    P.mm(pst[:, cs], U2T[:, g, :], WT[:, g, :], True, False, reads=[H + "U2T", H + "WT"], writes=[psk])
                    P.mm(pst[:, cs], Fr[:, g, :], Qre_b[:, g, :], False, False, reads=[H + "F", H + "Qb"], writes=[psk])
                    P.mm(pst[:, cs], Fi[:, g, :], Qsim_b[:, g, :], False, True, reads=[H + "F", H + "Qb"], writes=[psk])
                ga = g0 + g4 * 4
                uv = u_tok[:, ga:ga + 4, :, :]
                dbv = rap(P.arena, off(dB) + ga * 16, [[ASIZE, 128], [16, 4], [0, 8], [1, 16]])
                mk.op("pool", lambda e, uv=uv, dbv=dbv: e.tensor_tensor(out=uv, in0=uv, in1=dbv, op=ALU.mult), reads=[Z + "u_tok", Z + "dB", H + "U2T"], writes=[Z + "u_tok"])
                mk.op("dve", lambda e, uv=uv, pst=pst: e.tensor_tensor(out=uv, in0=uv, in1=pst[:].rearrange("p (g b c) -> p g b c", b=8, c=16), op=ALU.add),
                      reads=[psk, Z + "u_tok"], writes=[Z + "u_tok"])
            mk.fence()
        if mode == "pre":
            return
        P.atop = mark0
        if v == 0:
            hst = P.aalloc([128, 2, 8, 64])
            for ri in range(2):
                for s in range(4):
                    for d in range(2):
                        pst, psk = P.psum()
                        P.mm(pst[0:32, 0:64], hl[d * 64:(d + 1) * 64, ri, s, :], ident[d * 64:(d + 1) * 64, d * 64:(d + 1) * 64], True, True,
                             reads=[Z + "hl", "ident"], writes=[psk])
                        P.copy(P.evac_eng(), hst[0:32, ri, s * 2 + d, :], pst[0:32, 0:64], reads=[psk], writes=[Z + "hst"])
                ok = ("o_s5", ri)
                dst = o_s5r if ri == 0 else o_s5i
                mk.dma(rap(dst, 0, [[64, 32], [2048, 8], [1, 64]]), hst[0:32, ri, :, :], reads=[Z + "hst"], writes=[ok])
                P.outs.append(ok)
        ga_ = P.aalloc([128, 32, 8, 16])
        g_tok = P.aalloc([128, 8, 512], BF16)
        gT = P.aalloc([128, 4, T], BF16)
        bglu = P.aalloc([128, 4])
        UK = [Z + "u_tok"]
        mk.op("dve", lambda e: e.tensor_tensor(out=ga_, in0=u_tok, in1=u_tok, op=ALU.mult), reads=UK, writes=[Z + "ga"])
        mk.op("dve", lambda e: e.tensor_scalar(out=ga_, in0=ga_, scalar1=0.044715, scalar2=1.0, op0=ALU.mult, op1=ALU.add), reads=[Z + "ga"], writes=[Z + "ga"])
        mk.op("dve", lambda e: e.tensor_tensor(out=ga_, in0=ga_, in1=u_tok, op=ALU.mult), reads=UK + [Z + "ga"], writes=[Z + "ga"])
        mk.op("act", lambda e: e.activation(out=ga_, in_=ga_, func=AF.Sigmoid, scale=2.0 * math.sqrt(2.0 / math.pi)), reads=[Z + "ga"], writes=[Z + "ga"])
        mk.op("dve", lambda e: e.tensor_tensor(out=g_tok.rearrange("p b (g c) -> p g b c", c=16), in0=ga_, in1=u_tok, op=ALU.mult),
              reads=UK + [Z + "ga"], writes=[Z + "g_tok"])
        P.load_T(bglu, s5_b_glu, 0, 4, Z + "bglu")
        for b in range(8):
            pst, psk = P.psum()
            psb = pst[:].bitcast(BF16)
            for cc in range(4):
                P.transpose(psb[:, cc * 128:(cc + 1) * 128], g_tok[:, b, cc * 128:(cc + 1) * 128], identb[:], reads=[Z + "g_tok", "identb"], writes=[psk])
            P.copy(P.evac_eng(), gT[:, :, b:T:8], psb[:, 0:512].rearrange("p (c t) -> p c t", t=128), reads=[psk], writes=[Z + "gT"])
        i = state["w"] % 2
        state["w"] += 1
        wgl, wglk = wbuf[i], "wbuf%d" % i
        mk.dma(wgl[:, 0:4, :], rap(s5_w_glu, 0, [[512, 128], [128 * 512, 4], [1, 512]]), writes=[wglk], q="pool")
        for m in range(4):
            for th in range(2):
                ts = slice(th * 512, (th + 1) * 512)
                pst, psk = P.psum()
                for kc in range(4):
                    P.mm(pst[:], wgl[:, kc, m * 128:(m + 1) * 128], gT[:, kc, ts], kc == 0, kc == 3, reads=[wglk, Z + "gT"], writes=[psk])
                s_ = sq[(m * 2 + th) % 2]
                sk_ = "sq%d" % ((m * 2 + th) % 2)
                mk.op("act", lambda e, pst=pst, s_=s_, m=m: e.activation(out=s_[:], in_=pst[:], func=AF.Sigmoid, bias=bglu[:, m:m + 1]),
                      reads=[psk, Z + "bglu"], writes=[sk_])
                mk.op("dve", lambda e, s_=s_, m=m, ts=ts: e.tensor_tensor(out=mixT[:, m, ts], in0=gT[:, m, ts], in1=s_[:], op=ALU.mult),
                      reads=[sk_, Z + "gT"], writes=["mixT"])
        mk.fence()
        P.atop = 0

    def ab_layer(v):
        if v == 0:
            chains = []
            for s in range(4):
                chains.append((0, [2 * s, 2 * s + 1], s))
                chains.append((1, [2 * s + 1, 2 * s], s))
        else:
            chains = [(0, list(range(8)), 0), (1, list(range(7, -1, -1)), 0)]
        for _ in s5_part(v):
            pass
        gdn_part(v, chains)

    def xload_gen(v, xtok):
        n = len(xtok)
        for tt in range(8):
            xt_, xk = xtok[tt % n], "xtok%d" % (tt % n)
            mk.dma(xt_, xin.ap()[v, tt * 128:(tt + 1) * 128, :], writes=[xk])
            for cg in range(2):
                pst, psk = P.psum()
                for cc in range(4):
                    c = cg * 4 + cc
                    P.transpose(pst[:, cc * 128:(cc + 1) * 128], xt_[:, c * 128:(c + 1) * 128], ident[:],
                                reads=[xk, "ident"], writes=[psk])
                P.copy(P.evac_eng(), xT[:, cg * 4:(cg + 1) * 4, tt * 128:(tt + 1) * 128],
                       pst[:].rearrange("p (c t) -> p c t", t=128), reads=[psk], writes=[("xT", tt // 4)])
            yield

    P.psmod = 7
    P.atop = adaln_top
    ga, gothers = adaln_gen(), [s5_part(0, "pre", P.atop)]
    alive = True
    while alive or gothers:
        for _ in range(2):
            if alive:
                try:
                    next(ga)
                except StopIteration:
                    alive = False
        for g_ in list(gothers):
            try:
                next(g_)
            except StopIteration:
                gothers.remove(g_)
    P.psmod = 8
    adaln_finish()

    for v in range(2):
        seqlen = 256 if v == 0 else 1024
        P.atop = 0
        for _ in xload_gen(v, [P.aalloc([128, D]) for _ in range(4)]):
            pass
        mk.fence()
        for l in range(2):
            norm_mod(l, 0, v)
            if l == 0:
                ab_layer(v)
                out_proj(w_out_ab, l, v)
            else:
                attn_layer(v)
                out_proj(w_out_cd, l, v)
            norm_mod(l, 1, v)
            ffn(l, v, seqlen)
        P.atop = 0
        xtok = [P.aalloc([128, D]) for _ in range(4)]
        for tt in range(8):
            xt_, xk = xtok[tt % 4], "xtok%d" % (tt % 4)
            for cg in range(2):
                pst, psk = P.psum()
                for cc in range(4):
                    c = cg * 4 + cc
                    P.transpose(pst[:, cc * 128:(cc + 1) * 128], xT[:, c, tt * 128:(tt + 1) * 128], ident[:],
                                reads=[("xT", tt // 4), "ident"], writes=[psk])
                P.copy(P.evac_eng(), xt_[:, cg * 512:(cg + 1) * 512], pst[:], reads=[psk], writes=[xk])
            ok = ("out_y", v, tt)
            mk.dma(yout.ap()[v, tt * 128:(tt + 1) * 128, :], xt_, reads=[xk], writes=[ok])
            P.outs.append(ok)
        if v == 1:
            mk.fence()

    mk.emit(final_keys=P.outs)
    P.es.close()
    return nc


_CACHE = {}


def _consts():
    c = {}
    c["c_ident"] = np.eye(128, dtype=np.float32)
    n = T
    row = np.repeat(np.arange(n // 64), 64).astype(np.float32)
    col = np.tile(np.arange(64), n // 64).astype(np.float32)
    inv = (10000.0 ** (-np.arange(16, dtype=np.float32) / 16)).astype(np.float32)
    ar = row[:, None] * inv[None, :]
    ac = col[:, None] * inv[None, :]
    c["c_rope"] = np.stack([np.cos(ar), np.sin(ar), np.cos(ac), np.sin(ac)]).astype(np.float32)
    q = np.arange(128)[:, None]
    k = np.arange(128)[None, :]
    NEG = -30000.0
    m1 = np.where(k >= q, 0.0, NEG)
    m3 = np.where(k <= q, 0.0, NEG)
    c["c_wmask"] = np.concatenate([m1.T, np.zeros((128, 128)), m3.T], axis=1).astype(np.float32)
    i = np.arange(128)[:, None]
    j = np.arange(128)[None, :]
    same = (i // 64) == (j // 64)
    LE = ((i <= j) & same).astype(np.float32)
    GE = ((i >= j) & same).astype(np.float32)
    eye = np.eye(128, dtype=np.float32)
    c["c_tri"] = np.stack([LE, GE, 1 - LE, 1 - GE, GE - eye, LE - eye]).astype(np.float32)
    sel = np.zeros((4, 128, 128), np.float32)
    for n_, r_ in enumerate((63, 127, 0, 64)):
        sel[n_, r_, :] = 1.0
    c["c_sel"] = sel
    kkt = np.zeros((128, 8), np.float32)
    kkt[0:64, :] = np.arange(8)[None, :]
    kkt[64:128, :] = 7 - np.arange(8)[None, :]
    c["c_kk"] = kkt
    c["c_arow"] = np.tile(np.arange(33, dtype=np.float32)[None, :], (128, 1))
    bq = (np.arange(128) // 16)
    c["c_mfb"] = np.stack([(bq[:, None] <= bq[None, :]), (bq[:, None] >= bq[None, :])]).astype(np.float32)
    return c


SHARED = ("w_mod", "b_mod", "norm1_g", "norm2_g", "ffn_up", "ffn_conv_w", "ffn_conv_b", "ffn_down",
          "w_out_ab", "c_sink", "d_subln")


def make_in_map(inp, r, consts):
    b = r // 4
    m = {k: inp[k].reshape(inp[k].shape[1:]) if k in ("w_out_ab", "c_sink", "d_subln") else inp[k] for k in SHARED}
    m.update(consts)
    m["w_in_cd"] = inp["w_in_cd"][0]
    m["w_out_cd"] = inp["w_out_cd"][0]
    m["gcat"] = np.stack([inp["c_qn"][0], inp["c_kn"][0], inp["d_qn"][0], inp["d_kn"][0]])
    m["lqk"] = np.stack([inp["d_lq1"][0], inp["d_lk1"][0], inp["d_lq2"][0], inp["d_lk2"][0]])
    m["ccat"] = np.concatenate([inp["cache_c_k"][b, 0].reshape(512, 128), inp["cache_c_v"][b, 0].reshape(512, 128),
                                inp["cache_d_k"][b, 0].reshape(512, 512), inp["cache_d_v"][b, 0].reshape(512, 512)], axis=1)
    m["w_in_ab"] = inp["w_in_ab"][0]
    m["gdn_conv_w"] = inp["gdn_conv_w"][0]
    m["gdn_a_log"] = inp["gdn_a_log"][0].reshape(8)
    m["gdn_dt_bias"] = inp["gdn_dt_bias"][0].reshape(8)
    m["gdn_norm_g"] = inp["gdn_norm_g"][0]
    m["gstate"] = inp["state_gdn"][b, 0]
    for k_ in ("s5_lam_re", "s5_lam_im"):
        m[k_] = inp[k_][0].reshape(64, 64)
    m["s5_log_dt"] = inp["s5_log_dt"][0].reshape(64)
    for k_ in ("s5_b_re", "s5_b_im"):
        m[k_] = inp[k_][0].reshape(64, 64, 16)
    for k_ in ("s5_c_re", "s5_c_im"):
        m[k_] = inp[k_][0].reshape(64, 16, 64)
    m["s5_d"] = inp["s5_d"][0]
    m["s5_w_glu"] = inp["s5_w_glu"][0]
    m["s5_b_glu"] = inp["s5_b_glu"][0]
    m["s5h_re"] = inp["state_s5_re"][b, 0].reshape(64, 64)
    m["s5h_im"] = inp["state_s5_im"][b, 0].reshape(64, 64)
    m["xin"] = np.stack([inp["x_prompt"][4 * r:4 * r + 4].reshape(T, D), inp["x_sample"][b]])
    m["cvec"] = np.stack([inp["c_ctx"], inp["c"][b]])
    return {k: np.ascontiguousarray(v, dtype=np.float32) for k, v in m.items()}


def kernel(**inp):
    inp = {k: np.asarray(v) for k, v in inp.items()}
    ncores = 8
    if "nc" not in _CACHE:
        nc = bass.Bass("TRN2", target_bir_lowering=False)
        build(nc)
        _CACHE["nc"] = nc
    nc = _CACHE["nc"]
    consts = _consts()
    in_maps = [make_in_map(inp, r, consts) for r in range(ncores)]
    res = run_bass_kernel_spmd(nc, in_maps, core_ids=list(range(ncores)))
    rs = res.results
    y_prompt = np.concatenate([rs[r]["yout"][0].reshape(4, 256, D) for r in range(ncores)], axis=0)
    y_sample = np.stack([rs[0]["yout"][1], rs[4]["yout"][1]])
    new_c_k = np.concatenate([rs[r]["o_ck"].reshape(4, 1, 256, 2, 64) for r in range(ncores)], axis=0)
    new_c_v = np.concatenate([rs[r]["o_cv"].reshape(4, 1, 256, 2, 64) for r in range(ncores)], axis=0)
    new_d_k = np.concatenate([rs[r]["o_dk"].reshape(4, 1, 256, 4, 2, 64) for r in range(ncores)], axis=0)
    new_d_v = np.concatenate([rs[r]["o_dv"].reshape(4, 1, 256, 4, 128) for r in range(ncores)], axis=0)
    new_gdn = np.concatenate([rs[r]["o_gdn"].reshape(4, 1, 2, 4, 128, 128) for r in range(ncores)], axis=0)
    new_s5_re = np.concatenate([rs[r]["o_s5r"].reshape(4, 1, 2, 32, 64) for r in range(ncores)], axis=0)
    new_s5_im = np.concatenate([rs[r]["o_s5i"].reshape(4, 1, 2, 32, 64) for r in range(ncores)], axis=0)
    return y_prompt, y_sample, new_s5_re, new_s5_im, new_gdn, new_c_k, new_c_v, new_d_k, new_d_v
```
